# Optimizing a Trainium2 kernel written in Bass

```python
import jax, jax.numpy as jnp
from jax import lax
import numpy as np

D_MODEL = 1024
BATCH = 8
SEQ = 8192
DEPTH = 1

N_MEM = 256
EPS = 1e-6
FOX_HEADS = 8
FOX_HEAD_DIM = 64
FOX_WIDTH = FOX_HEADS * FOX_HEAD_DIM
Q_BLOCK = 128
GDN_HEADS = 4
GDN_HEAD_DIM = 128
GDN_WIDTH = GDN_HEADS * GDN_HEAD_DIM
CONV_WIDTH = 4
CHUNK = 64
MIX_WIDTH = FOX_WIDTH + GDN_WIDTH
IN_PROJ_SIZES = (FOX_WIDTH, FOX_WIDTH, FOX_WIDTH, FOX_HEADS,
                 GDN_WIDTH, GDN_WIDTH, GDN_WIDTH, GDN_WIDTH, GDN_HEADS, GDN_HEADS)
IN_PROJ_WIDTH = sum(IN_PROJ_SIZES)
SPLIT_POINTS = tuple(np.cumsum(IN_PROJ_SIZES)[:-1].tolist())
MEM_HEADS = 4
MEM_HEAD_DIM = D_MODEL // MEM_HEADS
D_FF = 2816
FFN_RESIDUAL_WEIGHT = 0.5

kernel_name = 'hymba_fox_gdn_macaron_sandwich_memory'


def rms_norm(x, gain):
    xf = x.astype(jnp.float32)
    y = xf * lax.rsqrt(jnp.mean(xf * xf, axis=-1, keepdims=True) + EPS)
    return (y * gain.astype(jnp.float32)).astype(x.dtype)


def l2_norm(x):
    return x * lax.rsqrt(jnp.sum(x * x, axis=-1, keepdims=True) + EPS)


def swiglu(x, w_gate, w_up, w_down):
    return (jax.nn.silu(x @ w_gate) * (x @ w_up)) @ w_down


def forgetting_attention(q, k, v, f_logit):
    B, S, H, Dh = q.shape
    nb = S // Q_BLOCK
    F = jnp.cumsum(jax.nn.log_sigmoid(f_logit.astype(jnp.float32)), axis=1).transpose(0, 2, 1)
    qb = (q * (Dh ** -0.5)).reshape(B, nb, Q_BLOCK, H, Dh).transpose(1, 0, 3, 2, 4)
    Fq = F.reshape(B, H, nb, Q_BLOCK).transpose(2, 0, 1, 3)
    q_pos = jnp.arange(S).reshape(nb, Q_BLOCK)
    k_pos = jnp.arange(S)

    def block(args):
        q_blk, F_blk, pos_blk = args
        s = jnp.einsum('bhqd,bshd->bhqs', q_blk, k, preferred_element_type=jnp.float32)
        s = s + F_blk[..., None] - F[:, :, None, :]
        s = jnp.where(pos_blk[:, None] >= k_pos[None, :], s, -jnp.inf)
        p = jax.nn.softmax(s, axis=-1).astype(v.dtype)
        return jnp.einsum('bhqs,bshd->bqhd', p, v)

    out = lax.map(block, (qb, Fq, q_pos))
    return out.transpose(1, 0, 2, 3, 4).reshape(B, S, H * Dh)


def causal_conv_silu(x, w):
    S = x.shape[1]
    xp = jnp.pad(x, ((0, 0), (CONV_WIDTH - 1, 0), (0, 0)))
    y = xp[:, 0:S] * w[0]
    for i in range(1, CONV_WIDTH):
        y = y + xp[:, i:i + S] * w[i]
    return jax.nn.silu(y)


def gated_delta_rule_chunked(q, k, v, g, beta):
    B, S, H, Dk = q.shape
    Dv = v.shape[-1]
    n = S // CHUNK

    def chunks(t):
        return t.reshape(B, n, CHUNK, H, -1).transpose(0, 3, 1, 2, 4)

    q = chunks(q) * (Dk ** -0.5)
    k = chunks(k)
    v = chunks(v)
    beta = beta.reshape(B, n, CHUNK, H).transpose(0, 3, 1, 2)
    g = jnp.cumsum(g.reshape(B, n, CHUNK, H).transpose(0, 3, 1, 2), axis=-1)
    k_beta = k * beta[..., None]
    v_beta = v * beta[..., None]
    incl = jnp.tril(jnp.ones((CHUNK, CHUNK), dtype=bool))
    strict = jnp.tril(jnp.ones((CHUNK, CHUNK), dtype=bool), -1)
    decay = jnp.exp(jnp.where(incl, g[..., :, None] - g[..., None, :], -jnp.inf))
    L = jnp.where(strict, jnp.einsum('bhncd,bhnsd->bhncs', k_beta, k) * decay, 0.0)
    A = L + jnp.eye(CHUNK, dtype=L.dtype)
    u = lax.linalg.triangular_solve(A, v_beta, left_side=True, lower=True)
    w = lax.linalg.triangular_solve(A, k_beta * jnp.exp(g)[..., None], left_side=True, lower=True)
    attn_intra = jnp.einsum('bhncd,bhnsd->bhncs', q, k) * decay
    q_decay = q * jnp.exp(g)[..., None]
    g_last = g[..., -1]
    k_tail = k * jnp.exp(g_last[..., None] - g)[..., None]

    def step(state, xs):
        w_c, u_c, qd_c, a_c, kt_c, gl_c = xs
        v_new = u_c - jnp.einsum('bhcd,bhde->bhce', w_c, state)
        o = jnp.einsum('bhcd,bhde->bhce', qd_c, state) + jnp.einsum('bhcs,bhse->bhce', a_c, v_new)
        state = state * jnp.exp(gl_c)[..., None, None] + jnp.einsum('bhcd,bhce->bhde', kt_c, v_new)
        return state, o

    xs = (jnp.moveaxis(w, 2, 0), jnp.moveaxis(u, 2, 0), jnp.moveaxis(q_decay, 2, 0),
          jnp.moveaxis(attn_intra, 2, 0), jnp.moveaxis(k_tail, 2, 0), jnp.moveaxis(g_last, 2, 0))
    state0 = jnp.zeros((B, H, Dk, Dv), jnp.float32)
    _, o = lax.scan(step, state0, xs)
    return o.transpose(1, 0, 3, 2, 4).reshape(B, S, H, Dv)


def hybrid_mixer(u, w_in, fox_f_bias, gdn_conv_w, gdn_a_log, gdn_dt_bias, gdn_out_norm):
    B, S, _ = u.shape
    proj = u @ w_in
    fq, fk, fv, ff, gq, gk, gv, gz, gb, ga = jnp.split(proj, SPLIT_POINTS, axis=-1)
    fox = forgetting_attention(fq.reshape(B, S, FOX_HEADS, FOX_HEAD_DIM),
                               fk.reshape(B, S, FOX_HEADS, FOX_HEAD_DIM),
                               fv.reshape(B, S, FOX_HEADS, FOX_HEAD_DIM),
                               ff + fox_f_bias)
    qkv = causal_conv_silu(jnp.concatenate([gq, gk, gv], axis=-1).astype(jnp.float32),
                           gdn_conv_w.astype(jnp.float32))
    cq, ck, cv = jnp.split(qkv, (GDN_WIDTH, 2 * GDN_WIDTH), axis=-1)
    q = l2_norm(cq.reshape(B, S, GDN_HEADS, GDN_HEAD_DIM))
    k = l2_norm(ck.reshape(B, S, GDN_HEADS, GDN_HEAD_DIM))
    v = cv.reshape(B, S, GDN_HEADS, GDN_HEAD_DIM)
    beta = jax.nn.sigmoid(gb.astype(jnp.float32))
    g = -jnp.exp(gdn_a_log.astype(jnp.float32)) * jax.nn.softplus(
        ga.astype(jnp.float32) + gdn_dt_bias.astype(jnp.float32))
    o = gated_delta_rule_chunked(q, k, v, g, beta)
    o = rms_norm(o, gdn_out_norm) * jax.nn.silu(
        gz.astype(jnp.float32).reshape(B, S, GDN_HEADS, GDN_HEAD_DIM))
    gdn = o.reshape(B, S, GDN_WIDTH).astype(u.dtype)
    return jnp.concatenate([fox.astype(u.dtype), gdn], axis=-1)


def memory_cross_attention(h, m, w_q, w_kv, w_o):
    B, S, _ = h.shape
    q = (h @ w_q).reshape(B, S, MEM_HEADS, MEM_HEAD_DIM)
    k, v = jnp.split(m @ w_kv, 2, axis=-1)
    k = k.reshape(B, -1, MEM_HEADS, MEM_HEAD_DIM)
    v = v.reshape(B, -1, MEM_HEADS, MEM_HEAD_DIM)
    s = jnp.einsum('bqhd,bmhd->bhqm', q, k, preferred_element_type=jnp.float32) * (MEM_HEAD_DIM ** -0.5)
    p = jax.nn.softmax(s, axis=-1).astype(v.dtype)
    o = jnp.einsum('bhqm,bmhd->bqhd', p, v).reshape(B, S, D_MODEL)
    return o @ w_o


def setup_inputs(seed: int = 0) -> dict:
    key = jax.random.key(seed)
    ks = iter(jax.random.split(key, 40))

    def normal(shape, scale):
        return jax.random.normal(next(ks), shape, jnp.float32) * scale

    def gain(width):
        return 1.0 + normal((DEPTH, width), 0.02)

    L = DEPTH
    dt = jnp.exp(jax.random.uniform(next(ks), (L, GDN_HEADS), jnp.float32,
                                    minval=float(np.log(1e-3)), maxval=float(np.log(1e-1))))
    return {
        'x': normal((BATCH, SEQ, D_MODEL), 1.0),
        'mem': normal((BATCH, N_MEM, D_MODEL), 1.0),
        'ffn1_pre_norm': gain(D_MODEL),
        'ffn1_w_gate': normal((L, D_MODEL, D_FF), D_MODEL ** -0.5),
        'ffn1_w_up': normal((L, D_MODEL, D_FF), D_MODEL ** -0.5),
        'ffn1_w_down': normal((L, D_FF, D_MODEL), D_FF ** -0.5),
        'ffn1_post_norm': gain(D_MODEL),
        'mix_pre_norm': gain(D_MODEL),
        'w_in': normal((L, D_MODEL, IN_PROJ_WIDTH), D_MODEL ** -0.5),
        'fox_f_bias': jax.random.uniform(next(ks), (L, FOX_HEADS), jnp.float32, minval=1.0, maxval=4.0),
        'gdn_conv_w': normal((L, CONV_WIDTH, 3 * GDN_WIDTH), CONV_WIDTH ** -0.5),
        'gdn_a_log': jnp.log(jax.random.uniform(next(ks), (L, GDN_HEADS), jnp.float32, minval=1.0, maxval=16.0)),
        'gdn_dt_bias': dt + jnp.log(-jnp.expm1(-dt)),
        'gdn_out_norm': gain(GDN_HEAD_DIM),
        'w_out': normal((L, MIX_WIDTH, D_MODEL), MIX_WIDTH ** -0.5),
        'mix_post_norm': gain(D_MODEL),
        'mem_pre_norm': gain(D_MODEL),
        'mem_kv_norm': gain(D_MODEL),
        'mem_w_q': normal((L, D_MODEL, D_MODEL), D_MODEL ** -0.5),
        'mem_w_kv': normal((L, D_MODEL, 2 * D_MODEL), D_MODEL ** -0.5),
        'mem_w_o': normal((L, D_MODEL, D_MODEL), D_MODEL ** -0.5),
        'mem_post_norm': gain(D_MODEL),
        'ffn2_pre_norm': gain(D_MODEL),
        'ffn2_w_gate': normal((L, D_MODEL, D_FF), D_MODEL ** -0.5),
        'ffn2_w_up': normal((L, D_MODEL, D_FF), D_MODEL ** -0.5),
        'ffn2_w_down': normal((L, D_FF, D_MODEL), D_FF ** -0.5),
        'ffn2_post_norm': gain(D_MODEL),
    }


def reference(x, mem, ffn1_pre_norm, ffn1_w_gate, ffn1_w_up, ffn1_w_down, ffn1_post_norm,
              mix_pre_norm, w_in, fox_f_bias, gdn_conv_w, gdn_a_log, gdn_dt_bias, gdn_out_norm,
              w_out, mix_post_norm, mem_pre_norm, mem_kv_norm, mem_w_q, mem_w_kv, mem_w_o,
              mem_post_norm, ffn2_pre_norm, ffn2_w_gate, ffn2_w_up, ffn2_w_down, ffn2_post_norm):
    h = x
    for l in range(DEPTH):
        f = swiglu(rms_norm(h, ffn1_pre_norm[l]), ffn1_w_gate[l], ffn1_w_up[l], ffn1_w_down[l])
        h = h + FFN_RESIDUAL_WEIGHT * rms_norm(f, ffn1_post_norm[l])
        mixed = hybrid_mixer(rms_norm(h, mix_pre_norm[l]), w_in[l], fox_f_bias[l], gdn_conv_w[l],
                             gdn_a_log[l], gdn_dt_bias[l], gdn_out_norm[l])
        h = h + rms_norm(mixed @ w_out[l], mix_post_norm[l])
        c = memory_cross_attention(rms_norm(h, mem_pre_norm[l]), rms_norm(mem, mem_kv_norm[l]),
                                   mem_w_q[l], mem_w_kv[l], mem_w_o[l])
        h = h + rms_norm(c, mem_post_norm[l])
        f = swiglu(rms_norm(h, ffn2_pre_norm[l]), ffn2_w_gate[l], ffn2_w_up[l], ffn2_w_down[l])
        h = h + FFN_RESIDUAL_WEIGHT * rms_norm(f, ffn2_post_norm[l])
    return h
```

```python
import os
from contextlib import ExitStack

import numpy as np
import concourse.bass as bass
import concourse.mybir as mybir
from concourse.bass_utils import run_bass_kernel_spmd

F32 = mybir.dt.float32
BF16 = mybir.dt.bfloat16
ALU = mybir.AluOpType
AF = mybir.ActivationFunctionType

NCORES = 8
S = int(os.environ.get('KSEQ', '8192'))
D = 1024
DC = D // 128
DFF = 2816
FC = DFF // 128
NMEM = 256
EPS = 1e-6
WIN = 3600

ENGS = ("pe", "act", "dve", "pool", "sp")


class Op:
    __slots__ = ("eng", "fn", "reads", "writes", "idx", "deps", "marked", "dma", "semkey", "cost", "lat",
                 "alldeps", "pos", "rank")

    def __init__(self, eng, fn, reads, writes, dma=False, semkey=None, cost=100.0, lat=0.0):
        self.eng = eng
        self.fn = fn
        self.reads = reads
        self.writes = writes
        self.deps = []
        self.alldeps = []
        self.marked = False
        self.dma = dma
        self.semkey = semkey
        self.cost = cost
        self.lat = lat
        self.pos = -1
        self.rank = 0


class Sched:
    SEM_LAT = 250.0
    WINDOW = int(os.environ.get("KWIN", "256"))

    def __init__(self, nc, stack):
        self.nc = nc
        self.stack = stack
        self.ops = []
        self.last_writer = {}
        self.readers = {}
        self.dma_count = {}
        self.dma_sems = {}
        self.seg_start = 0
        self.barrier_idx = -1
        self.eng_sem = {e: stack.enter_context(nc.semaphore("s_" + e)) for e in ENGS}
        self.eng_count = {e: 0 for e in ENGS}
        self.waited = {e: {} for e in ENGS}
        self.opval = {}
        self.reorder = os.environ.get("KREORDER", "1") == "1"
        self.recording = None

    def record(self, thunks):
        self.recording = []
        for th in thunks:
            th()
        rec, self.recording = self.recording, None
        return rec

    def replay(self, rec, lo, hi):
        for a in rec[lo:hi]:
            self.add(*a)

    def add(self, eng, fn, reads=(), writes=(), dma=False, semkey=None, cost=100.0, lat=0.0):
        if self.recording is not None:
            self.recording.append((eng, fn, reads, writes, dma, semkey, cost, lat))
            return None
        if dma:
            writes = tuple(writes) + (("semchain", semkey),)
        op = Op(eng, fn, tuple(reads), tuple(writes), dma, semkey, cost, lat)
        op.idx = len(self.ops)
        deps = {}
        for r in op.reads:
            w = self.last_writer.get(r)
            if w is not None:
                deps[w.idx] = w
        for k in op.writes:
            w = self.last_writer.get(k)
            if w is not None:
                deps[w.idx] = w
            for rd in self.readers.get(k, ()):
                deps[rd.idx] = rd
        for r in op.reads:
            self.readers.setdefault(r, []).append(op)
        for k in op.writes:
            self.last_writer[k] = op
            self.readers[k] = []
        op.alldeps = [d for d in deps.values() if d.idx > self.barrier_idx and d is not op]
        self.ops.append(op)
        if dma:
            if semkey not in self.dma_sems:
                self.dma_sems[semkey] = self.stack.enter_context(
                    self.nc.semaphore("d_%d" % len(self.dma_sems)))
            self.dma_count[semkey] = self.dma_count.get(semkey, 0) + 1
            op.rank = self.dma_count[semkey]
        return op

    def barrier(self):
        self.barrier_idx = len(self.ops) - 1

    def _schedule(self, seg, reorder=True):
        pend = {e: [o for o in seg if o.eng == e] for e in ENGS}
        if not (self.reorder and reorder):
            return pend
        order = {e: [] for e in ENGS}
        head = {e: 0 for e in ENGS}
        free = {e: 0.0 for e in ENGS}
        finish = {}
        done = set()
        nleft = len(seg)
        W = self.WINDOW
        LAT = self.SEM_LAT
        fixed = set(os.environ.get("KFIXED", "act,pe").split(",")) if reorder == "gdn" else set()
        while nleft:
            best = None
            bkey = None
            for e in ENGS:
                lst = pend[e]
                h = head[e]
                n = len(lst)
                while h < n and lst[h].idx in done:
                    h += 1
                head[e] = h
                fe = free[e]
                cnt = 0
                i = h
                We = 1 if e in fixed else W
                while i < n and cnt < We:
                    op = lst[i]
                    i += 1
                    if op.idx in done:
                        continue
                    cnt += 1
                    ready = fe
                    ok = True
                    for d in op.alldeps:
                        f = finish.get(d.idx)
                        if f is None:
                            ok = False
                            break
                        if d.dma or d.eng != e:
                            f += LAT
                        if f > ready:
                            ready = f
                    if not ok:
                        continue
                    key = (ready, op.idx)
                    if bkey is None or key < bkey:
                        bkey = key
                        best = op
                    if ready <= fe:
                        break
            op = best
            start = bkey[0]
            e = op.eng
            free[e] = start + op.cost
            finish[op.idx] = start + op.cost + op.lat
            done.add(op.idx)
            order[e].append(op)
            nleft -= 1
        self.sim_time = max(finish.values()) if finish else 0.0
        if os.environ.get("KVERB"):
            busy = {e: round(sum(o.cost for o in order[e]) / 1e3) for e in ENGS}
            print("segment ops", len(seg), "sim_us", round(self.sim_time / 1e3), "busy_us", busy, flush=True)
        return order

    def emit(self, reorder=True):
        nc = self.nc
        seg = self.ops[self.seg_start:]
        self.seg_start = len(self.ops)
        order = self._schedule(seg, reorder)
        for e in ENGS:
            for p, op in enumerate(order[e]):
                op.pos = p
        for op in seg:
            best = {}
            for d in op.alldeps:
                if d.dma:
                    key = ("d", d.semkey)
                    if key not in best or best[key].rank < d.rank:
                        best[key] = d
                else:
                    if d.eng == "pe" and op.eng == "pe" and not op.dma:
                        continue
                    key = ("e", d.eng)
                    if key not in best or best[key].pos < d.pos:
                        best[key] = d
            op.deps = list(best.values())
            for d in op.deps:
                if not d.dma:
                    d.marked = True
        for e in ENGS:
            for op in order[e]:
                if not op.dma and op.marked:
                    self.eng_count[e] += 1
                    self.opval[op.idx] = self.eng_count[e]
        seg_dma = {}
        for op in seg:
            if op.dma:
                seg_dma[op.semkey] = max(seg_dma.get(op.semkey, 0), op.rank)

        def run(engname, eng):
            waited = self.waited[engname]
            for op in order[engname]:
                for d in op.deps:
                    if d.dma:
                        sem = self.dma_sems[d.semkey]
                        val = 16 * d.rank
                        key = ("d", d.semkey)
                    else:
                        sem = self.eng_sem[d.eng]
                        val = self.opval[d.idx]
                        key = ("e", d.eng)
                    if waited.get(key, 0) < val:
                        eng.wait_ge(sem, val)
                        waited[key] = val
                ins = op.fn(eng)
                if op.dma:
                    ins.then_inc(self.dma_sems[op.semkey], 16)
                elif op.marked:
                    ins.then_inc(self.eng_sem[engname], 1)
            if engname == "sp":
                for semkey, rank in seg_dma.items():
                    key = ("d", semkey)
                    if waited.get(key, 0) < 16 * rank:
                        eng.wait_ge(self.dma_sems[semkey], 16 * rank)
                        waited[key] = 16 * rank

        with nc.Block() as block:
            @block.tensor
            def _(e):
                run("pe", e)

            @block.scalar
            def _(e):
                run("act", e)

            @block.vector
            def _(e):
                run("dve", e)

            @block.gpsimd
            def _(e):
                run("pool", e)

            @block.sync
            def _(e):
                run("sp", e)


class KB:
    def __init__(self, debug_stage=None):
        self.debug_stage = debug_stage
        self.nc = bass.Bass("TRN2", target_bir_lowering=False)
        self.top = ExitStack()
        self.sch = None
        self.phase = None
        self.nm = 0

    def sb(self, shape, dtype, stack=None, name=None):
        self.nm += 1
        st = stack if stack is not None else self.phase
        return st.enter_context(self.nc.sbuf_tensor(name or ("t%d" % self.nm), list(shape), dtype))

    def ps(self, shape, dtype, stack=None, name=None):
        self.nm += 1
        st = stack if stack is not None else self.phase
        return st.enter_context(self.nc.psum_tensor(name or ("p%d" % self.nm), list(shape), dtype))

    def dram(self, name, shape, dtype, kind="Internal"):
        return self.nc.dram_tensor(name, list(shape), dtype, kind=kind)

    @staticmethod
    def _fsz(ap):
        n = 1
        for d in ap.shape[1:]:
            n *= d
        return n

    @staticmethod
    def _psum(ap):
        return type(ap.tensor).__name__.startswith("PSum")

    def mm(self, out, lhsT, rhs, start, stop, r, w):
        n = max(64, self._fsz(rhs))
        c = n / 2.2 * (4.0 if rhs.dtype == F32 else 1.0) + 8
        self.sch.add("pe", lambda e: e.matmul(out, lhsT, rhs, start=start, stop=stop), r, w, cost=c, lat=60.0)

    def tr(self, out, in_, ident, r, w):
        self.sch.add("pe", lambda e: e.transpose(out, in_, ident), r, w, cost=70.0, lat=60.0)

    def _ecost(self, eng, out, ins, kind):
        n = self._fsz(out)
        ps = any(self._psum(a) for a in ins) or self._psum(out)
        if eng == "act":
            return (n + (180 if ps else 230)) / 1.2
        if eng == "pool":
            return (n + 160) / 1.0
        acc = 1.0
        if kind == "single" and not ps:
            acc = 4.0 if all(a.dtype == BF16 for a in ins) and out.dtype == BF16 else 2.0
        elif kind == "tt" and not ps and all(a.dtype == BF16 for a in ins):
            acc = 2.0
        return (n / acc + (120 if ps else 60)) / 0.96

    def act(self, out, in_, func, r, w, bias=None, scale=None, eng="act"):
        kw = {}
        if bias is not None:
            kw["bias"] = bias
        if scale is not None:
            kw["scale"] = scale
        self.sch.add(eng, lambda e: e.activation(out, in_, func, **kw), r, w, cost=self._ecost(eng, out, (in_,), "act"))

    def copy(self, eng, out, in_, r, w):
        c = self._ecost(eng, out, (in_,), "single")
        if eng == "act":
            self.sch.add("act", lambda e: e.copy(out, in_), r, w, cost=c)
        else:
            self.sch.add(eng, lambda e: e.tensor_copy(out, in_), r, w, cost=c)

    def tt(self, eng, out, a, b, op, r, w):
        self.sch.add(eng, lambda e: e.tensor_tensor(out, a, b, op), r, w, cost=self._ecost(eng, out, (a, b), "tt"))

    def ts(self, eng, out, a, s1, s2, op0, op1, r, w):
        c = self._ecost(eng, out, (a,), "single")
        if op1 is None:
            self.sch.add(eng, lambda e: e.tensor_scalar(out, a, s1, None, op0), r, w, cost=c)
        else:
            self.sch.add(eng, lambda e: e.tensor_scalar(out, a, s1, s2, op0, op1), r, w, cost=c)

    def stt(self, eng, out, a, s, b, op0, op1, r, w):
        self.sch.add(eng, lambda e: e.scalar_tensor_tensor(out, a, s, b, op0, op1), r, w,
                     cost=self._ecost(eng, out, (a, b), "tt"))

    def dma(self, eng, out, in_, r, w, semkey):
        nbytes = 1
        for d in out.shape:
            nbytes *= d
        nbytes *= 2 if out.dtype == BF16 else 4
        issue = 1000.0 if eng == "pool" else 100.0
        self.sch.add(eng, lambda e: e.dma_start(out, in_), r, w, dma=True, semkey=semkey,
                     cost=issue, lat=2000.0 + nbytes / 150.0)

    def build(self):
        nc = self.nc
        with self.top:
            with nc.allow_low_precision("bf16 matmul operands, fp32 accumulation"):
                self._build()
        return nc

    def _build(self):
        nc = self.nc
        top = self.top
        self.sch = Sched(nc, top)
        dbg = self.debug_stage
        X = self.dram("x", [S, D], F32, "ExternalInput")
        MEM = self.dram("mem", [NMEM, D], F32, "ExternalInput")
        VECS = self.dram("vecs", [128, 128], F32, "ExternalInput")
        self.W = {}
        for nm, shp in (("ffn1_w_gate", [D, DFF]), ("ffn1_w_up", [D, DFF]), ("ffn1_w_down", [DFF, D]),
                        ("w_in", [D, WIN]), ("w_out", [D, D]), ("mem_w_q", [D, D]), ("mem_w_kv", [D, 2 * D]),
                        ("mem_w_o", [D, D]),
                        ("ffn2_w_gate", [D, DFF]), ("ffn2_w_up", [D, DFF]), ("ffn2_w_down", [DFF, D])):
            self.W[nm] = self.dram(nm, shp, F32, "ExternalInput")
        kind_dbg = "ExternalOutput" if dbg else "Internal"
        self.H1T = self.dram("h1t", [D, S], F32, "ExternalOutput" if dbg in (1, 5) else "Internal")
        self.UNT = self.dram("unt", [D, S], BF16, kind_dbg if dbg == 1 else "Internal")
        OUT = self.dram("out", [S, D], F32, "ExternalOutput")
        SMALL = self.dram("small", [8, 4], F32, "ExternalInput")
        self.SMALL = SMALL
        dk = lambda st: "ExternalOutput" if dbg == st else "Internal"
        self.QA = self.dram("qa", [8, 70, S], BF16, dk(2))
        self.KA = self.dram("ka", [8, 70, S], BF16, dk(2))
        self.VA = self.dram("va", [S, 8, 128], BF16, dk(2))
        self.MF = self.dram("mf", [8, 64, S], BF16, dk(3))
        self.MG = self.dram("mg", [4, 128, S], BF16, dk(4))
        self.GQT = self.dram("gqt", [4, 128, S], F32, dk(2))
        self.GKT = self.dram("gkt", [4, 128, S], F32, dk(2))
        self.GVT = self.dram("gvt", [4, 128, S], F32, dk(2))
        self.ZT = self.dram("zt", [4, 128, S], BF16, dk(2))
        self.BGT = self.dram("bgt", [8, S], F32, dk(2))
        self.X, self.MEM, self.VECS, self.OUT = X, MEM, VECS, OUT

        self.ident = self.sb([128, 128], F32, stack=top, name="ident")
        self.ones_bf = self.sb([128, 128], BF16, stack=top, name="ones_bf")
        self.vec = self.sb([128, 128], F32, stack=top, name="vecT")
        self.setup_consts()
        self.phase_ffn(1)
        if dbg == 1:
            return
        self.phase_proj()
        if dbg == 2:
            return
        self.phase_fox()
        if dbg == 3:
            return
        self.phase_gdn()
        if dbg == 4:
            return
        self.phase_mix_mem()
        if dbg == 5:
            return
        self.phase_ffn(2)

    VROW = {"ffn1_pre_norm": 0, "ffn1_post_norm": 8, "mix_pre_norm": 16, "mix_post_norm": 24,
            "mem_pre_norm": 32, "mem_kv_norm": 40, "mem_post_norm": 48, "ffn2_pre_norm": 56,
            "ffn2_post_norm": 64, "gdn_conv_w": 72, "gdn_out_norm": 120}

    def setup_consts(self):
        nc = self.nc
        with ExitStack() as ph:
            self.phase = ph
            ones = self.sb([128, 128], F32)
            vs = self.sb([128, 128], F32)
            pt = self.ps([128, 128], F32)
            ident, ones_bf, vec = self.ident, self.ones_bf, self.vec
            self.sch.add("pool", lambda e: e.memset(ones[:], 1.0), (), ("ones",))
            self.sch.add("pool", lambda e: e.affine_select(ident[:], ones[:], [[-1, 128]], ALU.is_equal, 0.0,
                                                           base=0, channel_multiplier=1),
                         ("ones",), ("ident",))
            self.sch.add("pool", lambda e: e.memset(ones_bf[:], 1.0 / 1024.0), (), ("ones_bf",))
            self.dma("sp", vs[:], self.VECS.ap(), (), ("vs",), "vs")
            self.tr(pt[:], vs[:], ident[:], ("vs", "ident"), ("pt",))
            self.copy("dve", vec[:], pt[:], ("pt",), ("vec",))
            for nm in ("ffn1_post_norm", "ffn2_post_norm"):
                r0 = self.VROW[nm]
                self.sch.add("dve", lambda e, r0=r0: e.tensor_scalar(vec[:, r0:r0 + 8], vec[:, r0:r0 + 8], 0.5, None,
                                                                     ALU.mult), ("vec",), ("vec",))
            self.sch.barrier()
            self.sch.emit()
        self.phase = None

    def load_weight(self, wt, dram, kc, ncols, key, col0=0, row0=0, nsplit=None):
        src = dram.ap()[row0:row0 + kc * 128, col0:col0 + ncols].rearrange("(k p) n -> p k n", p=128)
        keys = []
        for k in range(kc):
            c0 = 0
            while c0 < ncols:
                c1 = min(ncols, c0 + 2816)
                self.dma("pool", wt[:, k, c0:c1], src[:, k, c0:c1], (), ((key, k),), key)
                c0 = c1
            keys.append((key, k))
        return keys

    def rms_stats(self, src, sq, st_ps, rstd, keys_src, key_sq, key_st, key_rstd, T, sq_eng="dve"):
        self.tt(sq_eng, sq[:], src[:], src[:], ALU.mult, keys_src, (key_sq,))
        for c in range(DC):
            self.mm(st_ps, self.ones_bf[:], sq[:, c, :], c == 0, c == DC - 1, (key_sq, "ones_bf"), (key_st,))
        self.act(rstd[:], st_ps, AF.Ln, (key_st,), (key_rstd,), bias=EPS)
        self.act(rstd[:], rstd[:], AF.Exp, (key_rstd,), (key_rstd,), scale=-0.5)

    def phase_ffn(self, which):
        nc = self.nc
        T = 256
        NT = S // T
        pre = "ffn%d_" % which
        with ExitStack() as ph:
            self.phase = ph
            wg = self.sb([128, DC, DFF], BF16)
            wu = self.sb([128, DC, DFF], BF16)
            wd = self.sb([128, FC, D], BF16)
            kg = self.load_weight(wg, self.W[pre + "w_gate"], DC, DFF, "wg")
            ku = self.load_weight(wu, self.W[pre + "w_up"], DC, DFF, "wu")
            kd = self.load_weight(wd, self.W[pre + "w_down"], FC, D, "wd")
            xs = self.sb([128, 2, D], F32)
            hT = [self.sb([128, DC, T], F32) for _ in range(2)]
            xn = [self.sb([128, DC, T], BF16) for _ in range(2)]
            sq = self.sb([128, DC, T], BF16)
            rstd = [self.sb([128, T], F32) for _ in range(2)]
            sg = [self.sb([128, T], F32) for _ in range(2)]
            a = self.sb([128, FC, T], BF16)
            f = self.sb([128, DC, T], F32)
            un = self.sb([128, DC, T], BF16)
            tp = [self.ps([128, 512], F32) for _ in range(1)]
            stp = self.ps([128, 512], F32)
            gp = [self.ps([128, 512], F32) for _ in range(2)]
            up = [self.ps([128, 512], F32) for _ in range(2)]
            dn = [self.ps([128, 512], F32) for _ in range(2)]
            vec = self.vec
            g1 = self.VROW[pre + "pre_norm"]
            g2 = self.VROW[pre + "post_norm"]
            g3 = self.VROW["mix_pre_norm"]
            ident = self.ident
            X = self.X.ap()
            H1T = self.H1T.ap().rearrange("(c p) t -> p c t", p=128)
            UNT = self.UNT.ap().rearrange("(c p) t -> p c t", p=128)

            def stage_a(i):
                sl = i % 2
                t0 = i * T
                if which == 2:
                    self.dma("sp", hT[sl][:], H1T[:, :, t0:t0 + T], (("H1T", i),),
                             tuple(("hT", sl, cp) for cp in range(4)), ("hTl", sl))
                else:
                    self.dma("sp", xs[:], X[t0:t0 + T, :].rearrange("(s p) d -> p s d", p=128), (), ("xs",), "xs")
                    for cp in range(4):
                        bank = tp[0]
                        bk = ("tp", 0)
                        for cc in range(2):
                            c = 2 * cp + cc
                            for s in range(2):
                                self.tr(bank[:, cc * 256 + s * 128: cc * 256 + (s + 1) * 128],
                                        xs[:, s, c * 128:(c + 1) * 128], ident[:], ("xs", "ident"), (bk,))
                        self.copy("act", hT[sl][:, 2 * cp:2 * cp + 2, :], bank[:], (bk,), (("hT", sl, cp),))
                hk = tuple(("hT", sl, cp) for cp in range(4))
                self.rms_stats(hT[sl], sq, stp[:, 0:T], rstd[0], hk, "sq", "st", ("rstd", 0), T)
                for c in range(DC):
                    self.stt("dve", xn[sl][:, c, :], hT[sl][:, c, :], vec[:, g1 + c:g1 + c + 1], rstd[0][:],
                             ALU.mult, ALU.mult, (("hT", sl, c // 2), ("rstd", 0), "vec"), (("xn", sl, c),))

            def stage_gu(i):
                sl = i % 2
                xk = tuple(("xn", sl, c) for c in range(DC))
                ksub = int(os.environ.get("KSUB", "9"))
                for ft in range(int(os.environ.get("KFT", FC))):
                    b = ft % 2
                    gbank = gp[b]
                    ubank = up[b]
                    for k in range(DC):
                        self.mm(gbank[:, 0:T], wg[:, k, ft * 128:(ft + 1) * 128], xn[sl][:, k, :],
                                k == 0, k == DC - 1, (("xn", sl, k), kg[k]), (("gp", b),))
                    for k in range(DC):
                        self.mm(ubank[:, 0:T], wu[:, k, ft * 128:(ft + 1) * 128], xn[sl][:, k, :],
                                k == 0, k == DC - 1, (("xn", sl, k), ku[k]), (("up", b),))
                    s2 = ft % int(os.environ.get('KSG', '2'))
                    if ksub < 1:
                        continue
                    self.act(sg[s2][:], gbank[:, 0:T], AF.Exp, (("gp", b),), (("sg", s2),), scale=-1.0)
                    self.act(sg[s2][:], sg[s2][:], AF.Ln, (("sg", s2),), (("sg", s2),), bias=1.0)
                    self.act(sg[s2][:], sg[s2][:], AF.Exp, (("sg", s2),), (("sg", s2),), scale=-1.0)
                    if ksub < 2:
                        continue
                    self.tt("dve", sg[s2][:], sg[s2][:], gbank[:, 0:T], ALU.mult,
                            (("sg", s2), ("gp", b)), (("sg", s2),))
                    self.tt("dve", a[:, ft, :], sg[s2][:], ubank[:, 0:T], ALU.mult,
                            (("sg", s2), ("up", b)), (("a", ft),))

            def stage_d(i):
                ak = tuple(("a", ft) for ft in range(FC))
                for dp in range(4):
                    bank = dn[dp % 2]
                    bk = ("dn", dp % 2)
                    for half in range(2):
                        dt = 2 * dp + half
                        for k in range(FC):
                            self.mm(bank[:, half * T:(half + 1) * T], wd[:, k, dt * 128:(dt + 1) * 128], a[:, k, :],
                                    k == 0, k == FC - 1, (("a", k), kd[k]), (bk,))
                    self.copy("act", f[:, 2 * dp:2 * dp + 2, :], bank[:], (bk,), (("f", dp),))

            def stage_post(i):
                sl = i % 2
                t0 = i * T
                fk = tuple(("f", dp) for dp in range(4))
                hk = tuple(("hT", sl, cp) for cp in range(4))
                self.rms_stats(f, sq, stp[:, 0:T], rstd[1], fk, "sq", "st", ("rstd", 1), T)
                for c in range(DC):
                    self.stt("dve", f[:, c, :], f[:, c, :], vec[:, g2 + c:g2 + c + 1], rstd[1][:],
                             ALU.mult, ALU.mult, (("f", c // 2), ("rstd", 1), "vec"), (("f", c // 2),))
                self.tt("dve", hT[sl][:], hT[sl][:], f[:], ALU.add, fk + hk, hk)
                if which == 2:
                    OUTA = self.OUT.ap()
                    for s_ in range(2):
                        for half in range(2):
                            for cc in range(4):
                                c = 4 * half + cc
                                self.tr(tp[0][:, cc * 128:(cc + 1) * 128], hT[sl][:, c, s_ * 128:(s_ + 1) * 128], ident[:],
                                        hk + ("ident",), (("tp", 0),))
                            self.copy("act", xs[:, s_, half * 512:(half + 1) * 512], tp[0][:], (("tp", 0),), ("xs",))
                    self.dma("pool", OUTA[t0:t0 + T, :].rearrange("(s p) d -> p s d", p=128), xs[:], ("xs",),
                             (("OUT", i),), "st_out")
                    return
                self.dma("pool", H1T[:, :, t0:t0 + T], hT[sl][:], hk, (("H1T", i),), ("st_h", sl))
                if which == 1:
                    self.rms_stats(hT[sl], sq, stp[:, 0:T], rstd[1], hk, "sq", "st", ("rstd", 1), T)
                    for c in range(DC):
                        self.stt("dve", un[:, c, :], hT[sl][:, c, :], vec[:, g3 + c:g3 + c + 1], rstd[1][:],
                                 ALU.mult, ALU.mult, (("hT", sl, c // 2), ("rstd", 1), "vec"), ("un",))
                    self.dma("pool", UNT[:, :, t0:t0 + T], un[:], ("un",), (("UNT", i),), "st_un")

            nt = NT if not self.debug_stage or os.environ.get("KFULL") else int(os.environ.get("KNT", "2"))
            self.nt_dbg = nt
            kstop = int(os.environ.get("KSTOP", "99"))
            if kstop >= 1:
                stage_a(0)
            for i in range(nt):
                if kstop >= 2:
                    stage_gu(i)
                if i + 1 < nt and kstop >= 1:
                    stage_a(i + 1)
                if kstop >= 3:
                    stage_d(i)
                if kstop >= 4:
                    stage_post(i)
            self.sch.barrier()
            self.sch.emit()
        self.phase = None


    def phase_proj(self):
        T = 512
        NT = S // T
        with ExitStack() as ph:
            self.phase = ph
            win = self.sb([128, DC, WIN], BF16)
            kw = self.load_weight(win, self.W["w_in"], DC, WIN, "win")
            un = [self.sb([128, DC, T], BF16) for _ in range(2)]
            qst = [self.sb([128, T], BF16) for _ in range(2)]
            vst = self.sb([128, 4, 8, 128], BF16)
            small = self.sb([8, 4], F32)
            fe = self.sb([8, T], F32)
            G = [self.sb([8, T], F32) for _ in range(2)]
            g8 = self.sb([8, T], F32)
            r1 = self.sb([8, T], F32)
            gk3 = self.sb([8, 3, T], BF16)
            ng3 = self.sb([8, 3, T], BF16)
            ones3 = self.sb([8, 3, T], BF16)
            onesf = self.sb([8, T], F32)
            pq = [self.ps([128, 512], F32) for _ in range(4)]
            pv = [self.ps([128, 512], F32) for _ in range(2)]
            pf = self.ps([128, 512], F32)
            pn = self.ps([128, 512], F32)
            cb = [self.sb([128, T + 3], F32) for _ in range(12)]
            acc = [self.sb([128, T], F32) for _ in range(2)]
            sgm = [self.sb([128, T], F32) for _ in range(2)]
            sqb = self.sb([128, T], BF16)
            rs = self.sb([128, T], F32)
            yst = [self.sb([128, T], F32) for _ in range(2)]
            zst = [self.sb([128, T], BF16) for _ in range(2)]
            ge = self.sb([8, T], F32)
            gr = self.sb([8, T], F32)
            gg = self.sb([8, T], F32)
            onec = self.sb([8, 1], F32)
            vec = self.vec
            cw0 = self.VROW["gdn_conv_w"]
            GQT, GKT, GVT, ZT, BGT = self.GQT.ap(), self.GKT.ap(), self.GVT.ap(), self.ZT.ap(), self.BGT.ap()
            for j in range(12):
                self.sch.add("pool", lambda e, j=j: e.memset(cb[j][:, 0:3], 0.0), (), (("cb", j),))
            UNT = self.UNT.ap().rearrange("(c p) t -> p c t", p=128)
            QA, KA, VA = self.QA.ap(), self.KA.ap(), self.VA.ap()
            self.sch.add("pool", lambda e: e.memset(vst[:], 1.0), (), tuple(("vst", s_) for s_ in range(4)))
            self.sch.add("pool", lambda e: e.memset(ones3[:], 1.0), (), ("ones3",))
            self.sch.add("pool", lambda e: e.memset(onesf[:], 1.0), (), ("onesf",))
            self.dma("sp", small[:], self.SMALL.ap(), (), ("small",), "small")
            self.ts("dve", small[:, 3:4], small[:, 0:1], -1.0, None, ALU.mult, None, ("small",), ("small",))
            self.sch.add("pool", lambda e: e.memset(onec[:], 1.0), (), ("onec",))
            gsc = self.sb([8, 2], F32)
            self.sch.add("pool", lambda e: e.affine_select(gsc[:, 0:1], onec[:], [[0, 1]], ALU.is_ge, -1.0,
                                                           base=-4, channel_multiplier=1), ("onec",), ("gsc",))
            self.act(gsc[:, 1:2], small[:, 1:2], AF.Exp, ("small", "gsc"), ("gsc",))
            self.ts("dve", gsc[:, 1:2], gsc[:, 1:2], -1.0, None, ALU.mult, None, ("gsc",), ("gsc",))
            ycnt = 0
            for i in range(NT):
                sl = i % 2
                t0 = i * T
                self.dma("sp", un[sl][:], UNT[:, :, t0:t0 + T], (("UNT", i),), (("un", sl),), ("un", sl))
                uk = ("un", sl)
                for g in range(8):
                    bank = pq[g % 4]
                    bk = ("pq", g % 4)
                    for k in range(DC):
                        self.mm(bank[:, 0:T], win[:, k, 128 * g:128 * g + 128], un[sl][:, k, :],
                                k == 0, k == DC - 1, (uk, kw[k]), (bk,))
                    self.copy("act", qst[g % 2][:], bank[:, 0:T], (bk,), (("qst", g % 2),))
                    dst = QA if g < 4 else KA
                    h0 = 2 * (g % 4)
                    for hh in range(2):
                        self.dma("pool", dst[h0 + hh, 0:64, t0:t0 + T], qst[g % 2][64 * hh:64 * hh + 64, :],
                                 (("qst", g % 2),), (("QK", g, hh, i),), ("qst", g % 2))
                for s_ in range(4):
                    bank = pv[s_ % 2]
                    bk = ("pv", s_ % 2)
                    for k in range(DC):
                        self.mm(bank[:, 0:512], un[sl][:, k, s_ * 128:(s_ + 1) * 128], win[:, k, 1024:1536],
                                k == 0, k == DC - 1, (uk, kw[k]), (bk,))
                    self.copy("dve", vst[:, s_, :, 0:64], bank[:, 0:512].rearrange("p (h c) -> p h c", c=64),
                              (bk,), (("vst", s_),))
                self.dma("pool", VA[t0:t0 + T].rearrange("(s p) h c -> p s h c", p=128), vst[:],
                         tuple(("vst", s_) for s_ in range(4)), (("VA", i),), "vst")
                for k in range(DC):
                    self.mm(pf[0:8, 0:T], win[:, k, 1536:1544], un[sl][:, k, :], k == 0, k == DC - 1, (uk, kw[k]), ("pf",))
                self.act(fe[:], pf[0:8, 0:T], AF.Exp, ("pf", "small"), ("fe",), bias=small[:, 3:4], scale=-1.0)
                self.act(fe[:], fe[:], AF.Ln, ("fe",), ("fe",), bias=1.0)
                init = 0.0 if i == 0 else G[1 - sl][:, T - 1:T]
                self.sch.add("dve", lambda e, sl=sl, init=init: e.tensor_tensor_scan(G[sl][:], onesf[:], fe[:], init,
                                                                                   ALU.mult, ALU.add),
                             ("fe", "onesf", ("G", 1 - sl)), (("G", sl),), cost=1100.0)
                self.ts("dve", g8[:], G[sl][:], 8.0, None, ALU.mult, None, (("G", sl),), ("g8",))
                self.copy("dve", gk3[:, 0, :], g8[:], ("g8",), ("gk3",))
                self.tt("dve", r1[:], g8[:], gk3[:, 0, :], ALU.subtract, ("g8", "gk3"), ("r1",))
                self.copy("dve", gk3[:, 1, :], r1[:], ("r1",), ("gk3",))
                self.tt("dve", r1[:], r1[:], gk3[:, 1, :], ALU.subtract, ("r1", "gk3"), ("r1",))
                self.copy("dve", gk3[:, 2, :], r1[:], ("r1",), ("gk3",))
                self.ts("dve", ng3[:], gk3[:], -1.0, None, ALU.mult, None, ("gk3",), ("ng3",))
                self.dma("pool", KA[:, 64:67, t0:t0 + T], gk3[:], ("gk3",), (("KAg", i),), "gk3")
                self.dma("pool", KA[:, 67:70, t0:t0 + T], ones3[:], ("ones3",), (("KAo", i),), "ones3")
                self.dma("pool", QA[:, 64:67, t0:t0 + T], ones3[:], ("ones3",), (("QAo", i),), "ones3")
                self.dma("pool", QA[:, 67:70, t0:t0 + T], ng3[:], ("ng3",), (("QAg", i),), "ng3")
                for j in range(16):
                    bank = pq[j % 4]
                    bk = ("pq", j % 4)
                    c0 = 1544 + 128 * j
                    for k in range(DC):
                        self.mm(bank[:, 0:T], win[:, k, c0:c0 + 128], un[sl][:, k, :], k == 0, k == DC - 1, (uk, kw[k]), (bk,))
                    hh = j % 4
                    if j >= 12:
                        zs = ycnt % 2
                        self.copy("act", zst[zs][:], bank[:, 0:T], (bk,), (("zst", zs),))
                        self.dma("pool", ZT[hh, :, t0:t0 + T], zst[zs][:], (("zst", zs),), (("ZT", hh, i),), ("zst", zs))
                        ycnt += 1
                        continue
                    ck = ("cb", j)
                    self.copy("act", cb[j][:, 3:3 + T], bank[:, 0:T], (bk,), (ck,))
                    a_ = ycnt % 2
                    ak = ("acc", a_)
                    self.ts("dve", acc[a_][:], cb[j][:, 0:T], vec[:, cw0 + j:cw0 + j + 1], None, ALU.mult, None,
                            (ck, "vec"), (ak,))
                    for tap in range(1, 4):
                        self.stt("dve", acc[a_][:], cb[j][:, tap:tap + T], vec[:, cw0 + 12 * tap + j:cw0 + 12 * tap + j + 1],
                                 acc[a_][:], ALU.mult, ALU.add, (ck, ak, "vec"), (ak,))
                    self.copy("dve", cb[j][:, 0:3], cb[j][:, T:T + 3], (ck,), (ck,))
                    sk = ("sgm", a_)
                    self.act(sgm[a_][:], acc[a_][:], AF.Exp, (ak,), (sk,), scale=-1.0)
                    self.act(sgm[a_][:], sgm[a_][:], AF.Ln, (sk,), (sk,), bias=1.0)
                    self.act(sgm[a_][:], sgm[a_][:], AF.Exp, (sk,), (sk,), scale=-1.0)
                    yk = ("yst", a_)
                    self.tt("dve", yst[a_][:], acc[a_][:], sgm[a_][:], ALU.mult, (ak, sk), (yk,))
                    if j < 8:
                        self.tt("dve", sqb[:], yst[a_][:], yst[a_][:], ALU.mult, (yk,), ("sqb",))
                        self.mm(pn[:, 0:T], self.ones_bf[:], sqb[:], True, True, ("sqb", "ones_bf"), ("pn",))
                        self.act(rs[:], pn[:, 0:T], AF.Ln, ("pn",), ("rs",), bias=EPS, scale=1024.0)
                        self.act(rs[:], rs[:], AF.Exp, ("rs",), ("rs",), scale=-0.5,
                                 bias=(-0.5 * float(np.log(128.0)) if j < 4 else 0.0))
                        self.tt("dve", yst[a_][:], yst[a_][:], rs[:], ALU.mult, (yk, "rs"), (yk,))
                    dstT = (GQT, GKT, GVT)[j // 4]
                    self.dma("pool", dstT[hh, :, t0:t0 + T], yst[a_][:], (yk,), (("G3", j, i),), ("yst", a_))
                    ycnt += 1
                for k in range(DC):
                    self.mm(pf[0:8, 0:T], win[:, k, 3592:3600], un[sl][:, k, :], k == 0, k == DC - 1, (uk, kw[k]), ("pf",))
                self.act(ge[:], pf[0:8, 0:T], AF.Exp, ("pf", "gsc", "small"), ("ge",), bias=small[:, 2:3], scale=gsc[:, 0:1])
                self.act(ge[:], ge[:], AF.Ln, ("ge",), ("ge",), bias=1.0)
                self.act(gr[:], ge[:], AF.Exp, ("ge",), ("gr",), scale=-1.0)
                self.ts("dve", gg[:], ge[:], gsc[:, 1:2], None, ALU.mult, None, ("ge", "gsc"), ("gg",))
                self.dma("pool", BGT[0:4, t0:t0 + T], gr[0:4, :], ("gr",), (("BGb", i),), "gr")
                self.dma("pool", BGT[4:8, t0:t0 + T], gg[4:8, :], ("gg",), (("BGg", i),), "gg")
            self.sch.barrier()
            self.sch.emit()
        self.phase = None

    def phase_fox(self):
        T = 512
        NT = S // T
        NS = S // 128
        with ExitStack() as ph:
            self.phase = ph
            ka = [self.sb([70, S], BF16) for _ in range(2)]
            va = [self.sb([128, NS, 128], BF16) for _ in range(2)]
            qa = [self.sb([70, T], BF16) for _ in range(2)]
            pT = [self.sb([128, 2 * T], BF16) for _ in range(3)]
            rec = self.sb([128, T], F32)
            osb = self.sb([64, T], F32)
            mfst = [self.sb([64, T], BF16) for _ in range(2)]
            onesq = self.sb([128, 64], F32)
            shf = self.sb([128, 64], F32)
            sc = [self.ps([128, 1024], F32) for _ in range(2)]
            ob = [self.ps([128, 512], F32) for _ in range(2)]
            rb = self.ps([128, 512], F32)
            QA, KA, VA, MF = self.QA.ap(), self.KA.ap(), self.VA.ap(), self.MF.ap()
            self.sch.add("pool", lambda e: e.memset(onesq[:], 1.0), (), ("onesq",))
            self.sch.add("pool", lambda e: e.affine_select(shf[:], onesq[:], [[-1, 64]], ALU.is_equal, 0.0,
                                                           base=-64, channel_multiplier=1), ("onesq",), ("shf",))
            LOOK = 2
            tiles = [(h, i) for h in range(8) for i in range(NT)]

            def load_head(h):
                hs = h % 2
                self.dma("sp", ka[hs][:], KA[h], (), (("ka", hs),), ("ka", hs))
                self.dma("sp", va[hs][:], VA[:, h, :].rearrange("(j p) c -> p j c", p=128), (), (("va", hs),), ("va", hs))

            def load_q(tix):
                h, i = tiles[tix]
                qs = tix % 2
                self.dma("sp", qa[qs][:], QA[h, :, i * T:(i + 1) * T], (), (("qa", qs),), ("qa", qs))

            items = []
            for tix, (h, i) in enumerate(tiles):
                nj = 4 * i + 4
                for j in range(0, 4 * i, 2):
                    items.append((tix, h, i, (j, j + 1), nj))
                for j in range(4 * i, nj):
                    items.append((tix, h, i, (j,), nj))

            def front(k):
                tix, h, i, js, nj = items[k]
                hs, qs = h % 2, tix % 2
                if js[0] == 0 and tix + 1 < len(tiles):
                    load_q(tix + 1)
                b = k % 2
                pb = k % 3
                if len(js) == 2:
                    for u, j in enumerate(js):
                        self.mm(sc[b][:, u * T:(u + 1) * T], ka[hs][:, j * 128:(j + 1) * 128], qa[qs][:, 0:T], True, True,
                                (("ka", hs), ("qa", qs)), (("sc", b),))
                    self.act(pT[pb][:, 0:2 * T], sc[b][:, 0:2 * T], AF.Exp, (("sc", b),), (("pT", pb),), scale=0.125)
                    return
                j = js[0]
                jj = j - 4 * i
                c0 = 128 * jj if jj > 0 else 0
                self.mm(sc[b][:, c0:T], ka[hs][:, j * 128:(j + 1) * 128], qa[qs][:, c0:T], True, True,
                        (("ka", hs), ("qa", qs)), (("sc", b),))
                self.act(pT[pb][:, c0:T], sc[b][:, c0:T], AF.Exp, (("sc", b),), (("pT", pb),), scale=0.125)
                self.sch.add("pool", lambda e, pb=pb, c0=c0: e.affine_select(
                    pT[pb][:, c0:c0 + 128], pT[pb][:, c0:c0 + 128], [[1, 128]], ALU.is_ge, 0.0,
                    base=0, channel_multiplier=-1), (("pT", pb),), (("pT", pb),), cost=300.0)

            def back(k):
                tix, h, i, js, nj = items[k]
                hs = h % 2
                pb = k % 3
                o = ob[tix % 2]
                okey = ("ob", tix % 2)
                if js[0] == 0 and i == 0 and h + 1 < 8:
                    load_head(h + 1)
                if len(js) == 2:
                    for u, j in enumerate(js):
                        self.mm(o[:, 0:T], va[hs][:, j, :], pT[pb][:, u * T:(u + 1) * T], j == 0, False,
                                (("va", hs), ("pT", pb)), (okey,))
                    return
                j = js[0]
                jj = j - 4 * i
                c0 = 128 * jj if jj > 0 else 0
                self.mm(o[:, c0:T], va[hs][:, j, :], pT[pb][:, c0:T], j == 0, j == nj - 1,
                        (("va", hs), ("pT", pb)), (okey,))
                if j == nj - 1:
                    t0 = i * T
                    self.sch.add("dve", lambda e, o=o: e.reciprocal(rec[64:128, :], o[64:128, 0:T]), (okey,), ("rec",), cost=660.0)
                    self.mm(rb[0:64, 0:T], shf[64:128, :], rec[64:128, :], True, True, ("rec", "shf"), ("rb",))
                    self.copy("act", osb[:], o[0:64, 0:T], (okey,), ("osb",))
                    ms = tix % 2
                    self.tt("dve", mfst[ms][:], osb[:], rb[0:64, 0:T], ALU.mult, ("osb", "rb"), (("mfst", ms),))
                    self.dma("pool", MF[h, :, t0:t0 + T], mfst[ms][:], (("mfst", ms),), (("MF", h, i),), ("mfst", ms))

            load_head(0)
            load_q(0)
            n_items = len(items)
            for k in range(min(LOOK, n_items)):
                front(k)
            for k in range(n_items):
                if k + LOOK < n_items:
                    front(k + LOOK)
                back(k)
            self.sch.barrier()
            self.sch.emit()
        self.phase = None


    def phase_mix_mem(self):
        T = 256
        NT = S // T
        MH = 4
        with ExitStack() as ph:
            self.phase = ph
            vec = self.vec
            ident = self.ident
            wof = self.sb([64, 8, D], BF16)
            wog = self.sb([128, 4, D], BF16)
            wq = self.sb([128, DC, D], BF16)
            wo = self.sb([128, DC, D], BF16)
            kmT = self.sb([128, DC, NMEM], BF16)
            vm = self.sb([128, 2, D], BF16)
            ones1 = self.sb([128, 128], BF16)
            self.sch.add("pool", lambda e: e.memset(ones1[:], 1.0), (), ("ones1",))
            WO = self.W["w_out"].ap()
            for h in range(8):
                self.dma("pool", wof[:, h, :], WO[64 * h:64 * h + 64, :], (), (("wof", h),), "wof")
            kof = [("wof", h) for h in range(8)]
            kog = self.load_weight(wog, self.W["w_out"], 4, D, "wog", row0=512)
            kq = self.load_weight(wq, self.W["mem_w_q"], DC, D, "wq")
            ko = self.load_weight(wo, self.W["mem_w_o"], DC, D, "wo")
            hT = [self.sb([128, DC, T], F32) for _ in range(2)]
            mf = [self.sb([64, 8, T], BF16) for _ in range(2)]
            mg = [self.sb([128, 4, T], BF16) for _ in range(2)]
            f_ = [self.sb([128, DC, T], F32) for _ in range(2)]
            sq_ = [self.sb([128, DC, T], BF16) for _ in range(2)]
            rstd_ = [self.sb([128, T], F32) for _ in range(2)]
            hq_ = [self.sb([128, DC, T], BF16) for _ in range(2)]
            qT_ = [self.sb([128, DC, T], BF16) for _ in range(2)]
            pT = [self.sb([128, T], BF16) for _ in range(4)]
            rec_ = [self.sb([128, T], F32) for _ in range(2)]
            oT_ = [self.sb([128, DC, T], BF16) for _ in range(2)]
            sq, rstd = sq_[0], rstd_[0]
            pa = [self.ps([128, 512], F32) for _ in range(2)]
            stp = self.ps([128, 512], F32)
            scp = [self.ps([128, 512], F32) for _ in range(2)]
            ssp = self.ps([128, 512], F32)
            op_ = [self.ps([128, 512], F32) for _ in range(2)]
            gpo = self.VROW["mix_post_norm"]
            gmq = self.VROW["mem_pre_norm"]
            gkv = self.VROW["mem_kv_norm"]
            gmo = self.VROW["mem_post_norm"]
            with ExitStack() as ph2:
                self.phase = ph2
                wkv = self.sb([128, DC, 2 * D], BF16)
                kkv = self.load_weight(wkv, self.W["mem_w_kv"], DC, 2 * D, "wkv")
                ms = self.sb([128, 2, D], F32)
                mT = self.sb([128, DC, NMEM], F32)
                mn = self.sb([128, DC, NMEM], BF16)
                self.dma("sp", ms[:], self.MEM.ap().rearrange("(s p) d -> p s d", p=128), (), ("ms",), "ms")
                for cp in range(4):
                    for cc in range(2):
                        c = 2 * cp + cc
                        for s_ in range(2):
                            self.tr(pa[0][:, cc * 256 + s_ * 128: cc * 256 + (s_ + 1) * 128],
                                    ms[:, s_, c * 128:(c + 1) * 128], ident[:], ("ms", "ident"), (("pa", 0),))
                    self.copy("act", mT[:, 2 * cp:2 * cp + 2, :], pa[0][:], (("pa", 0),), ("mT",))
                self.rms_stats(mT, sq, stp[:, 0:T], rstd, ("mT",), "sq", "st", "rstd", T)
                for c in range(DC):
                    self.stt("dve", mn[:, c, :], mT[:, c, :], vec[:, gkv + c:gkv + c + 1], rstd[:],
                             ALU.mult, ALU.mult, ("mT", "rstd", "vec"), ("mn",))
                for dt in range(DC):
                    b = dt % 2
                    for k in range(DC):
                        self.mm(pa[b][:, 0:NMEM], wkv[:, k, dt * 128:(dt + 1) * 128], mn[:, k, :], k == 0, k == DC - 1,
                                ("mn", kkv[k]), (("pa", b),))
                    self.copy("act", kmT[:, dt, :], pa[b][:, 0:NMEM], (("pa", b),), ("kmT",))
                for mt in range(2):
                    for half in range(2):
                        b = (2 * mt + half) % 2
                        for k in range(DC):
                            self.mm(pa[b][:, 0:512], mn[:, k, mt * 128:(mt + 1) * 128],
                                    wkv[:, k, D + half * 512:D + half * 512 + 512], k == 0, k == DC - 1,
                                    ("mn", kkv[k]), (("pa", b),))
                        self.copy("act", vm[:, mt, half * 512:half * 512 + 512], pa[b][:, 0:512], (("pa", b),), ("vm",))
                self.sch.barrier()
                self.sch.emit()
            self.phase = ph
            H1T = self.H1T.ap().rearrange("(c p) t -> p c t", p=128)
            MF = self.MF.ap()
            MG = self.MG.ap()
            pcnt_box = [0]

            def half1(i):
                sl = i % 2
                t0 = i * T
                self.dma("sp", hT[sl][:], H1T[:, :, t0:t0 + T], (("H1T", i),), (("hT", sl),), ("hT", sl))
                self.dma("sp", mf[sl][:], MF[:, :, t0:t0 + T].rearrange("h p t -> p h t"), (), (("mf", sl),), ("mf", sl))
                self.dma("sp", mg[sl][:], MG[:, :, t0:t0 + T].rearrange("h p t -> p h t"), (), (("mg", sl),), ("mg", sl))
                hk = ("hT", sl)
                f, sq, rstd, hq, qT, rec, oT = f_[sl], sq_[sl], rstd_[sl], hq_[sl], qT_[sl], rec_[sl], oT_[sl]
                fK, sqK, rsK, hqK, recK = ("f", sl), ("sq", sl), ("rstd", sl), ("hq", sl), ("rec", sl)
                for dt in range(DC):
                    b = 0
                    for h in range(8):
                        self.mm(pa[b][:, 0:T], wof[:, h, dt * 128:(dt + 1) * 128], mf[sl][:, h, :], h == 0, False,
                                (("mf", sl), kof[h]), (("pa", b),))
                    for j in range(4):
                        self.mm(pa[b][:, 0:T], wog[:, j, dt * 128:(dt + 1) * 128], mg[sl][:, j, :], False, j == 3,
                                (("mg", sl), kog[j]), (("pa", b),))
                    self.copy("act", f[:, dt, :], pa[b][:, 0:T], (("pa", b),), (fK,))
                self.rms_stats(f, sq, stp[:, 0:T], rstd, (fK,), sqK, "st", rsK, T)
                for c in range(DC):
                    self.stt("dve", f[:, c, :], f[:, c, :], vec[:, gpo + c:gpo + c + 1], rstd[:],
                             ALU.mult, ALU.mult, (fK, rsK, "vec"), (fK,))
                self.tt("dve", hT[sl][:], hT[sl][:], f[:], ALU.add, (fK, hk), (hk,))
                self.rms_stats(hT[sl], sq, stp[:, 0:T], rstd, (hk,), sqK, "st", rsK, T)
                for c in range(DC):
                    self.stt("dve", hq[:, c, :], hT[sl][:, c, :], vec[:, gmq + c:gmq + c + 1], rstd[:],
                             ALU.mult, ALU.mult, (hk, rsK, "vec"), (hqK,))
                for dt in range(DC):
                    b = 1
                    for k in range(DC):
                        self.mm(pa[b][:, 0:T], wq[:, k, dt * 128:(dt + 1) * 128], hq[:, k, :], k == 0, k == DC - 1,
                                (hqK, kq[k]), (("pa", b),))
                    self.copy("act", qT[:, dt, :], pa[b][:, 0:T], (("pa", b),), (("qT", sl, dt),))

            def half2(i):
                sl = i % 2
                t0 = i * T
                hk = ("hT", sl)
                f, sq, rstd, hq, qT, rec, oT = f_[sl], sq_[sl], rstd_[sl], hq_[sl], qT_[sl], rec_[sl], oT_[sl]
                fK, sqK, rsK, hqK, recK = ("f", sl), ("sq", sl), ("rstd", sl), ("hq", sl), ("rec", sl)
                pcnt = pcnt_box[0]
                for hd in range(MH):
                    pk = []
                    for mt in range(2):
                        b = pcnt % 2
                        pb = pcnt % 4
                        pcnt += 1
                        for dc in range(2):
                            self.mm(scp[b][:, 0:T], kmT[:, 2 * hd + dc, mt * 128:(mt + 1) * 128], qT[:, 2 * hd + dc, :],
                                    dc == 0, dc == 1, (("qT", sl, 2 * hd + dc), "kmT"), (("sc", b),))
                        self.act(pT[pb][:], scp[b][:, 0:T], AF.Exp, (("sc", b),), (("pT", pb),), scale=1.0 / 16.0)
                        pk.append(pb)
                    for mt in range(2):
                        self.mm(ssp[:, 0:T], ones1[:], pT[pk[mt]][:], mt == 0, mt == 1, (("pT", pk[mt]), "ones1"), ("ss",))
                    self.sch.add("dve", lambda e, rec=rec: e.reciprocal(rec[:], ssp[:, 0:T]), ("ss",), (recK,), cost=400.0)
                    for dc in range(2):
                        b = 0
                        for mt in range(2):
                            self.mm(op_[b][:, 0:T], vm[:, mt, (2 * hd + dc) * 128:(2 * hd + dc + 1) * 128], pT[pk[mt]][:],
                                    mt == 0, mt == 1, (("pT", pk[mt]), "vm"), (("op", b),))
                        self.tt("dve", oT[:, 2 * hd + dc, :], op_[b][:, 0:T], rec[:], ALU.mult, (("op", b), recK),
                                (("oT", sl, 2 * hd + dc),))
                for dt in range(DC):
                    for k in range(DC):
                        self.mm(op_[1][:, 0:T], wo[:, k, dt * 128:(dt + 1) * 128], oT[:, k, :], k == 0, k == DC - 1,
                                (("oT", sl, k), ko[k]), (("op", 1),))
                    self.copy("act", f[:, dt, :], op_[1][:, 0:T], (("op", 1),), (fK,))
                self.rms_stats(f, sq, stp[:, 0:T], rstd, (fK,), sqK, "st", rsK, T)
                for c in range(DC):
                    self.stt("dve", f[:, c, :], f[:, c, :], vec[:, gmo + c:gmo + c + 1], rstd[:],
                             ALU.mult, ALU.mult, (fK, rsK, "vec"), (fK,))
                self.tt("dve", hT[sl][:], hT[sl][:], f[:], ALU.add, (fK, hk), (hk,))
                self.dma("pool", H1T[:, :, t0:t0 + T], hT[sl][:], (hk,), (("H1T", i),), ("st_h", sl))
                pcnt_box[0] = pcnt

            half1(0)
            for i in range(NT):
                if i + 1 < NT:
                    half1(i + 1)
                half2(i)
            self.sch.barrier()
            self.sch.emit()
        self.phase = None

    def phase_gdn(self):
        T = 512
        NT = S // T
        C = 64
        NCH = T // C
        with ExitStack() as ph:
            self.phase = ph
            vec = self.vec
            ident = self.ident
            gno = self.VROW["gdn_out_norm"]
            GQT, GKT, GVT, ZT, BGT, MG = (self.GQT.ap(), self.GKT.ap(), self.GVT.ap(), self.ZT.ap(),
                                          self.BGT.ap(), self.MG.ap())
            f32t = lambda shape: self.sb(shape, F32)
            ones_r = f32t([1, 128])
            ones64 = f32t([64, 512])
            mSU, mIU, mSL, Iall = f32t([64, 512]), f32t([64, 512]), f32t([64, 512]), f32t([64, 512])
            self.sch.add("pool", lambda e: e.memset(ones_r[:], 1.0), (), ("ones_r",))
            self.sch.add("pool", lambda e: e.memset(ones64[:], 1.0), (), ("ones64",))
            pat = [[0, NCH], [1, C]]
            for m_, op_, base in ((mSU, ALU.is_gt, 0), (mIU, ALU.is_ge, 0), (Iall, ALU.is_equal, 0)):
                self.sch.add("pool", lambda e, m_=m_, op_=op_, base=base: e.affine_select(
                    m_[:], ones64[:], pat, op_, 0.0, base=base, channel_multiplier=-1), ("ones64",), (("mask", id(m_)),))
            self.sch.add("pool", lambda e: e.affine_select(mSL[:], ones64[:], [[0, NCH], [-1, C]], ALU.is_gt, 0.0,
                                                           base=0, channel_multiplier=1), ("ones64",), (("mask", id(mSL)),))
            mk = tuple(("mask", id(m_)) for m_ in (mSU, mIU, mSL, Iall))
            qT = [f32t([128, T]) for _ in range(2)]
            kT = [f32t([128, T]) for _ in range(2)]
            vT = [f32t([128, T]) for _ in range(2)]
            zT = [self.sb([128, T], BF16) for _ in range(2)]
            brow = [f32t([1, T]) for _ in range(2)]
            grow = [f32t([1, T]) for _ in range(2)]
            gc, ngc, eg, et = f32t([1, T]), f32t([1, T]), f32t([1, T]), f32t([1, T])
            kbT, vbT, kbgT, ktT = f32t([128, T]), f32t([128, T]), f32t([128, T]), f32t([128, T])
            dmin, dmax = f32t([64, T]), f32t([64, T])
            GTs, GTi, Gs = f32t([64, T]), f32t([64, T]), f32t([64, T])
            P = [f32t([64, T]) for _ in range(2)]
            Q = [f32t([64, T]) for _ in range(2)]
            FQ = f32t([64, T])
            Z = [f32t([64, T]) for _ in range(2)]
            vb_tok = f32t([64, NCH, 128])
            kbg_tok = f32t([64, NCH, 128])
            qgT = [f32t([128, T]) for _ in range(2)]
            AT = [f32t([64, T]) for _ in range(2)]
            kt_tok = [f32t([64, NCH, 128]) for _ in range(2)]
            U = [f32t([64, NCH, 128]) for _ in range(2)]
            WT = [f32t([128, T]) for _ in range(2)]
            eglB = [f32t([128, NCH]) for _ in range(2)]
            St = f32t([128, 128])
            vnew = [f32t([64, 128]) for _ in range(2)]
            osb = f32t([128, T])
            sqo = self.sb([128, T], BF16)
            rso = f32t([128, T])
            zsg = f32t([128, T])
            mgst = [self.sb([128, T], BF16) for _ in range(2)]
            pp = [self.ps([128, 512], F32) for _ in range(4)]
            pd = self.ps([128, 512], F32)
            pw = self.ps([128, 512], F32)
            pds = self.ps([128, 512], F32)
            po = self.ps([128, 512], F32)
            ppc = [0]

            def nextpp():
                b = ppc[0] % 4
                ppc[0] += 1
                return pp[b], ("pp", b)

            def cs(n):
                return slice(n * C, (n + 1) * C)

            def load(hh, i, ti):
                sl = ti % 2
                t0 = i * T
                self.dma("sp", qT[sl][:], GQT[hh, :, t0:t0 + T], (), (("qT", sl),), ("gq", sl))
                self.dma("sp", kT[sl][:], GKT[hh, :, t0:t0 + T], (), (("kT", sl),), ("gk", sl))
                self.dma("sp", vT[sl][:], GVT[hh, :, t0:t0 + T], (), (("vT", sl),), ("gv", sl))
                self.dma("sp", zT[sl][:], ZT[hh, :, t0:t0 + T], (), (("zT", sl),), ("gz", sl))
                self.dma("sp", brow[sl][:], BGT[hh:hh + 1, t0:t0 + T], (), (("brow", sl),), ("gb", sl))
                self.dma("sp", grow[sl][:], BGT[4 + hh:5 + hh, t0:t0 + T], (), (("grow", sl),), ("gg", sl))

            def pre(hh, i, ti):
                sl = ti % 2
                th = []
                A = th.append
                qk, kk, vk = ("qT", sl), ("kT", sl), ("vT", sl)

                def rows():
                    for n in range(NCH):
                        self.sch.add("dve", lambda e, n=n: e.tensor_tensor_scan(
                            gc[:, cs(n)], ones_r[:, 0:C], grow[sl][:, cs(n)], 0.0, ALU.mult, ALU.add),
                            (("grow", sl), "ones_r"), ("gc",), cost=200.0)
                    self.ts("dve", ngc[:], gc[:], -1.0, None, ALU.mult, None, ("gc",), ("ngc",))
                    self.act(eg[:], gc[:], AF.Exp, ("gc",), ("eg",))
                    for n in range(NCH):
                        self.ts("dve", et[:, cs(n)], gc[:, cs(n)], gc[:, n * C + C - 1:n * C + C], None, ALU.subtract, None,
                                ("gc",), ("et",))
                    self.act(et[:], et[:], AF.Exp, ("et",), ("et",), scale=-1.0)
                A(rows)

                def bcast():
                    bb, bbk = nextpp()
                    self.mm(bb[:, 0:T], ones_r[:, :], brow[sl][:], True, True, (("brow", sl), "ones_r"), (bbk,))
                    self.tt("dve", kbT[:], kT[sl][:], bb[:, 0:T], ALU.mult, (kk, bbk), ("kbT",))
                    self.tt("dve", vbT[:], vT[sl][:], bb[:, 0:T], ALU.mult, (vk, bbk), ("vbT",))
                    eb, ebk = nextpp()
                    self.mm(eb[:, 0:T], ones_r[:, :], eg[:], True, True, ("eg", "ones_r"), (ebk,))
                    self.tt("dve", kbgT[:], kbT[:], eb[:, 0:T], ALU.mult, ("kbT", ebk), ("kbgT",))
                    self.tt("dve", qgT[sl][:], qT[sl][:], eb[:, 0:T], ALU.mult, (qk, ebk), (("qgT", sl),))
                    tb, tbk = nextpp()
                    self.mm(tb[:, 0:T], ones_r[:, :], et[:], True, True, ("et", "ones_r"), (tbk,))
                    self.tt("dve", ktT[:], kT[sl][:], tb[:, 0:T], ALU.mult, (kk, tbk), ("ktT",))
                    lb, lbk = nextpp()
                    self.mm(lb[:, 0:NCH], ones_r[:, :], eg[:].rearrange("p (n c) -> p n c", c=C)[:, :, C - 1], True, True,
                            ("eg", "ones_r"), (lbk,))
                    self.copy("act", eglB[sl][:], lb[:, 0:NCH], (lbk,), (("eglB", sl),))
                A(bcast)

                def decay():
                    for n in range(NCH):
                        self.mm(pd[0:C, cs(n)], ones_r[:, 0:C], gc[:, cs(n)], True, False, ("gc", "ones_r"), ("pd",))
                        self.mm(pd[0:C, cs(n)], ngc[:, cs(n)], ones_r[:, 0:C], False, True, ("ngc", "ones_r"), ("pd",))
                    self.ts("dve", dmin[:], pd[0:C, 0:T], 0.0, None, ALU.min, None, ("pd",), ("dmin",))
                    self.ts("dve", dmax[:], pd[0:C, 0:T], 0.0, None, ALU.max, None, ("pd",), ("dmax",))
                    self.act(dmin[:], dmin[:], AF.Exp, ("dmin",), ("dmin",))
                    self.act(dmax[:], dmax[:], AF.Exp, ("dmax",), ("dmax",), scale=-1.0)
                    self.tt("pool", GTs[:], dmin[:], mSU[:], ALU.mult, ("dmin",) + mk, ("GTs",))
                    self.tt("pool", GTi[:], dmin[:], mIU[:], ALU.mult, ("dmin",) + mk, ("GTi",))
                    self.tt("pool", Gs[:], dmax[:], mSL[:], ALU.mult, ("dmax",) + mk, ("Gs",))
                A(decay)

                def gram():
                    b1, k1 = nextpp()
                    for n in range(NCH):
                        self.mm(b1[0:C, cs(n)], kT[sl][:, cs(n)], kbT[:, cs(n)], True, True, (kk, "kbT"), (k1,))
                    self.stt("dve", P[0][:], b1[0:C, 0:T], -1.0, GTs[:], ALU.mult, ALU.mult, (k1, "GTs"), (("P", 0),))
                    b2, k2 = nextpp()
                    for n in range(NCH):
                        self.mm(b2[0:C, cs(n)], kbT[:, cs(n)], kT[sl][:, cs(n)], True, True, (kk, "kbT"), (k2,))
                    self.stt("dve", Q[0][:], b2[0:C, 0:T], -1.0, Gs[:], ALU.mult, ALU.mult, (k2, "Gs"), (("Q", 0),))
                    b3, k3 = nextpp()
                    for n in range(NCH):
                        self.mm(b3[0:C, cs(n)], kT[sl][:, cs(n)], qT[sl][:, cs(n)], True, True, (kk, qk), (k3,))
                    self.tt("dve", AT[sl][:], b3[0:C, 0:T], GTi[:], ALU.mult, (k3, "GTi"), (("AT", sl),))
                    self.tt("pool", Z[0][:], P[0][:], Iall[:], ALU.add, (("P", 0),) + mk, (("Z", 0),))
                A(gram)

                def level(lv):
                    def f():
                        a, b = lv % 2, (lv + 1) % 2
                        bq, kq_ = nextpp()
                        for n in range(NCH):
                            self.mm(bq[0:C, cs(n)], P[a][:, cs(n)], Q[a][:, cs(n)], True, True, (("P", a), ("Q", a)), (kq_,))
                        if lv < 4:
                            bp, kp_ = nextpp()
                            for n in range(NCH):
                                self.mm(bp[0:C, cs(n)], Q[a][:, cs(n)], P[a][:, cs(n)], True, True, (("P", a), ("Q", a)), (kp_,))
                            self.copy("act", P[b][:], bp[0:C, 0:T], (kp_,), (("P", b),))
                            self.copy("act", Q[b][:], bq[0:C, 0:T], (kq_,), (("Q", b),))
                        self.tt("dve", FQ[:], bq[0:C, 0:T], Iall[:], ALU.add, (kq_,) + mk, ("FQ",))
                        bz, kz_ = nextpp()
                        for n in range(NCH):
                            self.mm(bz[0:C, cs(n)], FQ[:, cs(n)], Z[a][:, cs(n)], True, True, ("FQ", ("Z", a)), (kz_,))
                        self.copy("act", Z[b][:], bz[0:C, 0:T], (kz_,), (("Z", b),))
                    return f
                for lv in range(5):
                    A(level(lv))

                def toks():
                    for src, skey, dst, dkey in ((vbT, "vbT", vb_tok, "vb_tok"), (kbgT, "kbgT", kbg_tok, "kbg_tok"),
                                                 (ktT, "ktT", kt_tok[sl], ("kt_tok", sl))):
                        for half in range(2):
                            bt, kt_ = nextpp()
                            for n4 in range(4):
                                n = 4 * half + n4
                                self.tr(bt[0:C, n4 * 128:(n4 + 1) * 128], src[:, cs(n)], ident[:], (skey, "ident"), (kt_,))
                            self.copy("act", dst[:, 4 * half:4 * half + 4, :],
                                      bt[0:C, 0:512].rearrange("p (n e) -> p n e", e=128), (kt_,), (dkey,))
                A(toks)

                def uw():
                    TT = Z[1]
                    tk = ("Z", 1)
                    for half in range(2):
                        bu, ku_ = nextpp()
                        for n4 in range(4):
                            n = 4 * half + n4
                            self.mm(bu[0:C, n4 * 128:(n4 + 1) * 128], TT[:, cs(n)], vb_tok[:, n, :], True, True,
                                    (tk, "vb_tok"), (ku_,))
                        self.copy("act", U[sl][:, 4 * half:4 * half + 4, :],
                                  bu[0:C, 0:512].rearrange("p (n e) -> p n e", e=128), (ku_,), (("U", sl),))
                    bw, kw_ = nextpp()
                    for n in range(NCH):
                        self.mm(bw[:, cs(n)], kbg_tok[:, n, :], TT[:, cs(n)], True, True, (tk, "kbg_tok"), (kw_,))
                    self.copy("act", WT[sl][:], bw[:, 0:T], (kw_,), (("WT", sl),))
                A(uw)
                return th

            def seq_chunk(hh, i, n, ti, fill):
                sl = ti % 2
                v = vnew[n % 2]
                vk = ("vnew", n % 2)
                self.mm(pw[0:C, 0:128], WT[sl][:, cs(n)], St[:], True, True, (("WT", sl), "St"), ("pw",))
                self.tt("dve", v[:], U[sl][:, n, :], pw[0:C, 0:128], ALU.subtract, (("U", sl), "pw"), (vk,))
                fill()
                self.mm(po[:, cs(n)], St[:], qgT[sl][:, cs(n)], True, False, ("St", ("qgT", sl)), ("po",))
                self.mm(po[:, cs(n)], v[:], AT[sl][:, cs(n)], False, True, (vk, ("AT", sl)), ("po",))
                self.mm(pds[:, 0:128], kt_tok[sl][:, n, :], v[:], True, True, (("kt_tok", sl), vk), ("pds",))
                self.stt("dve", St[:], St[:], eglB[sl][:, n:n + 1], pds[:, 0:128], ALU.mult, ALU.add,
                         ("St", ("eglB", sl), "pds"), ("St",))
                fill()

            def finish(hh, i, ti):
                sl = ti % 2
                t0 = i * T
                ms = ti % 2
                self.copy("act", osb[:], po[:, 0:T], ("po",), ("osb",))
                self.tt("dve", sqo[:], osb[:], osb[:], ALU.mult, ("osb",), ("sqo",))
                bs, ks_ = nextpp()
                self.mm(bs[:, 0:T], self.ones_bf[:], sqo[:], True, True, ("sqo", "ones_bf"), (ks_,))
                self.act(rso[:], bs[:, 0:T], AF.Ln, (ks_,), ("rso",), bias=EPS, scale=8.0)
                self.act(rso[:], rso[:], AF.Exp, ("rso",), ("rso",), scale=-0.5)
                self.stt("dve", osb[:], osb[:], vec[:, gno:gno + 1], rso[:], ALU.mult, ALU.mult, ("osb", "rso", "vec"), ("osb",))
                self.act(zsg[:], zT[sl][:], AF.Exp, (("zT", sl),), ("zsg",), scale=-1.0)
                self.act(zsg[:], zsg[:], AF.Ln, ("zsg",), ("zsg",), bias=1.0)
                self.act(zsg[:], zsg[:], AF.Exp, ("zsg",), ("zsg",), scale=-1.0)
                self.tt("dve", zsg[:], zsg[:], zT[sl][:], ALU.mult, ("zsg", ("zT", sl)), ("zsg",))
                self.tt("dve", mgst[ms][:], osb[:], zsg[:], ALU.mult, ("osb", "zsg"), (("mgst", ms),))
                self.dma("pool", MG[hh, :, t0:t0 + T], mgst[ms][:], (("mgst", ms),), (("MG", hh, i),), ("mgst", ms))

            tiles = [(hh, i) for hh in range(4) for i in range(NT)]
            load(*tiles[0], 0)
            for th in pre(*tiles[0], 0):
                th()
            NSLOT = 2 * NCH
            for ti, (hh, i) in enumerate(tiles):
                rec = []
                if ti + 1 < len(tiles):
                    load(*tiles[ti + 1], ti + 1)
                    rec = self.sch.record(pre(*tiles[ti + 1], ti + 1))
                if i == 0:
                    self.sch.add("pool", lambda e: e.memset(St[:], 0.0), ("St",), ("St",))
                per = (len(rec) + NSLOT - 1) // NSLOT
                slot = [0]

                def fill():
                    k = slot[0]
                    self.sch.replay(rec, k * per, (k + 1) * per)
                    slot[0] += 1
                for n in range(NCH):
                    seq_chunk(hh, i, n, ti, fill)
                finish(hh, i, ti)
                self.sch.replay(rec, NSLOT * per, len(rec))
            self.sch.barrier()
            self.sch.emit(reorder="gdn" if os.environ.get("KGDN_RE", "1") == "1" else False)
        self.phase = None

    def phase_gdn_stub(self):
        with ExitStack() as ph:
            self.phase = ph
            z = self.sb([128, 4, 2048], BF16)
            self.sch.add("pool", lambda e: e.memset(z[:], 0.0), (), ("z",))
            MG = self.MG.ap()
            for t0 in range(0, S, 2048):
                n = min(2048, S - t0)
                self.dma("pool", MG[:, :, t0:t0 + n].rearrange("h p t -> p h t"), z[:, :, 0:n], ("z",), (("MGz", t0),), "mgz")
            self.sch.barrier()
            self.sch.emit()
        self.phase = None


def make_vecs(inp):
    v = np.zeros((128, 128), np.float32)
    for nm, r0 in KB.VROW.items():
        arr = np.asarray(inp[nm], np.float32).reshape(-1)
        n = arr.size // 128
        v[r0:r0 + n, :] = arr.reshape(n, 128)
    return v


WNAMES = ("ffn1_w_gate", "ffn1_w_up", "ffn1_w_down", "w_in", "w_out", "mem_w_q", "mem_w_kv", "mem_w_o",
          "ffn2_w_gate", "ffn2_w_up", "ffn2_w_down")


def make_in_maps(inp):
    vecs = make_vecs(inp)
    small = np.zeros((8, 4), np.float32)
    small[:, 0] = np.asarray(inp["fox_f_bias"], np.float32).reshape(-1)
    small[4:8, 1] = np.asarray(inp["gdn_a_log"], np.float32).reshape(-1)
    small[4:8, 2] = np.asarray(inp["gdn_dt_bias"], np.float32).reshape(-1)
    shared = {nm: np.ascontiguousarray(np.asarray(inp[nm], np.float32)[0]) for nm in WNAMES}
    maps = []
    for b in range(NCORES):
        m = dict(shared)
        m["x"] = np.ascontiguousarray(np.asarray(inp["x"], np.float32)[b][:S])
        m["mem"] = np.ascontiguousarray(np.asarray(inp["mem"], np.float32)[b])
        m["vecs"] = vecs
        m["small"] = small
        maps.append(m)
    return maps


def kernel(**inputs):
    kb = KB()
    nc = kb.build()
    res = run_bass_kernel_spmd(nc, make_in_maps(inputs), core_ids=list(range(NCORES)))
    return np.stack([np.asarray(r["out"], np.float32) for r in res.results], axis=0)
```

```python
import os
from contextlib import ExitStack

import numpy as np
import concourse.bass as bass
import concourse.mybir as mybir
from concourse.bass_utils import run_bass_kernel_spmd

F32 = mybir.dt.float32
BF16 = mybir.dt.bfloat16
ALU = mybir.AluOpType
AF = mybir.ActivationFunctionType

NCORES = 8
S = int(os.environ.get('KSEQ', '8192'))
D = 1024
DC = D // 128
DFF = 2816
FC = DFF // 128
NMEM = 256
EPS = 1e-6
WIN = 3600

ENGS = ("pe", "act", "dve", "pool", "sp")


class Op:
    __slots__ = ("eng", "fn", "reads", "writes", "idx", "deps", "marked", "dma", "semkey", "cost", "lat",
                 "alldeps", "pos", "rank")

    def __init__(self, eng, fn, reads, writes, dma=False, semkey=None, cost=100.0, lat=0.0):
        self.eng = eng
        self.fn = fn
        self.reads = reads
        self.writes = writes
        self.deps = []
        self.alldeps = []
        self.marked = False
        self.dma = dma
        self.semkey = semkey
        self.cost = cost
        self.lat = lat
        self.pos = -1
        self.rank = 0


class Sched:
    SEM_LAT = 250.0
    WINDOW = int(os.environ.get("KWIN", "256"))

    def __init__(self, nc, stack):
        self.nc = nc
        self.stack = stack
        self.ops = []
        self.last_writer = {}
        self.readers = {}
        self.dma_count = {}
        self.dma_sems = {}
        self.seg_start = 0
        self.barrier_idx = -1
        self.eng_sem = {e: stack.enter_context(nc.semaphore("s_" + e)) for e in ENGS}
        self.eng_count = {e: 0 for e in ENGS}
        self.waited = {e: {} for e in ENGS}
        self.opval = {}
        self.reorder = os.environ.get("KREORDER", "1") == "1"

    def add(self, eng, fn, reads=(), writes=(), dma=False, semkey=None, cost=100.0, lat=0.0):
        if dma:
            writes = tuple(writes) + (("semchain", semkey),)
        op = Op(eng, fn, tuple(reads), tuple(writes), dma, semkey, cost, lat)
        op.idx = len(self.ops)
        deps = {}
        for r in op.reads:
            w = self.last_writer.get(r)
            if w is not None:
                deps[w.idx] = w
        for k in op.writes:
            w = self.last_writer.get(k)
            if w is not None:
                deps[w.idx] = w
            for rd in self.readers.get(k, ()):
                deps[rd.idx] = rd
        for r in op.reads:
            self.readers.setdefault(r, []).append(op)
        for k in op.writes:
            self.last_writer[k] = op
            self.readers[k] = []
        op.alldeps = [d for d in deps.values() if d.idx > self.barrier_idx and d is not op]
        self.ops.append(op)
        if dma:
            if semkey not in self.dma_sems:
                self.dma_sems[semkey] = self.stack.enter_context(
                    self.nc.semaphore("d_%d" % len(self.dma_sems)))
            self.dma_count[semkey] = self.dma_count.get(semkey, 0) + 1
            op.rank = self.dma_count[semkey]
        return op

    def barrier(self):
        self.barrier_idx = len(self.ops) - 1

    def _schedule(self, seg, reorder=True):
        pend = {e: [o for o in seg if o.eng == e] for e in ENGS}
        if not (self.reorder and reorder):
            return pend
        order = {e: [] for e in ENGS}
        head = {e: 0 for e in ENGS}
        free = {e: 0.0 for e in ENGS}
        finish = {}
        done = set()
        nleft = len(seg)
        W = self.WINDOW if reorder == "gdn" else int(os.environ.get("KWIN2", "512"))
        LAT = self.SEM_LAT if reorder == "gdn" else float(os.environ.get("KLAT", "1000"))
        fixed = set(os.environ.get("KFIXED", "act,pe").split(",")) if reorder == "gdn" else set()
        while nleft:
            best = None
            bkey = None
            for e in ENGS:
                lst = pend[e]
                h = head[e]
                n = len(lst)
                while h < n and lst[h].idx in done:
                    h += 1
                head[e] = h
                fe = free[e]
                cnt = 0
                i = h
                We = 1 if e in fixed else W
                while i < n and cnt < We:
                    op = lst[i]
                    i += 1
                    if op.idx in done:
                        continue
                    cnt += 1
                    ready = fe
                    ok = True
                    for d in op.alldeps:
                        f = finish.get(d.idx)
                        if f is None:
                            ok = False
                            break
                        if d.dma or d.eng != e:
                            f += LAT
                        if f > ready:
                            ready = f
                    if not ok:
                        continue
                    key = (ready, op.idx)
                    if bkey is None or key < bkey:
                        bkey = key
                        best = op
                    if ready <= fe:
                        break
            op = best
            start = bkey[0]
            e = op.eng
            free[e] = start + op.cost
            finish[op.idx] = start + op.cost + op.lat
            done.add(op.idx)
            order[e].append(op)
            nleft -= 1
        self.sim_time = max(finish.values()) if finish else 0.0
        if os.environ.get("KVERB"):
            busy = {e: round(sum(o.cost for o in order[e]) / 1e3) for e in ENGS}
            print("segment ops", len(seg), "sim_us", round(self.sim_time / 1e3), "busy_us", busy, flush=True)
        return order

    def emit(self, reorder=True):
        nc = self.nc
        seg = self.ops[self.seg_start:]
        self.seg_start = len(self.ops)
        order = self._schedule(seg, reorder)
        for e in ENGS:
            for p, op in enumerate(order[e]):
                op.pos = p
        for op in seg:
            best = {}
            for d in op.alldeps:
                if d.dma:
                    key = ("d", d.semkey)
                    if key not in best or best[key].rank < d.rank:
                        best[key] = d
                else:
                    if d.eng == "pe" and op.eng == "pe" and not op.dma:
                        continue
                    key = ("e", d.eng)
                    if key not in best or best[key].pos < d.pos:
                        best[key] = d
            op.deps = list(best.values())
            for d in op.deps:
                if not d.dma:
                    d.marked = True
        for e in ENGS:
            for op in order[e]:
                if not op.dma and op.marked:
                    self.eng_count[e] += 1
                    self.opval[op.idx] = self.eng_count[e]
        seg_dma = {}
        for op in seg:
            if op.dma:
                seg_dma[op.semkey] = max(seg_dma.get(op.semkey, 0), op.rank)

        def run(engname, eng):
            waited = self.waited[engname]
            for op in order[engname]:
                for d in op.deps:
                    if d.dma:
                        sem = self.dma_sems[d.semkey]
                        val = 16 * d.rank
                        key = ("d", d.semkey)
                    else:
                        sem = self.eng_sem[d.eng]
                        val = self.opval[d.idx]
                        key = ("e", d.eng)
                    if waited.get(key, 0) < val:
                        eng.wait_ge(sem, val)
                        waited[key] = val
                ins = op.fn(eng)
                if op.dma:
                    ins.then_inc(self.dma_sems[op.semkey], 16)
                elif op.marked:
                    ins.then_inc(self.eng_sem[engname], 1)
            if engname == "sp":
                for semkey, rank in seg_dma.items():
                    key = ("d", semkey)
                    if waited.get(key, 0) < 16 * rank:
                        eng.wait_ge(self.dma_sems[semkey], 16 * rank)
                        waited[key] = 16 * rank

        with nc.Block() as block:
            @block.tensor
            def _(e):
                run("pe", e)

            @block.scalar
            def _(e):
                run("act", e)

            @block.vector
            def _(e):
                run("dve", e)

            @block.gpsimd
            def _(e):
                run("pool", e)

            @block.sync
            def _(e):
                run("sp", e)


class KB:
    def __init__(self, debug_stage=None):
        self.debug_stage = debug_stage
        self.nc = bass.Bass("TRN2", target_bir_lowering=False)
        self.top = ExitStack()
        self.sch = None
        self.phase = None
        self.nm = 0

    def sb(self, shape, dtype, stack=None, name=None):
        self.nm += 1
        st = stack if stack is not None else self.phase
        return st.enter_context(self.nc.sbuf_tensor(name or ("t%d" % self.nm), list(shape), dtype))

    def ps(self, shape, dtype, stack=None, name=None):
        self.nm += 1
        st = stack if stack is not None else self.phase
        return st.enter_context(self.nc.psum_tensor(name or ("p%d" % self.nm), list(shape), dtype))

    def dram(self, name, shape, dtype, kind="Internal"):
        return self.nc.dram_tensor(name, list(shape), dtype, kind=kind)

    @staticmethod
    def _fsz(ap):
        n = 1
        for d in ap.shape[1:]:
            n *= d
        return n

    @staticmethod
    def _psum(ap):
        return type(ap.tensor).__name__.startswith("PSum")

    def mm(self, out, lhsT, rhs, start, stop, r, w):
        n = max(64, self._fsz(rhs))
        c = n / 2.2 * (4.0 if rhs.dtype == F32 else 1.0) + 8
        self.sch.add("pe", lambda e: e.matmul(out, lhsT, rhs, start=start, stop=stop), r, w, cost=c, lat=60.0)

    def tr(self, out, in_, ident, r, w):
        self.sch.add("pe", lambda e: e.transpose(out, in_, ident), r, w, cost=70.0, lat=60.0)

    def _ecost(self, eng, out, ins, kind):
        n = self._fsz(out)
        ps = any(self._psum(a) for a in ins) or self._psum(out)
        if eng == "act":
            return (n + (180 if ps else 230)) / 1.2
        if eng == "pool":
            return (n + 160) / 1.0
        acc = 1.0
        if kind == "single" and not ps:
            acc = 4.0 if all(a.dtype == BF16 for a in ins) and out.dtype == BF16 else 2.0
        elif kind == "tt" and not ps and all(a.dtype == BF16 for a in ins):
            acc = 2.0
        return (n / acc + (120 if ps else 60)) / 0.96

    def act(self, out, in_, func, r, w, bias=None, scale=None, eng="act"):
        kw = {}
        if bias is not None:
            kw["bias"] = bias
        if scale is not None:
            kw["scale"] = scale
        self.sch.add(eng, lambda e: e.activation(out, in_, func, **kw), r, w, cost=self._ecost(eng, out, (in_,), "act"))

    def copy(self, eng, out, in_, r, w):
        c = self._ecost(eng, out, (in_,), "single")
        if eng == "act":
            self.sch.add("act", lambda e: e.copy(out, in_), r, w, cost=c)
        else:
            self.sch.add(eng, lambda e: e.tensor_copy(out, in_), r, w, cost=c)

    def tt(self, eng, out, a, b, op, r, w):
        self.sch.add(eng, lambda e: e.tensor_tensor(out, a, b, op), r, w, cost=self._ecost(eng, out, (a, b), "tt"))

    def ts(self, eng, out, a, s1, s2, op0, op1, r, w):
        c = self._ecost(eng, out, (a,), "single")
        if op1 is None:
            self.sch.add(eng, lambda e: e.tensor_scalar(out, a, s1, None, op0), r, w, cost=c)
        else:
            self.sch.add(eng, lambda e: e.tensor_scalar(out, a, s1, s2, op0, op1), r, w, cost=c)

    def stt(self, eng, out, a, s, b, op0, op1, r, w):
        self.sch.add(eng, lambda e: e.scalar_tensor_tensor(out, a, s, b, op0, op1), r, w,
                     cost=self._ecost(eng, out, (a, b), "tt"))

    def dma(self, eng, out, in_, r, w, semkey):
        nbytes = 1
        for d in out.shape:
            nbytes *= d
        nbytes *= 2 if out.dtype == BF16 else 4
        issue = 1000.0 if eng == "pool" else 100.0
        self.sch.add(eng, lambda e: e.dma_start(out, in_), r, w, dma=True, semkey=semkey,
                     cost=issue, lat=2000.0 + nbytes / 150.0)

    def build(self):
        nc = self.nc
        with self.top:
            with nc.allow_low_precision("bf16 matmul operands, fp32 accumulation"):
                self._build()
        return nc

    def _build(self):
        nc = self.nc
        top = self.top
        self.sch = Sched(nc, top)
        dbg = self.debug_stage
        X = self.dram("x", [S, D], F32, "ExternalInput")
        MEM = self.dram("mem", [NMEM, D], F32, "ExternalInput")
        VECS = self.dram("vecs", [128, 128], F32, "ExternalInput")
        self.W = {}
        for nm, shp in (("ffn1_w_gate", [D, DFF]), ("ffn1_w_up", [D, DFF]), ("ffn1_w_down", [DFF, D]),
                        ("w_in", [D, WIN]), ("w_out", [D, D]), ("mem_w_q", [D, D]), ("mem_w_kv", [D, 2 * D]),
                        ("mem_w_o", [D, D]),
                        ("ffn2_w_gate", [D, DFF]), ("ffn2_w_up", [D, DFF]), ("ffn2_w_down", [DFF, D])):
            self.W[nm] = self.dram(nm, shp, F32, "ExternalInput")
        kind_dbg = "ExternalOutput" if dbg else "Internal"
        self.H1T = self.dram("h1t", [D, S], F32, "ExternalOutput" if dbg in (1, 5) else "Internal")
        self.UNT = self.dram("unt", [D, S], BF16, kind_dbg if dbg == 1 else "Internal")
        OUT = self.dram("out", [S, D], F32, "ExternalOutput")
        SMALL = self.dram("small", [8, 4], F32, "ExternalInput")
        self.SMALL = SMALL
        dk = lambda st: "ExternalOutput" if dbg == st else "Internal"
        self.QA = self.dram("qa", [8, 70, S], BF16, dk(2))
        self.KA = self.dram("ka", [8, 70, S], BF16, dk(2))
        self.VA = self.dram("va", [S, 8, 128], BF16, dk(2))
        self.MF = self.dram("mf", [8, 64, S], BF16, dk(3))
        self.MG = self.dram("mg", [4, 128, S], BF16, dk(4))
        self.GQT = self.dram("gqt", [4, 128, S], F32, dk(2))
        self.GKT = self.dram("gkt", [4, 128, S], F32, dk(2))
        self.GVT = self.dram("gvt", [4, 128, S], F32, dk(2))
        self.ZT = self.dram("zt", [4, 128, S], BF16, dk(2))
        self.BGT = self.dram("bgt", [8, S], F32, dk(2))
        self.X, self.MEM, self.VECS, self.OUT = X, MEM, VECS, OUT

        self.ident = self.sb([128, 128], F32, stack=top, name="ident")
        self.ones_bf = self.sb([128, 128], BF16, stack=top, name="ones_bf")
        self.vec = self.sb([128, 128], F32, stack=top, name="vecT")
        self.setup_consts()
        self.phase_ffn(1)
        if dbg == 1:
            return
        self.phase_proj()
        if dbg == 2:
            return
        self.phase_fox()
        if dbg == 3:
            return
        self.phase_gdn()
        if dbg == 4:
            return
        self.phase_mix_mem()
        if dbg == 5:
            return
        self.phase_ffn(2)

    VROW = {"ffn1_pre_norm": 0, "ffn1_post_norm": 8, "mix_pre_norm": 16, "mix_post_norm": 24,
            "mem_pre_norm": 32, "mem_kv_norm": 40, "mem_post_norm": 48, "ffn2_pre_norm": 56,
            "ffn2_post_norm": 64, "gdn_conv_w": 72, "gdn_out_norm": 120}

    def setup_consts(self):
        nc = self.nc
        with ExitStack() as ph:
            self.phase = ph
            ones = self.sb([128, 128], F32)
            vs = self.sb([128, 128], F32)
            pt = self.ps([128, 128], F32)
            ident, ones_bf, vec = self.ident, self.ones_bf, self.vec
            self.sch.add("pool", lambda e: e.memset(ones[:], 1.0), (), ("ones",))
            self.sch.add("pool", lambda e: e.affine_select(ident[:], ones[:], [[-1, 128]], ALU.is_equal, 0.0,
                                                           base=0, channel_multiplier=1),
                         ("ones",), ("ident",))
            self.sch.add("pool", lambda e: e.memset(ones_bf[:], 1.0 / 1024.0), (), ("ones_bf",))
            self.dma("sp", vs[:], self.VECS.ap(), (), ("vs",), "vs")
            self.tr(pt[:], vs[:], ident[:], ("vs", "ident"), ("pt",))
            self.copy("dve", vec[:], pt[:], ("pt",), ("vec",))
            for nm in ("ffn1_post_norm", "ffn2_post_norm"):
                r0 = self.VROW[nm]
                self.sch.add("dve", lambda e, r0=r0: e.tensor_scalar(vec[:, r0:r0 + 8], vec[:, r0:r0 + 8], 0.5, None,
                                                                     ALU.mult), ("vec",), ("vec",))
            self.sch.barrier()
            self.sch.emit()
        self.phase = None

    def load_weight(self, wt, dram, kc, ncols, key, col0=0, row0=0, nsplit=None):
        src = dram.ap()[row0:row0 + kc * 128, col0:col0 + ncols].rearrange("(k p) n -> p k n", p=128)
        keys = []
        for k in range(kc):
            c0 = 0
            while c0 < ncols:
                c1 = min(ncols, c0 + 2816)
                self.dma("pool", wt[:, k, c0:c1], src[:, k, c0:c1], (), ((key, k),), key)
                c0 = c1
            keys.append((key, k))
        return keys

    def rms_stats(self, src, sq, st_ps, rstd, keys_src, key_sq, key_st, key_rstd, T, sq_eng="dve"):
        self.tt(sq_eng, sq[:], src[:], src[:], ALU.mult, keys_src, (key_sq,))
        for c in range(DC):
            self.mm(st_ps, self.ones_bf[:], sq[:, c, :], c == 0, c == DC - 1, (key_sq, "ones_bf"), (key_st,))
        self.act(rstd[:], st_ps, AF.Ln, (key_st,), (key_rstd,), bias=EPS)
        self.act(rstd[:], rstd[:], AF.Exp, (key_rstd,), (key_rstd,), scale=-0.5)

    def phase_ffn(self, which):
        nc = self.nc
        T = 256
        NT = S // T
        pre = "ffn%d_" % which
        with ExitStack() as ph:
            self.phase = ph
            wg = self.sb([128, DC, DFF], BF16)
            wu = self.sb([128, DC, DFF], BF16)
            wd = self.sb([128, FC, D], BF16)
            kg = self.load_weight(wg, self.W[pre + "w_gate"], DC, DFF, "wg")
            ku = self.load_weight(wu, self.W[pre + "w_up"], DC, DFF, "wu")
            kd = self.load_weight(wd, self.W[pre + "w_down"], FC, D, "wd")
            xs = self.sb([128, 2, D], F32)
            hT = [self.sb([128, DC, T], F32) for _ in range(2)]
            xn = [self.sb([128, DC, T], BF16) for _ in range(2)]
            sq = self.sb([128, DC, T], BF16)
            rstd = [self.sb([128, T], F32) for _ in range(2)]
            sg = [self.sb([128, T], F32) for _ in range(2)]
            a = self.sb([128, FC, T], BF16)
            f = self.sb([128, DC, T], F32)
            un = self.sb([128, DC, T], BF16)
            tp = [self.ps([128, 512], F32) for _ in range(1)]
            stp = self.ps([128, 512], F32)
            gp = [self.ps([128, 512], F32) for _ in range(2)]
            up = [self.ps([128, 512], F32) for _ in range(2)]
            dn = [self.ps([128, 512], F32) for _ in range(2)]
            vec = self.vec
            g1 = self.VROW[pre + "pre_norm"]
            g2 = self.VROW[pre + "post_norm"]
            g3 = self.VROW["mix_pre_norm"]
            ident = self.ident
            X = self.X.ap()
            H1T = self.H1T.ap().rearrange("(c p) t -> p c t", p=128)
            UNT = self.UNT.ap().rearrange("(c p) t -> p c t", p=128)

            def stage_a(i):
                sl = i % 2
                t0 = i * T
                if which == 2:
                    self.dma("sp", hT[sl][:], H1T[:, :, t0:t0 + T], (("H1T", i),),
                             tuple(("hT", sl, cp) for cp in range(4)), ("hTl", sl))
                else:
                    self.dma("sp", xs[:], X[t0:t0 + T, :].rearrange("(s p) d -> p s d", p=128), (), ("xs",), "xs")
                    for cp in range(4):
                        bank = tp[0]
                        bk = ("tp", 0)
                        for cc in range(2):
                            c = 2 * cp + cc
                            for s in range(2):
                                self.tr(bank[:, cc * 256 + s * 128: cc * 256 + (s + 1) * 128],
                                        xs[:, s, c * 128:(c + 1) * 128], ident[:], ("xs", "ident"), (bk,))
                        self.copy("act", hT[sl][:, 2 * cp:2 * cp + 2, :], bank[:], (bk,), (("hT", sl, cp),))
                hk = tuple(("hT", sl, cp) for cp in range(4))
                self.rms_stats(hT[sl], sq, stp[:, 0:T], rstd[0], hk, "sq", "st", ("rstd", 0), T)
                for c in range(DC):
                    self.stt("dve", xn[sl][:, c, :], hT[sl][:, c, :], vec[:, g1 + c:g1 + c + 1], rstd[0][:],
                             ALU.mult, ALU.mult, (("hT", sl, c // 2), ("rstd", 0), "vec"), (("xn", sl, c),))

            def stage_gu(i):
                sl = i % 2
                xk = tuple(("xn", sl, c) for c in range(DC))
                ksub = int(os.environ.get("KSUB", "9"))
                for ft in range(int(os.environ.get("KFT", FC))):
                    b = ft % 2
                    gbank = gp[b]
                    ubank = up[b]
                    for k in range(DC):
                        self.mm(gbank[:, 0:T], wg[:, k, ft * 128:(ft + 1) * 128], xn[sl][:, k, :],
                                k == 0, k == DC - 1, (("xn", sl, k), kg[k]), (("gp", b),))
                    for k in range(DC):
                        self.mm(ubank[:, 0:T], wu[:, k, ft * 128:(ft + 1) * 128], xn[sl][:, k, :],
                                k == 0, k == DC - 1, (("xn", sl, k), ku[k]), (("up", b),))
                    s2 = ft % int(os.environ.get('KSG', '2'))
                    if ksub < 1:
                        continue
                    self.act(sg[s2][:], gbank[:, 0:T], AF.Exp, (("gp", b),), (("sg", s2),), scale=-1.0)
                    self.act(sg[s2][:], sg[s2][:], AF.Ln, (("sg", s2),), (("sg", s2),), bias=1.0)
                    self.act(sg[s2][:], sg[s2][:], AF.Exp, (("sg", s2),), (("sg", s2),), scale=-1.0)
                    if ksub < 2:
                        continue
                    self.tt("dve", sg[s2][:], sg[s2][:], gbank[:, 0:T], ALU.mult,
                            (("sg", s2), ("gp", b)), (("sg", s2),))
                    self.tt("dve", a[:, ft, :], sg[s2][:], ubank[:, 0:T], ALU.mult,
                            (("sg", s2), ("up", b)), (("a", ft),))

            def stage_d(i):
                ak = tuple(("a", ft) for ft in range(FC))
                for dp in range(4):
                    bank = dn[dp % 2]
                    bk = ("dn", dp % 2)
                    for half in range(2):
                        dt = 2 * dp + half
                        for k in range(FC):
                            self.mm(bank[:, half * T:(half + 1) * T], wd[:, k, dt * 128:(dt + 1) * 128], a[:, k, :],
                                    k == 0, k == FC - 1, (("a", k), kd[k]), (bk,))
                    self.copy("act", f[:, 2 * dp:2 * dp + 2, :], bank[:], (bk,), (("f", dp),))

            def stage_post(i):
                sl = i % 2
                t0 = i * T
                fk = tuple(("f", dp) for dp in range(4))
                hk = tuple(("hT", sl, cp) for cp in range(4))
                self.rms_stats(f, sq, stp[:, 0:T], rstd[1], fk, "sq", "st", ("rstd", 1), T)
                for c in range(DC):
                    self.stt("dve", f[:, c, :], f[:, c, :], vec[:, g2 + c:g2 + c + 1], rstd[1][:],
                             ALU.mult, ALU.mult, (("f", c // 2), ("rstd", 1), "vec"), (("f", c // 2),))
                self.tt("dve", hT[sl][:], hT[sl][:], f[:], ALU.add, fk + hk, hk)
                if which == 2:
                    OUTA = self.OUT.ap()
                    for s_ in range(2):
                        for half in range(2):
                            for cc in range(4):
                                c = 4 * half + cc
                                self.tr(tp[0][:, cc * 128:(cc + 1) * 128], hT[sl][:, c, s_ * 128:(s_ + 1) * 128], ident[:],
                                        hk + ("ident",), (("tp", 0),))
                            self.copy("act", xs[:, s_, half * 512:(half + 1) * 512], tp[0][:], (("tp", 0),), ("xs",))
                    self.dma("pool", OUTA[t0:t0 + T, :].rearrange("(s p) d -> p s d", p=128), xs[:], ("xs",),
                             (("OUT", i),), "st_out")
                    return
                self.dma("pool", H1T[:, :, t0:t0 + T], hT[sl][:], hk, (("H1T", i),), ("st_h", sl))
                if which == 1:
                    self.rms_stats(hT[sl], sq, stp[:, 0:T], rstd[1], hk, "sq", "st", ("rstd", 1), T)
                    for c in range(DC):
                        self.stt("dve", un[:, c, :], hT[sl][:, c, :], vec[:, g3 + c:g3 + c + 1], rstd[1][:],
                                 ALU.mult, ALU.mult, (("hT", sl, c // 2), ("rstd", 1), "vec"), ("un",))
                    self.dma("pool", UNT[:, :, t0:t0 + T], un[:], ("un",), (("UNT", i),), "st_un")

            nt = NT if not self.debug_stage or os.environ.get("KFULL") else int(os.environ.get("KNT", "2"))
            self.nt_dbg = nt
            kstop = int(os.environ.get("KSTOP", "99"))
            if kstop >= 1:
                stage_a(0)
            for i in range(nt):
                if kstop >= 2:
                    stage_gu(i)
                if i + 1 < nt and kstop >= 1:
                    stage_a(i + 1)
                if kstop >= 3:
                    stage_d(i)
                if kstop >= 4:
                    stage_post(i)
            self.sch.barrier()
            self.sch.emit()
        self.phase = None


    def phase_proj(self):
        T = 512
        NT = S // T
        with ExitStack() as ph:
            self.phase = ph
            win = self.sb([128, DC, WIN], BF16)
            kw = self.load_weight(win, self.W["w_in"], DC, WIN, "win")
            un = [self.sb([128, DC, T], BF16) for _ in range(2)]
            qst = [self.sb([128, T], BF16) for _ in range(2)]
            vst = self.sb([128, 4, 8, 128], BF16)
            small = self.sb([8, 4], F32)
            fe = self.sb([8, T], F32)
            G = [self.sb([8, T], F32) for _ in range(2)]
            g8 = self.sb([8, T], F32)
            r1 = self.sb([8, T], F32)
            gk3 = self.sb([8, 3, T], BF16)
            ng3 = self.sb([8, 3, T], BF16)
            ones3 = self.sb([8, 3, T], BF16)
            onesf = self.sb([8, T], F32)
            pq = [self.ps([128, 512], F32) for _ in range(4)]
            pv = [self.ps([128, 512], F32) for _ in range(2)]
            pf = self.ps([128, 512], F32)
            pn = self.ps([128, 512], F32)
            cb = [self.sb([128, T + 3], F32) for _ in range(12)]
            acc = [self.sb([128, T], F32) for _ in range(2)]
            sgm = [self.sb([128, T], F32) for _ in range(2)]
            sqb = self.sb([128, T], BF16)
            rs = self.sb([128, T], F32)
            yst = [self.sb([128, T], F32) for _ in range(2)]
            zst = [self.sb([128, T], BF16) for _ in range(2)]
            ge = self.sb([8, T], F32)
            gr = self.sb([8, T], F32)
            gg = self.sb([8, T], F32)
            onec = self.sb([8, 1], F32)
            vec = self.vec
            cw0 = self.VROW["gdn_conv_w"]
            GQT, GKT, GVT, ZT, BGT = self.GQT.ap(), self.GKT.ap(), self.GVT.ap(), self.ZT.ap(), self.BGT.ap()
            for j in range(12):
                self.sch.add("pool", lambda e, j=j: e.memset(cb[j][:, 0:3], 0.0), (), (("cb", j),))
            UNT = self.UNT.ap().rearrange("(c p) t -> p c t", p=128)
            QA, KA, VA = self.QA.ap(), self.KA.ap(), self.VA.ap()
            self.sch.add("pool", lambda e: e.memset(vst[:], 1.0), (), tuple(("vst", s_) for s_ in range(4)))
            self.sch.add("pool", lambda e: e.memset(ones3[:], 1.0), (), ("ones3",))
            self.sch.add("pool", lambda e: e.memset(onesf[:], 1.0), (), ("onesf",))
            self.dma("sp", small[:], self.SMALL.ap(), (), ("small",), "small")
            self.ts("dve", small[:, 3:4], small[:, 0:1], -1.0, None, ALU.mult, None, ("small",), ("small",))
            self.sch.add("pool", lambda e: e.memset(onec[:], 1.0), (), ("onec",))
            gsc = self.sb([8, 2], F32)
            self.sch.add("pool", lambda e: e.affine_select(gsc[:, 0:1], onec[:], [[0, 1]], ALU.is_ge, -1.0,
                                                           base=-4, channel_multiplier=1), ("onec",), ("gsc",))
            self.act(gsc[:, 1:2], small[:, 1:2], AF.Exp, ("small", "gsc"), ("gsc",))
            self.ts("dve", gsc[:, 1:2], gsc[:, 1:2], -1.0, None, ALU.mult, None, ("gsc",), ("gsc",))
            ycnt = 0
            for i in range(NT):
                sl = i % 2
                t0 = i * T
                self.dma("sp", un[sl][:], UNT[:, :, t0:t0 + T], (("UNT", i),), (("un", sl),), ("un", sl))
                uk = ("un", sl)
                for g in range(8):
                    bank = pq[g % 4]
                    bk = ("pq", g % 4)
                    for k in range(DC):
                        self.mm(bank[:, 0:T], win[:, k, 128 * g:128 * g + 128], un[sl][:, k, :],
                                k == 0, k == DC - 1, (uk, kw[k]), (bk,))
                    self.copy("act", qst[g % 2][:], bank[:, 0:T], (bk,), (("qst", g % 2),))
                    dst = QA if g < 4 else KA
                    h0 = 2 * (g % 4)
                    for hh in range(2):
                        self.dma("pool", dst[h0 + hh, 0:64, t0:t0 + T], qst[g % 2][64 * hh:64 * hh + 64, :],
                                 (("qst", g % 2),), (("QK", g, hh, i),), ("qst", g % 2))
                for s_ in range(4):
                    bank = pv[s_ % 2]
                    bk = ("pv", s_ % 2)
                    for k in range(DC):
                        self.mm(bank[:, 0:512], un[sl][:, k, s_ * 128:(s_ + 1) * 128], win[:, k, 1024:1536],
                                k == 0, k == DC - 1, (uk, kw[k]), (bk,))
                    self.copy("dve", vst[:, s_, :, 0:64], bank[:, 0:512].rearrange("p (h c) -> p h c", c=64),
                              (bk,), (("vst", s_),))
                self.dma("pool", VA[t0:t0 + T].rearrange("(s p) h c -> p s h c", p=128), vst[:],
                         tuple(("vst", s_) for s_ in range(4)), (("VA", i),), "vst")
                for k in range(DC):
                    self.mm(pf[0:8, 0:T], win[:, k, 1536:1544], un[sl][:, k, :], k == 0, k == DC - 1, (uk, kw[k]), ("pf",))
                self.act(fe[:], pf[0:8, 0:T], AF.Exp, ("pf", "small"), ("fe",), bias=small[:, 3:4], scale=-1.0)
                self.act(fe[:], fe[:], AF.Ln, ("fe",), ("fe",), bias=1.0)
                init = 0.0 if i == 0 else G[1 - sl][:, T - 1:T]
                self.sch.add("dve", lambda e, sl=sl, init=init: e.tensor_tensor_scan(G[sl][:], onesf[:], fe[:], init,
                                                                                   ALU.mult, ALU.add),
                             ("fe", "onesf", ("G", 1 - sl)), (("G", sl),), cost=1100.0)
                self.ts("dve", g8[:], G[sl][:], 8.0, None, ALU.mult, None, (("G", sl),), ("g8",))
                self.copy("dve", gk3[:, 0, :], g8[:], ("g8",), ("gk3",))
                self.tt("dve", r1[:], g8[:], gk3[:, 0, :], ALU.subtract, ("g8", "gk3"), ("r1",))
                self.copy("dve", gk3[:, 1, :], r1[:], ("r1",), ("gk3",))
                self.tt("dve", r1[:], r1[:], gk3[:, 1, :], ALU.subtract, ("r1", "gk3"), ("r1",))
                self.copy("dve", gk3[:, 2, :], r1[:], ("r1",), ("gk3",))
                self.ts("dve", ng3[:], gk3[:], -1.0, None, ALU.mult, None, ("gk3",), ("ng3",))
                self.dma("pool", KA[:, 64:67, t0:t0 + T], gk3[:], ("gk3",), (("KAg", i),), "gk3")
                self.dma("pool", KA[:, 67:70, t0:t0 + T], ones3[:], ("ones3",), (("KAo", i),), "ones3")
                self.dma("pool", QA[:, 64:67, t0:t0 + T], ones3[:], ("ones3",), (("QAo", i),), "ones3")
                self.dma("pool", QA[:, 67:70, t0:t0 + T], ng3[:], ("ng3",), (("QAg", i),), "ng3")
                for j in range(16):
                    bank = pq[j % 4]
                    bk = ("pq", j % 4)
                    c0 = 1544 + 128 * j
                    for k in range(DC):
                        self.mm(bank[:, 0:T], win[:, k, c0:c0 + 128], un[sl][:, k, :], k == 0, k == DC - 1, (uk, kw[k]), (bk,))
                    hh = j % 4
                    if j >= 12:
                        zs = ycnt % 2
                        self.copy("act", zst[zs][:], bank[:, 0:T], (bk,), (("zst", zs),))
                        self.dma("pool", ZT[hh, :, t0:t0 + T], zst[zs][:], (("zst", zs),), (("ZT", hh, i),), ("zst", zs))
                        ycnt += 1
                        continue
                    ck = ("cb", j)
                    self.copy("act", cb[j][:, 3:3 + T], bank[:, 0:T], (bk,), (ck,))
                    a_ = ycnt % 2
                    ak = ("acc", a_)
                    self.ts("dve", acc[a_][:], cb[j][:, 0:T], vec[:, cw0 + j:cw0 + j + 1], None, ALU.mult, None,
                            (ck, "vec"), (ak,))
                    for tap in range(1, 4):
                        self.stt("dve", acc[a_][:], cb[j][:, tap:tap + T], vec[:, cw0 + 12 * tap + j:cw0 + 12 * tap + j + 1],
                                 acc[a_][:], ALU.mult, ALU.add, (ck, ak, "vec"), (ak,))
                    self.copy("dve", cb[j][:, 0:3], cb[j][:, T:T + 3], (ck,), (ck,))
                    sk = ("sgm", a_)
                    self.act(sgm[a_][:], acc[a_][:], AF.Exp, (ak,), (sk,), scale=-1.0)
                    self.act(sgm[a_][:], sgm[a_][:], AF.Ln, (sk,), (sk,), bias=1.0)
                    self.act(sgm[a_][:], sgm[a_][:], AF.Exp, (sk,), (sk,), scale=-1.0)
                    yk = ("yst", a_)
                    self.tt("dve", yst[a_][:], acc[a_][:], sgm[a_][:], ALU.mult, (ak, sk), (yk,))
                    if j < 8:
                        self.tt("dve", sqb[:], yst[a_][:], yst[a_][:], ALU.mult, (yk,), ("sqb",))
                        self.mm(pn[:, 0:T], self.ones_bf[:], sqb[:], True, True, ("sqb", "ones_bf"), ("pn",))
                        self.act(rs[:], pn[:, 0:T], AF.Ln, ("pn",), ("rs",), bias=EPS, scale=1024.0)
                        self.act(rs[:], rs[:], AF.Exp, ("rs",), ("rs",), scale=-0.5,
                                 bias=(-0.5 * float(np.log(128.0)) if j < 4 else 0.0))
                        self.tt("dve", yst[a_][:], yst[a_][:], rs[:], ALU.mult, (yk, "rs"), (yk,))
                    dstT = (GQT, GKT, GVT)[j // 4]
                    self.dma("pool", dstT[hh, :, t0:t0 + T], yst[a_][:], (yk,), (("G3", j, i),), ("yst", a_))
                    ycnt += 1
                for k in range(DC):
                    self.mm(pf[0:8, 0:T], win[:, k, 3592:3600], un[sl][:, k, :], k == 0, k == DC - 1, (uk, kw[k]), ("pf",))
                self.act(ge[:], pf[0:8, 0:T], AF.Exp, ("pf", "gsc", "small"), ("ge",), bias=small[:, 2:3], scale=gsc[:, 0:1])
                self.act(ge[:], ge[:], AF.Ln, ("ge",), ("ge",), bias=1.0)
                self.act(gr[:], ge[:], AF.Exp, ("ge",), ("gr",), scale=-1.0)
                self.ts("dve", gg[:], ge[:], gsc[:, 1:2], None, ALU.mult, None, ("ge", "gsc"), ("gg",))
                self.dma("pool", BGT[0:4, t0:t0 + T], gr[0:4, :], ("gr",), (("BGb", i),), "gr")
                self.dma("pool", BGT[4:8, t0:t0 + T], gg[4:8, :], ("gg",), (("BGg", i),), "gg")
            self.sch.barrier()
            self.sch.emit()
        self.phase = None

    def phase_fox(self):
        T = 512
        NT = S // T
        NS = S // 128
        with ExitStack() as ph:
            self.phase = ph
            ka = [self.sb([70, S], BF16) for _ in range(2)]
            va = [self.sb([128, NS, 128], BF16) for _ in range(2)]
            qa = [self.sb([70, T], BF16) for _ in range(2)]
            pT = [self.sb([128, T], BF16) for _ in range(3)]
            rec = self.sb([128, T], F32)
            osb = self.sb([64, T], F32)
            mfst = [self.sb([64, T], BF16) for _ in range(2)]
            onesq = self.sb([128, 64], F32)
            shf = self.sb([128, 64], F32)
            sc = [self.ps([128, 512], F32) for _ in range(3)]
            ob = [self.ps([128, 512], F32) for _ in range(2)]
            rb = self.ps([128, 512], F32)
            QA, KA, VA, MF = self.QA.ap(), self.KA.ap(), self.VA.ap(), self.MF.ap()
            self.sch.add("pool", lambda e: e.memset(onesq[:], 1.0), (), ("onesq",))
            self.sch.add("pool", lambda e: e.affine_select(shf[:], onesq[:], [[-1, 64]], ALU.is_equal, 0.0,
                                                           base=-64, channel_multiplier=1), ("onesq",), ("shf",))
            LOOK = 2
            tiles = [(h, i) for h in range(8) for i in range(NT)]

            def load_head(h):
                hs = h % 2
                self.dma("sp", ka[hs][:], KA[h], (), (("ka", hs),), ("ka", hs))
                self.dma("sp", va[hs][:], VA[:, h, :].rearrange("(j p) c -> p j c", p=128), (), (("va", hs),), ("va", hs))

            def load_q(tix):
                h, i = tiles[tix]
                qs = tix % 2
                self.dma("sp", qa[qs][:], QA[h, :, i * T:(i + 1) * T], (), (("qa", qs),), ("qa", qs))

            items = []
            for tix, (h, i) in enumerate(tiles):
                nj = 4 * i + 4
                for j in range(nj):
                    items.append((tix, h, i, j, nj))

            def front(k):
                tix, h, i, j, nj = items[k]
                hs, qs = h % 2, tix % 2
                if j == 0 and tix + 1 < len(tiles):
                    load_q(tix + 1)
                jj = j - 4 * i
                c0 = 128 * jj if jj > 0 else 0
                b = k % 3
                self.mm(sc[b][:, c0:T], ka[hs][:, j * 128:(j + 1) * 128], qa[qs][:, c0:T], True, True,
                        (("ka", hs), ("qa", qs)), (("sc", b),))
                self.act(pT[b][:, c0:T], sc[b][:, c0:T], AF.Exp, (("sc", b),), (("pT", b),), scale=0.125)
                if jj >= 0:
                    self.sch.add("pool", lambda e, b=b, c0=c0: e.affine_select(
                        pT[b][:, c0:c0 + 128], pT[b][:, c0:c0 + 128], [[1, 128]], ALU.is_ge, 0.0,
                        base=0, channel_multiplier=-1), (("pT", b),), (("pT", b),), cost=300.0)

            def back(k):
                tix, h, i, j, nj = items[k]
                hs = h % 2
                jj = j - 4 * i
                c0 = 128 * jj if jj > 0 else 0
                b = k % 3
                o = ob[tix % 2]
                okey = ("ob", tix % 2)
                if j == 0 and i == 0 and h + 1 < 8:
                    load_head(h + 1)
                self.mm(o[:, c0:T], va[hs][:, j, :], pT[b][:, c0:T], j == 0, j == nj - 1,
                        (("va", hs), ("pT", b)), (okey,))
                if j == nj - 1:
                    t0 = i * T
                    self.sch.add("dve", lambda e, o=o: e.reciprocal(rec[64:128, :], o[64:128, 0:T]), (okey,), ("rec",), cost=660.0)
                    self.mm(rb[0:64, 0:T], shf[64:128, :], rec[64:128, :], True, True, ("rec", "shf"), ("rb",))
                    self.copy("act", osb[:], o[0:64, 0:T], (okey,), ("osb",))
                    ms = tix % 2
                    self.tt("dve", mfst[ms][:], osb[:], rb[0:64, 0:T], ALU.mult, ("osb", "rb"), (("mfst", ms),))
                    self.dma("pool", MF[h, :, t0:t0 + T], mfst[ms][:], (("mfst", ms),), (("MF", h, i),), ("mfst", ms))

            load_head(0)
            load_q(0)
            n_items = len(items)
            for k in range(min(LOOK, n_items)):
                front(k)
            for k in range(n_items):
                if k + LOOK < n_items:
                    front(k + LOOK)
                back(k)
            self.sch.barrier()
            self.sch.emit()
        self.phase = None


    def phase_mix_mem(self):
        T = 256
        NT = S // T
        MH = 4
        with ExitStack() as ph:
            self.phase = ph
            vec = self.vec
            ident = self.ident
            wof = self.sb([64, 8, D], BF16)
            wog = self.sb([128, 4, D], BF16)
            wq = self.sb([128, DC, D], BF16)
            wo = self.sb([128, DC, D], BF16)
            kmT = self.sb([128, DC, NMEM], BF16)
            vm = self.sb([128, 2, D], BF16)
            ones1 = self.sb([128, 128], BF16)
            self.sch.add("pool", lambda e: e.memset(ones1[:], 1.0), (), ("ones1",))
            WO = self.W["w_out"].ap()
            for h in range(8):
                self.dma("pool", wof[:, h, :], WO[64 * h:64 * h + 64, :], (), (("wof", h),), "wof")
            kof = [("wof", h) for h in range(8)]
            kog = self.load_weight(wog, self.W["w_out"], 4, D, "wog", row0=512)
            kq = self.load_weight(wq, self.W["mem_w_q"], DC, D, "wq")
            ko = self.load_weight(wo, self.W["mem_w_o"], DC, D, "wo")
            hT = [self.sb([128, DC, T], F32) for _ in range(2)]
            mf = [self.sb([64, 8, T], BF16) for _ in range(2)]
            mg = [self.sb([128, 4, T], BF16) for _ in range(2)]
            f_ = [self.sb([128, DC, T], F32) for _ in range(2)]
            sq_ = [self.sb([128, DC, T], BF16) for _ in range(2)]
            rstd_ = [self.sb([128, T], F32) for _ in range(2)]
            hq_ = [self.sb([128, DC, T], BF16) for _ in range(2)]
            qT_ = [self.sb([128, DC, T], BF16) for _ in range(2)]
            pT = [self.sb([128, T], BF16) for _ in range(4)]
            rec_ = [self.sb([128, T], F32) for _ in range(2)]
            oT_ = [self.sb([128, DC, T], BF16) for _ in range(2)]
            sq, rstd = sq_[0], rstd_[0]
            pa = [self.ps([128, 512], F32) for _ in range(2)]
            stp = self.ps([128, 512], F32)
            scp = [self.ps([128, 512], F32) for _ in range(2)]
            ssp = self.ps([128, 512], F32)
            op_ = [self.ps([128, 512], F32) for _ in range(2)]
            gpo = self.VROW["mix_post_norm"]
            gmq = self.VROW["mem_pre_norm"]
            gkv = self.VROW["mem_kv_norm"]
            gmo = self.VROW["mem_post_norm"]
            with ExitStack() as ph2:
                self.phase = ph2
                wkv = self.sb([128, DC, 2 * D], BF16)
                kkv = self.load_weight(wkv, self.W["mem_w_kv"], DC, 2 * D, "wkv")
                ms = self.sb([128, 2, D], F32)
                mT = self.sb([128, DC, NMEM], F32)
                mn = self.sb([128, DC, NMEM], BF16)
                self.dma("sp", ms[:], self.MEM.ap().rearrange("(s p) d -> p s d", p=128), (), ("ms",), "ms")
                for cp in range(4):
                    for cc in range(2):
                        c = 2 * cp + cc
                        for s_ in range(2):
                            self.tr(pa[0][:, cc * 256 + s_ * 128: cc * 256 + (s_ + 1) * 128],
                                    ms[:, s_, c * 128:(c + 1) * 128], ident[:], ("ms", "ident"), (("pa", 0),))
                    self.copy("act", mT[:, 2 * cp:2 * cp + 2, :], pa[0][:], (("pa", 0),), ("mT",))
                self.rms_stats(mT, sq, stp[:, 0:T], rstd, ("mT",), "sq", "st", "rstd", T)
                for c in range(DC):
                    self.stt("dve", mn[:, c, :], mT[:, c, :], vec[:, gkv + c:gkv + c + 1], rstd[:],
                             ALU.mult, ALU.mult, ("mT", "rstd", "vec"), ("mn",))
                for dt in range(DC):
                    b = dt % 2
                    for k in range(DC):
                        self.mm(pa[b][:, 0:NMEM], wkv[:, k, dt * 128:(dt + 1) * 128], mn[:, k, :], k == 0, k == DC - 1,
                                ("mn", kkv[k]), (("pa", b),))
                    self.copy("act", kmT[:, dt, :], pa[b][:, 0:NMEM], (("pa", b),), ("kmT",))
                for mt in range(2):
                    for half in range(2):
                        b = (2 * mt + half) % 2
                        for k in range(DC):
                            self.mm(pa[b][:, 0:512], mn[:, k, mt * 128:(mt + 1) * 128],
                                    wkv[:, k, D + half * 512:D + half * 512 + 512], k == 0, k == DC - 1,
                                    ("mn", kkv[k]), (("pa", b),))
                        self.copy("act", vm[:, mt, half * 512:half * 512 + 512], pa[b][:, 0:512], (("pa", b),), ("vm",))
                self.sch.barrier()
                self.sch.emit()
            self.phase = ph
            H1T = self.H1T.ap().rearrange("(c p) t -> p c t", p=128)
            MF = self.MF.ap()
            MG = self.MG.ap()
            pcnt_box = [0]

            def half1(i):
                sl = i % 2
                t0 = i * T
                self.dma("sp", hT[sl][:], H1T[:, :, t0:t0 + T], (("H1T", i),), (("hT", sl),), ("hT", sl))
                self.dma("sp", mf[sl][:], MF[:, :, t0:t0 + T].rearrange("h p t -> p h t"), (), (("mf", sl),), ("mf", sl))
                self.dma("sp", mg[sl][:], MG[:, :, t0:t0 + T].rearrange("h p t -> p h t"), (), (("mg", sl),), ("mg", sl))
                hk = ("hT", sl)
                f, sq, rstd, hq, qT, rec, oT = f_[sl], sq_[sl], rstd_[sl], hq_[sl], qT_[sl], rec_[sl], oT_[sl]
                fK, sqK, rsK, hqK, recK = ("f", sl), ("sq", sl), ("rstd", sl), ("hq", sl), ("rec", sl)
                for dt in range(DC):
                    b = 0
                    for h in range(8):
                        self.mm(pa[b][:, 0:T], wof[:, h, dt * 128:(dt + 1) * 128], mf[sl][:, h, :], h == 0, False,
                                (("mf", sl), kof[h]), (("pa", b),))
                    for j in range(4):
                        self.mm(pa[b][:, 0:T], wog[:, j, dt * 128:(dt + 1) * 128], mg[sl][:, j, :], False, j == 3,
                                (("mg", sl), kog[j]), (("pa", b),))
                    self.copy("act", f[:, dt, :], pa[b][:, 0:T], (("pa", b),), (fK,))
                self.rms_stats(f, sq, stp[:, 0:T], rstd, (fK,), sqK, "st", rsK, T)
                for c in range(DC):
                    self.stt("dve", f[:, c, :], f[:, c, :], vec[:, gpo + c:gpo + c + 1], rstd[:],
                             ALU.mult, ALU.mult, (fK, rsK, "vec"), (fK,))
                self.tt("dve", hT[sl][:], hT[sl][:], f[:], ALU.add, (fK, hk), (hk,))
                self.rms_stats(hT[sl], sq, stp[:, 0:T], rstd, (hk,), sqK, "st", rsK, T)
                for c in range(DC):
                    self.stt("dve", hq[:, c, :], hT[sl][:, c, :], vec[:, gmq + c:gmq + c + 1], rstd[:],
                             ALU.mult, ALU.mult, (hk, rsK, "vec"), (hqK,))
                for dt in range(DC):
                    b = 1
                    for k in range(DC):
                        self.mm(pa[b][:, 0:T], wq[:, k, dt * 128:(dt + 1) * 128], hq[:, k, :], k == 0, k == DC - 1,
                                (hqK, kq[k]), (("pa", b),))
                    self.copy("act", qT[:, dt, :], pa[b][:, 0:T], (("pa", b),), (("qT", sl, dt),))

            def half2(i):
                sl = i % 2
                t0 = i * T
                hk = ("hT", sl)
                f, sq, rstd, hq, qT, rec, oT = f_[sl], sq_[sl], rstd_[sl], hq_[sl], qT_[sl], rec_[sl], oT_[sl]
                fK, sqK, rsK, hqK, recK = ("f", sl), ("sq", sl), ("rstd", sl), ("hq", sl), ("rec", sl)
                pcnt = pcnt_box[0]
                for hd in range(MH):
                    pk = []
                    for mt in range(2):
                        b = pcnt % 2
                        pb = pcnt % 4
                        pcnt += 1
                        for dc in range(2):
                            self.mm(scp[b][:, 0:T], kmT[:, 2 * hd + dc, mt * 128:(mt + 1) * 128], qT[:, 2 * hd + dc, :],
                                    dc == 0, dc == 1, (("qT", sl, 2 * hd + dc), "kmT"), (("sc", b),))
                        self.act(pT[pb][:], scp[b][:, 0:T], AF.Exp, (("sc", b),), (("pT", pb),), scale=1.0 / 16.0)
                        pk.append(pb)
                    for mt in range(2):
                        self.mm(ssp[:, 0:T], ones1[:], pT[pk[mt]][:], mt == 0, mt == 1, (("pT", pk[mt]), "ones1"), ("ss",))
                    self.sch.add("dve", lambda e, rec=rec: e.reciprocal(rec[:], ssp[:, 0:T]), ("ss",), (recK,), cost=400.0)
                    for dc in range(2):
                        b = 0
                        for mt in range(2):
                            self.mm(op_[b][:, 0:T], vm[:, mt, (2 * hd + dc) * 128:(2 * hd + dc + 1) * 128], pT[pk[mt]][:],
                                    mt == 0, mt == 1, (("pT", pk[mt]), "vm"), (("op", b),))
                        self.tt("dve", oT[:, 2 * hd + dc, :], op_[b][:, 0:T], rec[:], ALU.mult, (("op", b), recK),
                                (("oT", sl, 2 * hd + dc),))
                for dt in range(DC):
                    for k in range(DC):
                        self.mm(op_[1][:, 0:T], wo[:, k, dt * 128:(dt + 1) * 128], oT[:, k, :], k == 0, k == DC - 1,
                                (("oT", sl, k), ko[k]), (("op", 1),))
                    self.copy("act", f[:, dt, :], op_[1][:, 0:T], (("op", 1),), (fK,))
                self.rms_stats(f, sq, stp[:, 0:T], rstd, (fK,), sqK, "st", rsK, T)
                for c in range(DC):
                    self.stt("dve", f[:, c, :], f[:, c, :], vec[:, gmo + c:gmo + c + 1], rstd[:],
                             ALU.mult, ALU.mult, (fK, rsK, "vec"), (fK,))
                self.tt("dve", hT[sl][:], hT[sl][:], f[:], ALU.add, (fK, hk), (hk,))
                self.dma("pool", H1T[:, :, t0:t0 + T], hT[sl][:], (hk,), (("H1T", i),), ("st_h", sl))
                pcnt_box[0] = pcnt

            half1(0)
            for i in range(NT):
                if i + 1 < NT:
                    half1(i + 1)
                half2(i)
            self.sch.barrier()
            self.sch.emit()
        self.phase = None

    def phase_gdn(self):
        T = 512
        NT = S // T
        C = 64
        NCH = T // C
        with ExitStack() as ph:
            self.phase = ph
            vec = self.vec
            ident = self.ident
            gno = self.VROW["gdn_out_norm"]
            GQT, GKT, GVT, ZT, BGT, MG = (self.GQT.ap(), self.GKT.ap(), self.GVT.ap(), self.ZT.ap(),
                                          self.BGT.ap(), self.MG.ap())
            f32t = lambda shape: self.sb(shape, F32)
            ones_r = f32t([1, 128])
            ones64 = f32t([64, 512])
            mSU, mIU, mSL, Iall = f32t([64, 512]), f32t([64, 512]), f32t([64, 512]), f32t([64, 512])
            self.sch.add("pool", lambda e: e.memset(ones_r[:], 1.0), (), ("ones_r",))
            self.sch.add("pool", lambda e: e.memset(ones64[:], 1.0), (), ("ones64",))
            pat = [[0, NCH], [1, C]]
            for m_, op_, base in ((mSU, ALU.is_gt, 0), (mIU, ALU.is_ge, 0), (Iall, ALU.is_equal, 0)):
                self.sch.add("pool", lambda e, m_=m_, op_=op_, base=base: e.affine_select(
                    m_[:], ones64[:], pat, op_, 0.0, base=base, channel_multiplier=-1), ("ones64",), (("mask", id(m_)),))
            self.sch.add("pool", lambda e: e.affine_select(mSL[:], ones64[:], [[0, NCH], [-1, C]], ALU.is_gt, 0.0,
                                                           base=0, channel_multiplier=1), ("ones64",), (("mask", id(mSL)),))
            mk = tuple(("mask", id(m_)) for m_ in (mSU, mIU, mSL, Iall))
            qT = [f32t([128, T]) for _ in range(2)]
            kT = [f32t([128, T]) for _ in range(2)]
            vT = [f32t([128, T]) for _ in range(2)]
            zT = [self.sb([128, T], BF16) for _ in range(2)]
            brow = [f32t([1, T]) for _ in range(2)]
            grow = [f32t([1, T]) for _ in range(2)]
            gc, ngc, eg, et = f32t([1, T]), f32t([1, T]), f32t([1, T]), f32t([1, T])
            kbT, vbT, kbgT, ktT = f32t([128, T]), f32t([128, T]), f32t([128, T]), f32t([128, T])
            dmin, dmax = f32t([64, T]), f32t([64, T])
            GTs, GTi, Gs = f32t([64, T]), f32t([64, T]), f32t([64, T])
            P = [f32t([64, T]) for _ in range(2)]
            Q = [f32t([64, T]) for _ in range(2)]
            FQ = f32t([64, T])
            Z = [f32t([64, T]) for _ in range(2)]
            vb_tok = f32t([64, NCH, 128])
            kbg_tok = f32t([64, NCH, 128])
            qgT = [f32t([128, T]) for _ in range(2)]
            AT = [f32t([64, T]) for _ in range(2)]
            kt_tok = [f32t([64, NCH, 128]) for _ in range(2)]
            U = [f32t([64, NCH, 128]) for _ in range(2)]
            WT = [f32t([128, T]) for _ in range(2)]
            eglB = [f32t([128, NCH]) for _ in range(2)]
            St = f32t([128, 128])
            vnew = [f32t([64, 128]) for _ in range(2)]
            osb = f32t([128, T])
            sqo = self.sb([128, T], BF16)
            rso = f32t([128, T])
            zsg = f32t([128, T])
            mgst = [self.sb([128, T], BF16) for _ in range(2)]
            pp = [self.ps([128, 512], F32) for _ in range(4)]
            pd = self.ps([128, 512], F32)
            pw = self.ps([128, 512], F32)
            pds = self.ps([128, 512], F32)
            po = self.ps([128, 512], F32)
            ppc = [0]

            def nextpp():
                b = ppc[0] % 4
                ppc[0] += 1
                return pp[b], ("pp", b)

            def cs(n):
                return slice(n * C, (n + 1) * C)

            def load(hh, i, ti):
                sl = ti % 2
                t0 = i * T
                self.dma("sp", qT[sl][:], GQT[hh, :, t0:t0 + T], (), (("qT", sl),), ("gq", sl))
                self.dma("sp", kT[sl][:], GKT[hh, :, t0:t0 + T], (), (("kT", sl),), ("gk", sl))
                self.dma("sp", vT[sl][:], GVT[hh, :, t0:t0 + T], (), (("vT", sl),), ("gv", sl))
                self.dma("sp", zT[sl][:], ZT[hh, :, t0:t0 + T], (), (("zT", sl),), ("gz", sl))
                self.dma("sp", brow[sl][:], BGT[hh:hh + 1, t0:t0 + T], (), (("brow", sl),), ("gb", sl))
                self.dma("sp", grow[sl][:], BGT[4 + hh:5 + hh, t0:t0 + T], (), (("grow", sl),), ("gg", sl))

            def pre(hh, i, ti):
                sl = ti % 2
                th = []
                A = th.append
                qk, kk, vk = ("qT", sl), ("kT", sl), ("vT", sl)

                def rows():
                    for n in range(NCH):
                        self.sch.add("dve", lambda e, n=n: e.tensor_tensor_scan(
                            gc[:, cs(n)], ones_r[:, 0:C], grow[sl][:, cs(n)], 0.0, ALU.mult, ALU.add),
                            (("grow", sl), "ones_r"), ("gc",), cost=200.0)
                    self.ts("dve", ngc[:], gc[:], -1.0, None, ALU.mult, None, ("gc",), ("ngc",))
                    self.act(eg[:], gc[:], AF.Exp, ("gc",), ("eg",))
                    for n in range(NCH):
                        self.ts("dve", et[:, cs(n)], gc[:, cs(n)], gc[:, n * C + C - 1:n * C + C], None, ALU.subtract, None,
                                ("gc",), ("et",))
                    self.act(et[:], et[:], AF.Exp, ("et",), ("et",), scale=-1.0)
                A(rows)

                def bcast():
                    bb, bbk = nextpp()
                    self.mm(bb[:, 0:T], ones_r[:, :], brow[sl][:], True, True, (("brow", sl), "ones_r"), (bbk,))
                    self.tt("dve", kbT[:], kT[sl][:], bb[:, 0:T], ALU.mult, (kk, bbk), ("kbT",))
                    self.tt("dve", vbT[:], vT[sl][:], bb[:, 0:T], ALU.mult, (vk, bbk), ("vbT",))
                    eb, ebk = nextpp()
                    self.mm(eb[:, 0:T], ones_r[:, :], eg[:], True, True, ("eg", "ones_r"), (ebk,))
                    self.tt("dve", kbgT[:], kbT[:], eb[:, 0:T], ALU.mult, ("kbT", ebk), ("kbgT",))
                    self.tt("dve", qgT[sl][:], qT[sl][:], eb[:, 0:T], ALU.mult, (qk, ebk), (("qgT", sl),))
                    tb, tbk = nextpp()
                    self.mm(tb[:, 0:T], ones_r[:, :], et[:], True, True, ("et", "ones_r"), (tbk,))
                    self.tt("dve", ktT[:], kT[sl][:], tb[:, 0:T], ALU.mult, (kk, tbk), ("ktT",))
                    lb, lbk = nextpp()
                    self.mm(lb[:, 0:NCH], ones_r[:, :], eg[:].rearrange("p (n c) -> p n c", c=C)[:, :, C - 1], True, True,
                            ("eg", "ones_r"), (lbk,))
                    self.copy("act", eglB[sl][:], lb[:, 0:NCH], (lbk,), (("eglB", sl),))
                A(bcast)

                def decay():
                    for n in range(NCH):
                        self.mm(pd[0:C, cs(n)], ones_r[:, 0:C], gc[:, cs(n)], True, False, ("gc", "ones_r"), ("pd",))
                        self.mm(pd[0:C, cs(n)], ngc[:, cs(n)], ones_r[:, 0:C], False, True, ("ngc", "ones_r"), ("pd",))
                    self.ts("dve", dmin[:], pd[0:C, 0:T], 0.0, None, ALU.min, None, ("pd",), ("dmin",))
                    self.ts("dve", dmax[:], pd[0:C, 0:T], 0.0, None, ALU.max, None, ("pd",), ("dmax",))
                    self.act(dmin[:], dmin[:], AF.Exp, ("dmin",), ("dmin",))
                    self.act(dmax[:], dmax[:], AF.Exp, ("dmax",), ("dmax",), scale=-1.0)
                    self.tt("pool", GTs[:], dmin[:], mSU[:], ALU.mult, ("dmin",) + mk, ("GTs",))
                    self.tt("pool", GTi[:], dmin[:], mIU[:], ALU.mult, ("dmin",) + mk, ("GTi",))
                    self.tt("pool", Gs[:], dmax[:], mSL[:], ALU.mult, ("dmax",) + mk, ("Gs",))
                A(decay)

                def gram():
                    b1, k1 = nextpp()
                    for n in range(NCH):
                        self.mm(b1[0:C, cs(n)], kT[sl][:, cs(n)], kbT[:, cs(n)], True, True, (kk, "kbT"), (k1,))
                    self.stt("dve", P[0][:], b1[0:C, 0:T], -1.0, GTs[:], ALU.mult, ALU.mult, (k1, "GTs"), (("P", 0),))
                    b2, k2 = nextpp()
                    for n in range(NCH):
                        self.mm(b2[0:C, cs(n)], kbT[:, cs(n)], kT[sl][:, cs(n)], True, True, (kk, "kbT"), (k2,))
                    self.stt("dve", Q[0][:], b2[0:C, 0:T], -1.0, Gs[:], ALU.mult, ALU.mult, (k2, "Gs"), (("Q", 0),))
                    b3, k3 = nextpp()
                    for n in range(NCH):
                        self.mm(b3[0:C, cs(n)], kT[sl][:, cs(n)], qT[sl][:, cs(n)], True, True, (kk, qk), (k3,))
                    self.tt("dve", AT[sl][:], b3[0:C, 0:T], GTi[:], ALU.mult, (k3, "GTi"), (("AT", sl),))
                    self.tt("pool", Z[0][:], P[0][:], Iall[:], ALU.add, (("P", 0),) + mk, (("Z", 0),))
                A(gram)

                def level(lv):
                    def f():
                        a, b = lv % 2, (lv + 1) % 2
                        bq, kq_ = nextpp()
                        for n in range(NCH):
                            self.mm(bq[0:C, cs(n)], P[a][:, cs(n)], Q[a][:, cs(n)], True, True, (("P", a), ("Q", a)), (kq_,))
                        if lv < 4:
                            bp, kp_ = nextpp()
                            for n in range(NCH):
                                self.mm(bp[0:C, cs(n)], Q[a][:, cs(n)], P[a][:, cs(n)], True, True, (("P", a), ("Q", a)), (kp_,))
                            self.copy("act", P[b][:], bp[0:C, 0:T], (kp_,), (("P", b),))
                            self.copy("act", Q[b][:], bq[0:C, 0:T], (kq_,), (("Q", b),))
                        self.tt("dve", FQ[:], bq[0:C, 0:T], Iall[:], ALU.add, (kq_,) + mk, ("FQ",))
                        bz, kz_ = nextpp()
                        for n in range(NCH):
                            self.mm(bz[0:C, cs(n)], FQ[:, cs(n)], Z[a][:, cs(n)], True, True, ("FQ", ("Z", a)), (kz_,))
                        self.copy("act", Z[b][:], bz[0:C, 0:T], (kz_,), (("Z", b),))
                    return f
                for lv in range(5):
                    A(level(lv))

                def toks():
                    for src, skey, dst, dkey in ((vbT, "vbT", vb_tok, "vb_tok"), (kbgT, "kbgT", kbg_tok, "kbg_tok"),
                                                 (ktT, "ktT", kt_tok[sl], ("kt_tok", sl))):
                        for half in range(2):
                            bt, kt_ = nextpp()
                            for n4 in range(4):
                                n = 4 * half + n4
                                self.tr(bt[0:C, n4 * 128:(n4 + 1) * 128], src[:, cs(n)], ident[:], (skey, "ident"), (kt_,))
                            self.copy("act", dst[:, 4 * half:4 * half + 4, :],
                                      bt[0:C, 0:512].rearrange("p (n e) -> p n e", e=128), (kt_,), (dkey,))
                A(toks)

                def uw():
                    TT = Z[1]
                    tk = ("Z", 1)
                    for half in range(2):
                        bu, ku_ = nextpp()
                        for n4 in range(4):
                            n = 4 * half + n4
                            self.mm(bu[0:C, n4 * 128:(n4 + 1) * 128], TT[:, cs(n)], vb_tok[:, n, :], True, True,
                                    (tk, "vb_tok"), (ku_,))
                        self.copy("act", U[sl][:, 4 * half:4 * half + 4, :],
                                  bu[0:C, 0:512].rearrange("p (n e) -> p n e", e=128), (ku_,), (("U", sl),))
                    bw, kw_ = nextpp()
                    for n in range(NCH):
                        self.mm(bw[:, cs(n)], kbg_tok[:, n, :], TT[:, cs(n)], True, True, (tk, "kbg_tok"), (kw_,))
                    self.copy("act", WT[sl][:], bw[:, 0:T], (kw_,), (("WT", sl),))
                A(uw)
                return th

            def seq_chunk(hh, i, n, ti):
                sl = ti % 2
                v = vnew[n % 2]
                vk = ("vnew", n % 2)
                self.mm(pw[0:C, 0:128], WT[sl][:, cs(n)], St[:], True, True, (("WT", sl), "St"), ("pw",))
                self.tt("dve", v[:], U[sl][:, n, :], pw[0:C, 0:128], ALU.subtract, (("U", sl), "pw"), (vk,))
                self.mm(po[:, cs(n)], St[:], qgT[sl][:, cs(n)], True, False, ("St", ("qgT", sl)), ("po",))
                self.mm(po[:, cs(n)], v[:], AT[sl][:, cs(n)], False, True, (vk, ("AT", sl)), ("po",))
                self.mm(pds[:, 0:128], kt_tok[sl][:, n, :], v[:], True, True, (("kt_tok", sl), vk), ("pds",))
                self.stt("dve", St[:], St[:], eglB[sl][:, n:n + 1], pds[:, 0:128], ALU.mult, ALU.add,
                         ("St", ("eglB", sl), "pds"), ("St",))

            def finish(hh, i, ti):
                sl = ti % 2
                t0 = i * T
                ms = ti % 2
                self.copy("act", osb[:], po[:, 0:T], ("po",), ("osb",))
                self.tt("dve", sqo[:], osb[:], osb[:], ALU.mult, ("osb",), ("sqo",))
                bs, ks_ = nextpp()
                self.mm(bs[:, 0:T], self.ones_bf[:], sqo[:], True, True, ("sqo", "ones_bf"), (ks_,))
                self.act(rso[:], bs[:, 0:T], AF.Ln, (ks_,), ("rso",), bias=EPS, scale=8.0)
                self.act(rso[:], rso[:], AF.Exp, ("rso",), ("rso",), scale=-0.5)
                self.stt("dve", osb[:], osb[:], vec[:, gno:gno + 1], rso[:], ALU.mult, ALU.mult, ("osb", "rso", "vec"), ("osb",))
                self.act(zsg[:], zT[sl][:], AF.Exp, (("zT", sl),), ("zsg",), scale=-1.0)
                self.act(zsg[:], zsg[:], AF.Ln, ("zsg",), ("zsg",), bias=1.0)
                self.act(zsg[:], zsg[:], AF.Exp, ("zsg",), ("zsg",), scale=-1.0)
                self.tt("dve", zsg[:], zsg[:], zT[sl][:], ALU.mult, ("zsg", ("zT", sl)), ("zsg",))
                self.tt("dve", mgst[ms][:], osb[:], zsg[:], ALU.mult, ("osb", "zsg"), (("mgst", ms),))
                self.dma("pool", MG[hh, :, t0:t0 + T], mgst[ms][:], (("mgst", ms),), (("MG", hh, i),), ("mgst", ms))

            interleave = os.environ.get("KGDN_IL", "1") == "1"
            tiles = [(hh, i) for hh in range(4) for i in range(NT)]
            load(*tiles[0], 0)
            cur = pre(*tiles[0], 0)
            for th in cur:
                th()
            for ti, (hh, i) in enumerate(tiles):
                nxt = []
                if ti + 1 < len(tiles):
                    load(*tiles[ti + 1], ti + 1)
                    nxt = pre(*tiles[ti + 1], ti + 1)
                if i == 0:
                    self.sch.add("pool", lambda e: e.memset(St[:], 0.0), ("St",), ("St",))
                per = (len(nxt) + NCH - 1) // NCH if interleave else 0
                for n in range(NCH):
                    seq_chunk(hh, i, n, ti)
                    for th in nxt[n * per:(n + 1) * per]:
                        th()
                finish(hh, i, ti)
                for th in nxt[NCH * per:]:
                    th()
            self.sch.barrier()
            self.sch.emit(reorder="gdn" if os.environ.get("KGDN_RE", "1") == "1" else False)
        self.phase = None

    def phase_gdn_stub(self):
        with ExitStack() as ph:
            self.phase = ph
            z = self.sb([128, 4, 2048], BF16)
            self.sch.add("pool", lambda e: e.memset(z[:], 0.0), (), ("z",))
            MG = self.MG.ap()
            for t0 in range(0, S, 2048):
                n = min(2048, S - t0)
                self.dma("pool", MG[:, :, t0:t0 + n].rearrange("h p t -> p h t"), z[:, :, 0:n], ("z",), (("MGz", t0),), "mgz")
            self.sch.barrier()
            self.sch.emit()
        self.phase = None


def make_vecs(inp):
    v = np.zeros((128, 128), np.float32)
    for nm, r0 in KB.VROW.items():
        arr = np.asarray(inp[nm], np.float32).reshape(-1)
        n = arr.size // 128
        v[r0:r0 + n, :] = arr.reshape(n, 128)
    return v


WNAMES = ("ffn1_w_gate", "ffn1_w_up", "ffn1_w_down", "w_in", "w_out", "mem_w_q", "mem_w_kv", "mem_w_o",
          "ffn2_w_gate", "ffn2_w_up", "ffn2_w_down")


def make_in_maps(inp):
    vecs = make_vecs(inp)
    small = np.zeros((8, 4), np.float32)
    small[:, 0] = np.asarray(inp["fox_f_bias"], np.float32).reshape(-1)
    small[4:8, 1] = np.asarray(inp["gdn_a_log"], np.float32).reshape(-1)
    small[4:8, 2] = np.asarray(inp["gdn_dt_bias"], np.float32).reshape(-1)
    shared = {nm: np.ascontiguousarray(np.asarray(inp[nm], np.float32)[0]) for nm in WNAMES}
    maps = []
    for b in range(NCORES):
        m = dict(shared)
        m["x"] = np.ascontiguousarray(np.asarray(inp["x"], np.float32)[b][:S])
        m["mem"] = np.ascontiguousarray(np.asarray(inp["mem"], np.float32)[b])
        m["vecs"] = vecs
        m["small"] = small
        maps.append(m)
    return maps


def kernel(**inputs):
    kb = KB()
    nc = kb.build()
    res = run_bass_kernel_spmd(nc, make_in_maps(inputs), core_ids=list(range(NCORES)))
    return np.stack([np.asarray(r["out"], np.float32) for r in res.results], axis=0)
```

```python
import os
from contextlib import ExitStack

import numpy as np
import concourse.bass as bass
import concourse.mybir as mybir
from concourse.bass_utils import run_bass_kernel_spmd

F32 = mybir.dt.float32
BF16 = mybir.dt.bfloat16
ALU = mybir.AluOpType
AF = mybir.ActivationFunctionType

NCORES = 8
S = int(os.environ.get('KSEQ', '8192'))
D = 1024
DC = D // 128
DFF = 2816
FC = DFF // 128
NMEM = 256
EPS = 1e-6
WIN = 3600

ENGS = ("pe", "act", "dve", "pool", "sp")


class Op:
    __slots__ = ("eng", "fn", "reads", "writes", "idx", "deps", "marked", "dma", "semkey", "cost", "lat",
                 "alldeps", "pos", "rank")

    def __init__(self, eng, fn, reads, writes, dma=False, semkey=None, cost=100.0, lat=0.0):
        self.eng = eng
        self.fn = fn
        self.reads = reads
        self.writes = writes
        self.deps = []
        self.alldeps = []
        self.marked = False
        self.dma = dma
        self.semkey = semkey
        self.cost = cost
        self.lat = lat
        self.pos = -1
        self.rank = 0


class Sched:
    SEM_LAT = 250.0
    WINDOW = int(os.environ.get("KWIN", "256"))

    def __init__(self, nc, stack):
        self.nc = nc
        self.stack = stack
        self.ops = []
        self.last_writer = {}
        self.readers = {}
        self.dma_count = {}
        self.dma_sems = {}
        self.seg_start = 0
        self.barrier_idx = -1
        self.eng_sem = {e: stack.enter_context(nc.semaphore("s_" + e)) for e in ENGS}
        self.eng_count = {e: 0 for e in ENGS}
        self.waited = {e: {} for e in ENGS}
        self.opval = {}
        self.reorder = os.environ.get("KREORDER", "1") == "1"

    def add(self, eng, fn, reads=(), writes=(), dma=False, semkey=None, cost=100.0, lat=0.0):
        if dma:
            writes = tuple(writes) + (("semchain", semkey),)
        op = Op(eng, fn, tuple(reads), tuple(writes), dma, semkey, cost, lat)
        op.idx = len(self.ops)
        deps = {}
        for r in op.reads:
            w = self.last_writer.get(r)
            if w is not None:
                deps[w.idx] = w
        for k in op.writes:
            w = self.last_writer.get(k)
            if w is not None:
                deps[w.idx] = w
            for rd in self.readers.get(k, ()):
                deps[rd.idx] = rd
        for r in op.reads:
            self.readers.setdefault(r, []).append(op)
        for k in op.writes:
            self.last_writer[k] = op
            self.readers[k] = []
        op.alldeps = [d for d in deps.values() if d.idx > self.barrier_idx and d is not op]
        self.ops.append(op)
        if dma:
            if semkey not in self.dma_sems:
                self.dma_sems[semkey] = self.stack.enter_context(
                    self.nc.semaphore("d_%d" % len(self.dma_sems)))
            self.dma_count[semkey] = self.dma_count.get(semkey, 0) + 1
            op.rank = self.dma_count[semkey]
        return op

    def barrier(self):
        self.barrier_idx = len(self.ops) - 1

    def _schedule(self, seg, reorder=True):
        pend = {e: [o for o in seg if o.eng == e] for e in ENGS}
        if not (self.reorder and reorder):
            return pend
        order = {e: [] for e in ENGS}
        head = {e: 0 for e in ENGS}
        free = {e: 0.0 for e in ENGS}
        finish = {}
        done = set()
        nleft = len(seg)
        W = self.WINDOW if reorder == "gdn" else int(os.environ.get("KWIN2", "512"))
        LAT = self.SEM_LAT if reorder == "gdn" else float(os.environ.get("KLAT", "1000"))
        fixed = set(os.environ.get("KFIXED", "act,pe").split(",")) if reorder == "gdn" else set()
        while nleft:
            best = None
            bkey = None
            for e in ENGS:
                lst = pend[e]
                h = head[e]
                n = len(lst)
                while h < n and lst[h].idx in done:
                    h += 1
                head[e] = h
                fe = free[e]
                cnt = 0
                i = h
                We = 1 if e in fixed else W
                while i < n and cnt < We:
                    op = lst[i]
                    i += 1
                    if op.idx in done:
                        continue
                    cnt += 1
                    ready = fe
                    ok = True
                    for d in op.alldeps:
                        f = finish.get(d.idx)
                        if f is None:
                            ok = False
                            break
                        if d.dma or d.eng != e:
                            f += LAT
                        if f > ready:
                            ready = f
                    if not ok:
                        continue
                    key = (ready, op.idx)
                    if bkey is None or key < bkey:
                        bkey = key
                        best = op
                    if ready <= fe:
                        break
            op = best
            start = bkey[0]
            e = op.eng
            free[e] = start + op.cost
            finish[op.idx] = start + op.cost + op.lat
            done.add(op.idx)
            order[e].append(op)
            nleft -= 1
        self.sim_time = max(finish.values()) if finish else 0.0
        if os.environ.get("KVERB"):
            busy = {e: round(sum(o.cost for o in order[e]) / 1e3) for e in ENGS}
            print("segment ops", len(seg), "sim_us", round(self.sim_time / 1e3), "busy_us", busy, flush=True)
        return order

    def emit(self, reorder=True):
        nc = self.nc
        seg = self.ops[self.seg_start:]
        self.seg_start = len(self.ops)
        order = self._schedule(seg, reorder)
        for e in ENGS:
            for p, op in enumerate(order[e]):
                op.pos = p
        for op in seg:
            best = {}
            for d in op.alldeps:
                if d.dma:
                    key = ("d", d.semkey)
                    if key not in best or best[key].rank < d.rank:
                        best[key] = d
                else:
                    if d.eng == "pe" and op.eng == "pe" and not op.dma:
                        continue
                    key = ("e", d.eng)
                    if key not in best or best[key].pos < d.pos:
                        best[key] = d
            op.deps = list(best.values())
            for d in op.deps:
                if not d.dma:
                    d.marked = True
        for e in ENGS:
            for op in order[e]:
                if not op.dma and op.marked:
                    self.eng_count[e] += 1
                    self.opval[op.idx] = self.eng_count[e]
        seg_dma = {}
        for op in seg:
            if op.dma:
                seg_dma[op.semkey] = max(seg_dma.get(op.semkey, 0), op.rank)

        def run(engname, eng):
            waited = self.waited[engname]
            for op in order[engname]:
                for d in op.deps:
                    if d.dma:
                        sem = self.dma_sems[d.semkey]
                        val = 16 * d.rank
                        key = ("d", d.semkey)
                    else:
                        sem = self.eng_sem[d.eng]
                        val = self.opval[d.idx]
                        key = ("e", d.eng)
                    if waited.get(key, 0) < val:
                        eng.wait_ge(sem, val)
                        waited[key] = val
                ins = op.fn(eng)
                if op.dma:
                    ins.then_inc(self.dma_sems[op.semkey], 16)
                elif op.marked:
                    ins.then_inc(self.eng_sem[engname], 1)
            if engname == "sp":
                for semkey, rank in seg_dma.items():
                    key = ("d", semkey)
                    if waited.get(key, 0) < 16 * rank:
                        eng.wait_ge(self.dma_sems[semkey], 16 * rank)
                        waited[key] = 16 * rank

        with nc.Block() as block:
            @block.tensor
            def _(e):
                run("pe", e)

            @block.scalar
            def _(e):
                run("act", e)

            @block.vector
            def _(e):
                run("dve", e)

            @block.gpsimd
            def _(e):
                run("pool", e)

            @block.sync
            def _(e):
                run("sp", e)


class KB:
    def __init__(self, debug_stage=None):
        self.debug_stage = debug_stage
        self.nc = bass.Bass("TRN2", target_bir_lowering=False)
        self.top = ExitStack()
        self.sch = None
        self.phase = None
        self.nm = 0

    def sb(self, shape, dtype, stack=None, name=None):
        self.nm += 1
        st = stack if stack is not None else self.phase
        return st.enter_context(self.nc.sbuf_tensor(name or ("t%d" % self.nm), list(shape), dtype))

    def ps(self, shape, dtype, stack=None, name=None):
        self.nm += 1
        st = stack if stack is not None else self.phase
        return st.enter_context(self.nc.psum_tensor(name or ("p%d" % self.nm), list(shape), dtype))

    def dram(self, name, shape, dtype, kind="Internal"):
        return self.nc.dram_tensor(name, list(shape), dtype, kind=kind)

    @staticmethod
    def _fsz(ap):
        n = 1
        for d in ap.shape[1:]:
            n *= d
        return n

    @staticmethod
    def _psum(ap):
        return type(ap.tensor).__name__.startswith("PSum")

    def mm(self, out, lhsT, rhs, start, stop, r, w):
        n = max(64, self._fsz(rhs))
        c = n / 2.2 * (4.0 if rhs.dtype == F32 else 1.0) + 8
        self.sch.add("pe", lambda e: e.matmul(out, lhsT, rhs, start=start, stop=stop), r, w, cost=c, lat=60.0)

    def tr(self, out, in_, ident, r, w):
        self.sch.add("pe", lambda e: e.transpose(out, in_, ident), r, w, cost=70.0, lat=60.0)

    def _ecost(self, eng, out, ins, kind):
        n = self._fsz(out)
        ps = any(self._psum(a) for a in ins) or self._psum(out)
        if eng == "act":
            return (n + (180 if ps else 230)) / 1.2
        if eng == "pool":
            return (n + 160) / 1.0
        acc = 1.0
        if kind == "single" and not ps:
            acc = 4.0 if all(a.dtype == BF16 for a in ins) and out.dtype == BF16 else 2.0
        elif kind == "tt" and not ps and all(a.dtype == BF16 for a in ins):
            acc = 2.0
        return (n / acc + (120 if ps else 60)) / 0.96

    def act(self, out, in_, func, r, w, bias=None, scale=None, eng="act"):
        kw = {}
        if bias is not None:
            kw["bias"] = bias
        if scale is not None:
            kw["scale"] = scale
        self.sch.add(eng, lambda e: e.activation(out, in_, func, **kw), r, w, cost=self._ecost(eng, out, (in_,), "act"))

    def copy(self, eng, out, in_, r, w):
        c = self._ecost(eng, out, (in_,), "single")
        if eng == "act":
            self.sch.add("act", lambda e: e.copy(out, in_), r, w, cost=c)
        else:
            self.sch.add(eng, lambda e: e.tensor_copy(out, in_), r, w, cost=c)

    def tt(self, eng, out, a, b, op, r, w):
        self.sch.add(eng, lambda e: e.tensor_tensor(out, a, b, op), r, w, cost=self._ecost(eng, out, (a, b), "tt"))

    def ts(self, eng, out, a, s1, s2, op0, op1, r, w):
        c = self._ecost(eng, out, (a,), "single")
        if op1 is None:
            self.sch.add(eng, lambda e: e.tensor_scalar(out, a, s1, None, op0), r, w, cost=c)
        else:
            self.sch.add(eng, lambda e: e.tensor_scalar(out, a, s1, s2, op0, op1), r, w, cost=c)

    def stt(self, eng, out, a, s, b, op0, op1, r, w):
        self.sch.add(eng, lambda e: e.scalar_tensor_tensor(out, a, s, b, op0, op1), r, w,
                     cost=self._ecost(eng, out, (a, b), "tt"))

    def dma(self, eng, out, in_, r, w, semkey):
        nbytes = 1
        for d in out.shape:
            nbytes *= d
        nbytes *= 2 if out.dtype == BF16 else 4
        issue = 1000.0 if eng == "pool" else 100.0
        self.sch.add(eng, lambda e: e.dma_start(out, in_), r, w, dma=True, semkey=semkey,
                     cost=issue, lat=2000.0 + nbytes / 150.0)

    def build(self):
        nc = self.nc
        with self.top:
            with nc.allow_low_precision("bf16 matmul operands, fp32 accumulation"):
                self._build()
        return nc

    def _build(self):
        nc = self.nc
        top = self.top
        self.sch = Sched(nc, top)
        dbg = self.debug_stage
        X = self.dram("x", [S, D], F32, "ExternalInput")
        MEM = self.dram("mem", [NMEM, D], F32, "ExternalInput")
        VECS = self.dram("vecs", [128, 128], F32, "ExternalInput")
        self.W = {}
        for nm, shp in (("ffn1_w_gate", [D, DFF]), ("ffn1_w_up", [D, DFF]), ("ffn1_w_down", [DFF, D]),
                        ("w_in", [D, WIN]), ("w_out", [D, D]), ("mem_w_q", [D, D]), ("mem_w_kv", [D, 2 * D]),
                        ("mem_w_o", [D, D]),
                        ("ffn2_w_gate", [D, DFF]), ("ffn2_w_up", [D, DFF]), ("ffn2_w_down", [DFF, D])):
            self.W[nm] = self.dram(nm, shp, F32, "ExternalInput")
        kind_dbg = "ExternalOutput" if dbg else "Internal"
        self.H1T = self.dram("h1t", [D, S], F32, "ExternalOutput" if dbg in (1, 5) else "Internal")
        self.UNT = self.dram("unt", [D, S], BF16, kind_dbg if dbg == 1 else "Internal")
        OUT = self.dram("out", [S, D], F32, "ExternalOutput")
        SMALL = self.dram("small", [8, 4], F32, "ExternalInput")
        self.SMALL = SMALL
        dk = lambda st: "ExternalOutput" if dbg == st else "Internal"
        self.QA = self.dram("qa", [8, 70, S], BF16, dk(2))
        self.KA = self.dram("ka", [8, 70, S], BF16, dk(2))
        self.VA = self.dram("va", [S, 8, 128], BF16, dk(2))
        self.MF = self.dram("mf", [8, 64, S], BF16, dk(3))
        self.MG = self.dram("mg", [4, 128, S], BF16, dk(4))
        self.GQT = self.dram("gqt", [4, 128, S], F32, dk(2))
        self.GKT = self.dram("gkt", [4, 128, S], F32, dk(2))
        self.GVT = self.dram("gvt", [4, 128, S], F32, dk(2))
        self.ZT = self.dram("zt", [4, 128, S], BF16, dk(2))
        self.BGT = self.dram("bgt", [8, S], F32, dk(2))
        self.X, self.MEM, self.VECS, self.OUT = X, MEM, VECS, OUT

        self.ident = self.sb([128, 128], F32, stack=top, name="ident")
        self.ones_bf = self.sb([128, 128], BF16, stack=top, name="ones_bf")
        self.vec = self.sb([128, 128], F32, stack=top, name="vecT")
        self.setup_consts()
        self.phase_ffn(1)
        if dbg == 1:
            return
        self.phase_proj()
        if dbg == 2:
            return
        self.phase_fox()
        if dbg == 3:
            return
        self.phase_gdn()
        if dbg == 4:
            return
        self.phase_mix_mem()
        if dbg == 5:
            return
        self.phase_ffn(2)

    VROW = {"ffn1_pre_norm": 0, "ffn1_post_norm": 8, "mix_pre_norm": 16, "mix_post_norm": 24,
            "mem_pre_norm": 32, "mem_kv_norm": 40, "mem_post_norm": 48, "ffn2_pre_norm": 56,
            "ffn2_post_norm": 64, "gdn_conv_w": 72, "gdn_out_norm": 120}

    def setup_consts(self):
        nc = self.nc
        with ExitStack() as ph:
            self.phase = ph
            ones = self.sb([128, 128], F32)
            vs = self.sb([128, 128], F32)
            pt = self.ps([128, 128], F32)
            ident, ones_bf, vec = self.ident, self.ones_bf, self.vec
            self.sch.add("pool", lambda e: e.memset(ones[:], 1.0), (), ("ones",))
            self.sch.add("pool", lambda e: e.affine_select(ident[:], ones[:], [[-1, 128]], ALU.is_equal, 0.0,
                                                           base=0, channel_multiplier=1),
                         ("ones",), ("ident",))
            self.sch.add("pool", lambda e: e.memset(ones_bf[:], 1.0 / 1024.0), (), ("ones_bf",))
            self.dma("sp", vs[:], self.VECS.ap(), (), ("vs",), "vs")
            self.tr(pt[:], vs[:], ident[:], ("vs", "ident"), ("pt",))
            self.copy("dve", vec[:], pt[:], ("pt",), ("vec",))
            for nm in ("ffn1_post_norm", "ffn2_post_norm"):
                r0 = self.VROW[nm]
                self.sch.add("dve", lambda e, r0=r0: e.tensor_scalar(vec[:, r0:r0 + 8], vec[:, r0:r0 + 8], 0.5, None,
                                                                     ALU.mult), ("vec",), ("vec",))
            self.sch.barrier()
            self.sch.emit()
        self.phase = None

    def load_weight(self, wt, dram, kc, ncols, key, col0=0, row0=0, nsplit=None):
        src = dram.ap()[row0:row0 + kc * 128, col0:col0 + ncols].rearrange("(k p) n -> p k n", p=128)
        keys = []
        for k in range(kc):
            c0 = 0
            while c0 < ncols:
                c1 = min(ncols, c0 + 2816)
                self.dma("pool", wt[:, k, c0:c1], src[:, k, c0:c1], (), ((key, k),), (key, k % 3))
                c0 = c1
            keys.append((key, k))
        return keys

    def rms_stats(self, src, sq, st_ps, rstd, keys_src, key_sq, key_st, key_rstd, T, sq_eng="dve"):
        self.tt(sq_eng, sq[:], src[:], src[:], ALU.mult, keys_src, (key_sq,))
        for c in range(DC):
            self.mm(st_ps, self.ones_bf[:], sq[:, c, :], c == 0, c == DC - 1, (key_sq, "ones_bf"), (key_st,))
        self.act(rstd[:], st_ps, AF.Ln, (key_st,), (key_rstd,), bias=EPS)
        self.act(rstd[:], rstd[:], AF.Exp, (key_rstd,), (key_rstd,), scale=-0.5)

    def phase_ffn(self, which):
        nc = self.nc
        T = 256
        NT = S // T
        pre = "ffn%d_" % which
        with ExitStack() as ph:
            self.phase = ph
            wg = self.sb([128, DC, DFF], BF16)
            wu = self.sb([128, DC, DFF], BF16)
            wd = self.sb([128, FC, D], BF16)
            kg = self.load_weight(wg, self.W[pre + "w_gate"], DC, DFF, "wg")
            ku = self.load_weight(wu, self.W[pre + "w_up"], DC, DFF, "wu")
            kd = self.load_weight(wd, self.W[pre + "w_down"], FC, D, "wd")
            xs = self.sb([128, 2, D], F32)
            hT = [self.sb([128, DC, T], F32) for _ in range(2)]
            xn = [self.sb([128, DC, T], BF16) for _ in range(2)]
            sq = self.sb([128, DC, T], BF16)
            rstd = [self.sb([128, T], F32) for _ in range(2)]
            sg = [self.sb([128, T], F32) for _ in range(2)]
            a = self.sb([128, FC, T], BF16)
            f = self.sb([128, DC, T], F32)
            un = self.sb([128, DC, T], BF16)
            tp = [self.ps([128, 512], F32) for _ in range(1)]
            stp = self.ps([128, 512], F32)
            gp = [self.ps([128, 512], F32) for _ in range(2)]
            up = [self.ps([128, 512], F32) for _ in range(2)]
            dn = [self.ps([128, 512], F32) for _ in range(2)]
            vec = self.vec
            g1 = self.VROW[pre + "pre_norm"]
            g2 = self.VROW[pre + "post_norm"]
            g3 = self.VROW["mix_pre_norm"]
            ident = self.ident
            X = self.X.ap()
            H1T = self.H1T.ap().rearrange("(c p) t -> p c t", p=128)
            UNT = self.UNT.ap().rearrange("(c p) t -> p c t", p=128)

            def stage_a(i):
                sl = i % 2
                t0 = i * T
                if which == 2:
                    self.dma("sp", hT[sl][:], H1T[:, :, t0:t0 + T], (("H1T", i),),
                             tuple(("hT", sl, cp) for cp in range(4)), ("hTl", sl))
                else:
                    self.dma("sp", xs[:], X[t0:t0 + T, :].rearrange("(s p) d -> p s d", p=128), (), ("xs",), "xs")
                    for cp in range(4):
                        bank = tp[0]
                        bk = ("tp", 0)
                        for cc in range(2):
                            c = 2 * cp + cc
                            for s in range(2):
                                self.tr(bank[:, cc * 256 + s * 128: cc * 256 + (s + 1) * 128],
                                        xs[:, s, c * 128:(c + 1) * 128], ident[:], ("xs", "ident"), (bk,))
                        self.copy("act", hT[sl][:, 2 * cp:2 * cp + 2, :], bank[:], (bk,), (("hT", sl, cp),))
                hk = tuple(("hT", sl, cp) for cp in range(4))
                self.rms_stats(hT[sl], sq, stp[:, 0:T], rstd[0], hk, "sq", "st", ("rstd", 0), T)
                for c in range(DC):
                    self.stt("dve", xn[sl][:, c, :], hT[sl][:, c, :], vec[:, g1 + c:g1 + c + 1], rstd[0][:],
                             ALU.mult, ALU.mult, (("hT", sl, c // 2), ("rstd", 0), "vec"), (("xn", sl, c),))

            def stage_gu(i):
                sl = i % 2
                xk = tuple(("xn", sl, c) for c in range(DC))
                ksub = int(os.environ.get("KSUB", "9"))
                for ft in range(int(os.environ.get("KFT", FC))):
                    b = ft % 2
                    gbank = gp[b]
                    ubank = up[b]
                    for k in range(DC):
                        self.mm(gbank[:, 0:T], wg[:, k, ft * 128:(ft + 1) * 128], xn[sl][:, k, :],
                                k == 0, k == DC - 1, (("xn", sl, k), kg[k]), (("gp", b),))
                    for k in range(DC):
                        self.mm(ubank[:, 0:T], wu[:, k, ft * 128:(ft + 1) * 128], xn[sl][:, k, :],
                                k == 0, k == DC - 1, (("xn", sl, k), ku[k]), (("up", b),))
                    s2 = ft % int(os.environ.get('KSG', '2'))
                    if ksub < 1:
                        continue
                    self.act(sg[s2][:], gbank[:, 0:T], AF.Exp, (("gp", b),), (("sg", s2),), scale=-1.0)
                    self.act(sg[s2][:], sg[s2][:], AF.Ln, (("sg", s2),), (("sg", s2),), bias=1.0)
                    self.act(sg[s2][:], sg[s2][:], AF.Exp, (("sg", s2),), (("sg", s2),), scale=-1.0)
                    if ksub < 2:
                        continue
                    self.tt("dve", sg[s2][:], sg[s2][:], gbank[:, 0:T], ALU.mult,
                            (("sg", s2), ("gp", b)), (("sg", s2),))
                    self.tt("dve", a[:, ft, :], sg[s2][:], ubank[:, 0:T], ALU.mult,
                            (("sg", s2), ("up", b)), (("a", ft),))

            def stage_d(i):
                ak = tuple(("a", ft) for ft in range(FC))
                for dp in range(4):
                    bank = dn[dp % 2]
                    bk = ("dn", dp % 2)
                    for half in range(2):
                        dt = 2 * dp + half
                        for k in range(FC):
                            self.mm(bank[:, half * T:(half + 1) * T], wd[:, k, dt * 128:(dt + 1) * 128], a[:, k, :],
                                    k == 0, k == FC - 1, (("a", k), kd[k]), (bk,))
                    self.copy("act", f[:, 2 * dp:2 * dp + 2, :], bank[:], (bk,), (("f", dp),))

            def stage_post(i):
                sl = i % 2
                t0 = i * T
                fk = tuple(("f", dp) for dp in range(4))
                hk = tuple(("hT", sl, cp) for cp in range(4))
                self.rms_stats(f, sq, stp[:, 0:T], rstd[1], fk, "sq", "st", ("rstd", 1), T)
                for c in range(DC):
                    self.stt("dve", f[:, c, :], f[:, c, :], vec[:, g2 + c:g2 + c + 1], rstd[1][:],
                             ALU.mult, ALU.mult, (("f", c // 2), ("rstd", 1), "vec"), (("f", c // 2),))
                self.tt("dve", hT[sl][:], hT[sl][:], f[:], ALU.add, fk + hk, hk)
                if which == 2:
                    OUTA = self.OUT.ap()
                    for s_ in range(2):
                        for half in range(2):
                            for cc in range(4):
                                c = 4 * half + cc
                                self.tr(tp[0][:, cc * 128:(cc + 1) * 128], hT[sl][:, c, s_ * 128:(s_ + 1) * 128], ident[:],
                                        hk + ("ident",), (("tp", 0),))
                            self.copy("act", xs[:, s_, half * 512:(half + 1) * 512], tp[0][:], (("tp", 0),), ("xs",))
                    self.dma("pool", OUTA[t0:t0 + T, :].rearrange("(s p) d -> p s d", p=128), xs[:], ("xs",),
                             (("OUT", i),), "st_out")
                    return
                self.dma("pool", H1T[:, :, t0:t0 + T], hT[sl][:], hk, (("H1T", i),), ("st_h", sl))
                if which == 1:
                    self.rms_stats(hT[sl], sq, stp[:, 0:T], rstd[1], hk, "sq", "st", ("rstd", 1), T)
                    for c in range(DC):
                        self.stt("dve", un[:, c, :], hT[sl][:, c, :], vec[:, g3 + c:g3 + c + 1], rstd[1][:],
                                 ALU.mult, ALU.mult, (("hT", sl, c // 2), ("rstd", 1), "vec"), ("un",))
                    self.dma("pool", UNT[:, :, t0:t0 + T], un[:], ("un",), (("UNT", i),), "st_un")

            nt = NT if not self.debug_stage or os.environ.get("KFULL") else int(os.environ.get("KNT", "2"))
            self.nt_dbg = nt
            kstop = int(os.environ.get("KSTOP", "99"))
            if kstop >= 1:
                stage_a(0)
            for i in range(nt):
                if kstop >= 2:
                    stage_gu(i)
                if i + 1 < nt and kstop >= 1:
                    stage_a(i + 1)
                if kstop >= 3:
                    stage_d(i)
                if kstop >= 4:
                    stage_post(i)
            self.sch.barrier()
            self.sch.emit()
        self.phase = None


    def phase_proj(self):
        T = 512
        NT = S // T
        with ExitStack() as ph:
            self.phase = ph
            win = self.sb([128, DC, WIN], BF16)
            kw = self.load_weight(win, self.W["w_in"], DC, WIN, "win")
            un = [self.sb([128, DC, T], BF16) for _ in range(2)]
            qst = [self.sb([128, T], BF16) for _ in range(2)]
            vst = self.sb([128, 4, 8, 128], BF16)
            small = self.sb([8, 4], F32)
            fe = self.sb([8, T], F32)
            G = [self.sb([8, T], F32) for _ in range(2)]
            g8 = self.sb([8, T], F32)
            r1 = self.sb([8, T], F32)
            gk3 = self.sb([8, 3, T], BF16)
            ng3 = self.sb([8, 3, T], BF16)
            ones3 = self.sb([8, 3, T], BF16)
            onesf = self.sb([8, T], F32)
            pq = [self.ps([128, 512], F32) for _ in range(4)]
            pv = [self.ps([128, 512], F32) for _ in range(2)]
            pf = self.ps([128, 512], F32)
            pn = self.ps([128, 512], F32)
            cb = [self.sb([128, T + 3], F32) for _ in range(12)]
            acc = [self.sb([128, T], F32) for _ in range(2)]
            sgm = [self.sb([128, T], F32) for _ in range(2)]
            sqb = self.sb([128, T], BF16)
            rs = self.sb([128, T], F32)
            yst = [self.sb([128, T], F32) for _ in range(2)]
            zst = [self.sb([128, T], BF16) for _ in range(2)]
            ge = self.sb([8, T], F32)
            gr = self.sb([8, T], F32)
            gg = self.sb([8, T], F32)
            onec = self.sb([8, 1], F32)
            vec = self.vec
            cw0 = self.VROW["gdn_conv_w"]
            GQT, GKT, GVT, ZT, BGT = self.GQT.ap(), self.GKT.ap(), self.GVT.ap(), self.ZT.ap(), self.BGT.ap()
            for j in range(12):
                self.sch.add("pool", lambda e, j=j: e.memset(cb[j][:, 0:3], 0.0), (), (("cb", j),))
            UNT = self.UNT.ap().rearrange("(c p) t -> p c t", p=128)
            QA, KA, VA = self.QA.ap(), self.KA.ap(), self.VA.ap()
            self.sch.add("pool", lambda e: e.memset(vst[:], 1.0), (), tuple(("vst", s_) for s_ in range(4)))
            self.sch.add("pool", lambda e: e.memset(ones3[:], 1.0), (), ("ones3",))
            self.sch.add("pool", lambda e: e.memset(onesf[:], 1.0), (), ("onesf",))
            self.dma("sp", small[:], self.SMALL.ap(), (), ("small",), "small")
            self.ts("dve", small[:, 3:4], small[:, 0:1], -1.0, None, ALU.mult, None, ("small",), ("small",))
            self.sch.add("pool", lambda e: e.memset(onec[:], 1.0), (), ("onec",))
            gsc = self.sb([8, 2], F32)
            self.sch.add("pool", lambda e: e.affine_select(gsc[:, 0:1], onec[:], [[0, 1]], ALU.is_ge, -1.0,
                                                           base=-4, channel_multiplier=1), ("onec",), ("gsc",))
            self.act(gsc[:, 1:2], small[:, 1:2], AF.Exp, ("small", "gsc"), ("gsc",))
            self.ts("dve", gsc[:, 1:2], gsc[:, 1:2], -1.0, None, ALU.mult, None, ("gsc",), ("gsc",))
            ycnt = 0
            for i in range(NT):
                sl = i % 2
                t0 = i * T
                self.dma("sp", un[sl][:], UNT[:, :, t0:t0 + T], (("UNT", i),), (("un", sl),), ("un", sl))
                uk = ("un", sl)
                for g in range(8):
                    bank = pq[g % 4]
                    bk = ("pq", g % 4)
                    for k in range(DC):
                        self.mm(bank[:, 0:T], win[:, k, 128 * g:128 * g + 128], un[sl][:, k, :],
                                k == 0, k == DC - 1, (uk, kw[k]), (bk,))
                    self.copy("act", qst[g % 2][:], bank[:, 0:T], (bk,), (("qst", g % 2),))
                    dst = QA if g < 4 else KA
                    h0 = 2 * (g % 4)
                    for hh in range(2):
                        self.dma("pool", dst[h0 + hh, 0:64, t0:t0 + T], qst[g % 2][64 * hh:64 * hh + 64, :],
                                 (("qst", g % 2),), (("QK", g, hh, i),), ("qst", g % 2))
                for s_ in range(4):
                    bank = pv[s_ % 2]
                    bk = ("pv", s_ % 2)
                    for k in range(DC):
                        self.mm(bank[:, 0:512], un[sl][:, k, s_ * 128:(s_ + 1) * 128], win[:, k, 1024:1536],
                                k == 0, k == DC - 1, (uk, kw[k]), (bk,))
                    self.copy("dve", vst[:, s_, :, 0:64], bank[:, 0:512].rearrange("p (h c) -> p h c", c=64),
                              (bk,), (("vst", s_),))
                self.dma("pool", VA[t0:t0 + T].rearrange("(s p) h c -> p s h c", p=128), vst[:],
                         tuple(("vst", s_) for s_ in range(4)), (("VA", i),), "vst")
                for k in range(DC):
                    self.mm(pf[0:8, 0:T], win[:, k, 1536:1544], un[sl][:, k, :], k == 0, k == DC - 1, (uk, kw[k]), ("pf",))
                self.act(fe[:], pf[0:8, 0:T], AF.Exp, ("pf", "small"), ("fe",), bias=small[:, 3:4], scale=-1.0)
                self.act(fe[:], fe[:], AF.Ln, ("fe",), ("fe",), bias=1.0)
                init = 0.0 if i == 0 else G[1 - sl][:, T - 1:T]
                self.sch.add("dve", lambda e, sl=sl, init=init: e.tensor_tensor_scan(G[sl][:], onesf[:], fe[:], init,
                                                                                   ALU.mult, ALU.add),
                             ("fe", "onesf", ("G", 1 - sl)), (("G", sl),), cost=1100.0)
                self.ts("dve", g8[:], G[sl][:], 8.0, None, ALU.mult, None, (("G", sl),), ("g8",))
                self.copy("dve", gk3[:, 0, :], g8[:], ("g8",), ("gk3",))
                self.tt("dve", r1[:], g8[:], gk3[:, 0, :], ALU.subtract, ("g8", "gk3"), ("r1",))
                self.copy("dve", gk3[:, 1, :], r1[:], ("r1",), ("gk3",))
                self.tt("dve", r1[:], r1[:], gk3[:, 1, :], ALU.subtract, ("r1", "gk3"), ("r1",))
                self.copy("dve", gk3[:, 2, :], r1[:], ("r1",), ("gk3",))
                self.ts("dve", ng3[:], gk3[:], -1.0, None, ALU.mult, None, ("gk3",), ("ng3",))
                self.dma("pool", KA[:, 64:67, t0:t0 + T], gk3[:], ("gk3",), (("KAg", i),), "gk3")
                self.dma("pool", KA[:, 67:70, t0:t0 + T], ones3[:], ("ones3",), (("KAo", i),), "ones3")
                self.dma("pool", QA[:, 64:67, t0:t0 + T], ones3[:], ("ones3",), (("QAo", i),), "ones3")
                self.dma("pool", QA[:, 67:70, t0:t0 + T], ng3[:], ("ng3",), (("QAg", i),), "ng3")
                for j in range(16):
                    bank = pq[j % 4]
                    bk = ("pq", j % 4)
                    c0 = 1544 + 128 * j
                    for k in range(DC):
                        self.mm(bank[:, 0:T], win[:, k, c0:c0 + 128], un[sl][:, k, :], k == 0, k == DC - 1, (uk, kw[k]), (bk,))
                    hh = j % 4
                    if j >= 12:
                        zs = ycnt % 2
                        self.copy("act", zst[zs][:], bank[:, 0:T], (bk,), (("zst", zs),))
                        self.dma("pool", ZT[hh, :, t0:t0 + T], zst[zs][:], (("zst", zs),), (("ZT", hh, i),), ("zst", zs))
                        ycnt += 1
                        continue
                    ck = ("cb", j)
                    self.copy("act", cb[j][:, 3:3 + T], bank[:, 0:T], (bk,), (ck,))
                    a_ = ycnt % 2
                    ak = ("acc", a_)
                    self.ts("dve", acc[a_][:], cb[j][:, 0:T], vec[:, cw0 + j:cw0 + j + 1], None, ALU.mult, None,
                            (ck, "vec"), (ak,))
                    for tap in range(1, 4):
                        self.stt("dve", acc[a_][:], cb[j][:, tap:tap + T], vec[:, cw0 + 12 * tap + j:cw0 + 12 * tap + j + 1],
                                 acc[a_][:], ALU.mult, ALU.add, (ck, ak, "vec"), (ak,))
                    self.copy("dve", cb[j][:, 0:3], cb[j][:, T:T + 3], (ck,), (ck,))
                    sk = ("sgm", a_)
                    self.act(sgm[a_][:], acc[a_][:], AF.Exp, (ak,), (sk,), scale=-1.0)
                    self.act(sgm[a_][:], sgm[a_][:], AF.Ln, (sk,), (sk,), bias=1.0)
                    self.act(sgm[a_][:], sgm[a_][:], AF.Exp, (sk,), (sk,), scale=-1.0)
                    yk = ("yst", a_)
                    self.tt("dve", yst[a_][:], acc[a_][:], sgm[a_][:], ALU.mult, (ak, sk), (yk,))
                    if j < 8:
                        self.tt("dve", sqb[:], yst[a_][:], yst[a_][:], ALU.mult, (yk,), ("sqb",))
                        self.mm(pn[:, 0:T], self.ones_bf[:], sqb[:], True, True, ("sqb", "ones_bf"), ("pn",))
                        self.act(rs[:], pn[:, 0:T], AF.Ln, ("pn",), ("rs",), bias=EPS, scale=1024.0)
                        self.act(rs[:], rs[:], AF.Exp, ("rs",), ("rs",), scale=-0.5,
                                 bias=(-0.5 * float(np.log(128.0)) if j < 4 else 0.0))
                        self.tt("dve", yst[a_][:], yst[a_][:], rs[:], ALU.mult, (yk, "rs"), (yk,))
                    dstT = (GQT, GKT, GVT)[j // 4]
                    self.dma("pool", dstT[hh, :, t0:t0 + T], yst[a_][:], (yk,), (("G3", j, i),), ("yst", a_))
                    ycnt += 1
                for k in range(DC):
                    self.mm(pf[0:8, 0:T], win[:, k, 3592:3600], un[sl][:, k, :], k == 0, k == DC - 1, (uk, kw[k]), ("pf",))
                self.act(ge[:], pf[0:8, 0:T], AF.Exp, ("pf", "gsc", "small"), ("ge",), bias=small[:, 2:3], scale=gsc[:, 0:1])
                self.act(ge[:], ge[:], AF.Ln, ("ge",), ("ge",), bias=1.0)
                self.act(gr[:], ge[:], AF.Exp, ("ge",), ("gr",), scale=-1.0)
                self.ts("dve", gg[:], ge[:], gsc[:, 1:2], None, ALU.mult, None, ("ge", "gsc"), ("gg",))
                self.dma("pool", BGT[0:4, t0:t0 + T], gr[0:4, :], ("gr",), (("BGb", i),), "gr")
                self.dma("pool", BGT[4:8, t0:t0 + T], gg[4:8, :], ("gg",), (("BGg", i),), "gg")
            self.sch.barrier()
            self.sch.emit()
        self.phase = None

    def phase_fox(self):
        T = 512
        NT = S // T
        NS = S // 128
        with ExitStack() as ph:
            self.phase = ph
            ka = [self.sb([70, S], BF16) for _ in range(2)]
            va = [self.sb([128, NS, 128], BF16) for _ in range(2)]
            qa = [self.sb([70, T], BF16) for _ in range(2)]
            pT = [self.sb([128, T], BF16) for _ in range(3)]
            rec = self.sb([128, T], F32)
            osb = self.sb([64, T], F32)
            mfst = [self.sb([64, T], BF16) for _ in range(2)]
            onesq = self.sb([128, 64], F32)
            shf = self.sb([128, 64], F32)
            sc = [self.ps([128, 512], F32) for _ in range(3)]
            ob = [self.ps([128, 512], F32) for _ in range(2)]
            rb = self.ps([128, 512], F32)
            QA, KA, VA, MF = self.QA.ap(), self.KA.ap(), self.VA.ap(), self.MF.ap()
            self.sch.add("pool", lambda e: e.memset(onesq[:], 1.0), (), ("onesq",))
            self.sch.add("pool", lambda e: e.affine_select(shf[:], onesq[:], [[-1, 64]], ALU.is_equal, 0.0,
                                                           base=-64, channel_multiplier=1), ("onesq",), ("shf",))
            LOOK = 2
            tiles = [(h, i) for h in range(8) for i in range(NT)]

            def load_head(h):
                hs = h % 2
                self.dma("sp", ka[hs][:], KA[h], (), (("ka", hs),), ("ka", hs))
                self.dma("sp", va[hs][:], VA[:, h, :].rearrange("(j p) c -> p j c", p=128), (), (("va", hs),), ("va", hs))

            def load_q(tix):
                h, i = tiles[tix]
                qs = tix % 2
                self.dma("sp", qa[qs][:], QA[h, :, i * T:(i + 1) * T], (), (("qa", qs),), ("qa", qs))

            items = []
            for tix, (h, i) in enumerate(tiles):
                nj = 4 * i + 4
                for j in range(nj):
                    items.append((tix, h, i, j, nj))

            def front(k):
                tix, h, i, j, nj = items[k]
                hs, qs = h % 2, tix % 2
                if j == 0 and tix + 1 < len(tiles):
                    load_q(tix + 1)
                jj = j - 4 * i
                c0 = 128 * jj if jj > 0 else 0
                b = k % 3
                self.mm(sc[b][:, c0:T], ka[hs][:, j * 128:(j + 1) * 128], qa[qs][:, c0:T], True, True,
                        (("ka", hs), ("qa", qs)), (("sc", b),))
                self.act(pT[b][:, c0:T], sc[b][:, c0:T], AF.Exp, (("sc", b),), (("pT", b),), scale=0.125)
                if jj >= 0:
                    self.sch.add("pool", lambda e, b=b, c0=c0: e.affine_select(
                        pT[b][:, c0:c0 + 128], pT[b][:, c0:c0 + 128], [[1, 128]], ALU.is_ge, 0.0,
                        base=0, channel_multiplier=-1), (("pT", b),), (("pT", b),), cost=300.0)

            def back(k):
                tix, h, i, j, nj = items[k]
                hs = h % 2
                jj = j - 4 * i
                c0 = 128 * jj if jj > 0 else 0
                b = k % 3
                o = ob[tix % 2]
                okey = ("ob", tix % 2)
                if j == 0 and i == 0 and h + 1 < 8:
                    load_head(h + 1)
                self.mm(o[:, c0:T], va[hs][:, j, :], pT[b][:, c0:T], j == 0, j == nj - 1,
                        (("va", hs), ("pT", b)), (okey,))
                if j == nj - 1:
                    t0 = i * T
                    self.sch.add("dve", lambda e, o=o: e.reciprocal(rec[64:128, :], o[64:128, 0:T]), (okey,), ("rec",), cost=660.0)
                    self.mm(rb[0:64, 0:T], shf[64:128, :], rec[64:128, :], True, True, ("rec", "shf"), ("rb",))
                    self.copy("act", osb[:], o[0:64, 0:T], (okey,), ("osb",))
                    ms = tix % 2
                    self.tt("dve", mfst[ms][:], osb[:], rb[0:64, 0:T], ALU.mult, ("osb", "rb"), (("mfst", ms),))
                    self.dma("pool", MF[h, :, t0:t0 + T], mfst[ms][:], (("mfst", ms),), (("MF", h, i),), ("mfst", ms))

            load_head(0)
            load_q(0)
            n_items = len(items)
            for k in range(min(LOOK, n_items)):
                front(k)
            for k in range(n_items):
                if k + LOOK < n_items:
                    front(k + LOOK)
                back(k)
            self.sch.barrier()
            self.sch.emit()
        self.phase = None


    def phase_mix_mem(self):
        T = 256
        NT = S // T
        MH = 4
        with ExitStack() as ph:
            self.phase = ph
            vec = self.vec
            ident = self.ident
            wof = self.sb([64, 8, D], BF16)
            wog = self.sb([128, 4, D], BF16)
            wq = self.sb([128, DC, D], BF16)
            wo = self.sb([128, DC, D], BF16)
            kmT = self.sb([128, DC, NMEM], BF16)
            vm = self.sb([128, 2, D], BF16)
            ones1 = self.sb([128, 128], BF16)
            self.sch.add("pool", lambda e: e.memset(ones1[:], 1.0), (), ("ones1",))
            WO = self.W["w_out"].ap()
            for h in range(8):
                self.dma("pool", wof[:, h, :], WO[64 * h:64 * h + 64, :], (), (("wof", h),), "wof")
            kof = [("wof", h) for h in range(8)]
            kog = self.load_weight(wog, self.W["w_out"], 4, D, "wog", row0=512)
            kq = self.load_weight(wq, self.W["mem_w_q"], DC, D, "wq")
            ko = self.load_weight(wo, self.W["mem_w_o"], DC, D, "wo")
            hT = [self.sb([128, DC, T], F32) for _ in range(2)]
            mf = [self.sb([64, 8, T], BF16) for _ in range(2)]
            mg = [self.sb([128, 4, T], BF16) for _ in range(2)]
            f_ = [self.sb([128, DC, T], F32) for _ in range(2)]
            sq_ = [self.sb([128, DC, T], BF16) for _ in range(2)]
            rstd_ = [self.sb([128, T], F32) for _ in range(2)]
            hq_ = [self.sb([128, DC, T], BF16) for _ in range(2)]
            qT_ = [self.sb([128, DC, T], BF16) for _ in range(2)]
            pT = [self.sb([128, T], BF16) for _ in range(4)]
            rec_ = [self.sb([128, T], F32) for _ in range(2)]
            oT_ = [self.sb([128, DC, T], BF16) for _ in range(2)]
            sq, rstd = sq_[0], rstd_[0]
            pa = [self.ps([128, 512], F32) for _ in range(2)]
            stp = self.ps([128, 512], F32)
            scp = [self.ps([128, 512], F32) for _ in range(2)]
            ssp = self.ps([128, 512], F32)
            op_ = [self.ps([128, 512], F32) for _ in range(2)]
            gpo = self.VROW["mix_post_norm"]
            gmq = self.VROW["mem_pre_norm"]
            gkv = self.VROW["mem_kv_norm"]
            gmo = self.VROW["mem_post_norm"]
            with ExitStack() as ph2:
                self.phase = ph2
                wkv = self.sb([128, DC, 2 * D], BF16)
                kkv = self.load_weight(wkv, self.W["mem_w_kv"], DC, 2 * D, "wkv")
                ms = self.sb([128, 2, D], F32)
                mT = self.sb([128, DC, NMEM], F32)
                mn = self.sb([128, DC, NMEM], BF16)
                self.dma("sp", ms[:], self.MEM.ap().rearrange("(s p) d -> p s d", p=128), (), ("ms",), "ms")
                for cp in range(4):
                    for cc in range(2):
                        c = 2 * cp + cc
                        for s_ in range(2):
                            self.tr(pa[0][:, cc * 256 + s_ * 128: cc * 256 + (s_ + 1) * 128],
                                    ms[:, s_, c * 128:(c + 1) * 128], ident[:], ("ms", "ident"), (("pa", 0),))
                    self.copy("act", mT[:, 2 * cp:2 * cp + 2, :], pa[0][:], (("pa", 0),), ("mT",))
                self.rms_stats(mT, sq, stp[:, 0:T], rstd, ("mT",), "sq", "st", "rstd", T)
                for c in range(DC):
                    self.stt("dve", mn[:, c, :], mT[:, c, :], vec[:, gkv + c:gkv + c + 1], rstd[:],
                             ALU.mult, ALU.mult, ("mT", "rstd", "vec"), ("mn",))
                for dt in range(DC):
                    b = dt % 2
                    for k in range(DC):
                        self.mm(pa[b][:, 0:NMEM], wkv[:, k, dt * 128:(dt + 1) * 128], mn[:, k, :], k == 0, k == DC - 1,
                                ("mn", kkv[k]), (("pa", b),))
                    self.copy("act", kmT[:, dt, :], pa[b][:, 0:NMEM], (("pa", b),), ("kmT",))
                for mt in range(2):
                    for half in range(2):
                        b = (2 * mt + half) % 2
                        for k in range(DC):
                            self.mm(pa[b][:, 0:512], mn[:, k, mt * 128:(mt + 1) * 128],
                                    wkv[:, k, D + half * 512:D + half * 512 + 512], k == 0, k == DC - 1,
                                    ("mn", kkv[k]), (("pa", b),))
                        self.copy("act", vm[:, mt, half * 512:half * 512 + 512], pa[b][:, 0:512], (("pa", b),), ("vm",))
                self.sch.barrier()
                self.sch.emit()
            self.phase = ph
            H1T = self.H1T.ap().rearrange("(c p) t -> p c t", p=128)
            MF = self.MF.ap()
            MG = self.MG.ap()
            pcnt_box = [0]

            def half1(i):
                sl = i % 2
                t0 = i * T
                self.dma("sp", hT[sl][:], H1T[:, :, t0:t0 + T], (("H1T", i),), (("hT", sl),), ("hT", sl))
                self.dma("sp", mf[sl][:], MF[:, :, t0:t0 + T].rearrange("h p t -> p h t"), (), (("mf", sl),), ("mf", sl))
                self.dma("sp", mg[sl][:], MG[:, :, t0:t0 + T].rearrange("h p t -> p h t"), (), (("mg", sl),), ("mg", sl))
                hk = ("hT", sl)
                f, sq, rstd, hq, qT, rec, oT = f_[sl], sq_[sl], rstd_[sl], hq_[sl], qT_[sl], rec_[sl], oT_[sl]
                fK, sqK, rsK, hqK, recK = ("f", sl), ("sq", sl), ("rstd", sl), ("hq", sl), ("rec", sl)
                for dt in range(DC):
                    b = 0
                    for h in range(8):
                        self.mm(pa[b][:, 0:T], wof[:, h, dt * 128:(dt + 1) * 128], mf[sl][:, h, :], h == 0, False,
                                (("mf", sl), kof[h]), (("pa", b),))
                    for j in range(4):
                        self.mm(pa[b][:, 0:T], wog[:, j, dt * 128:(dt + 1) * 128], mg[sl][:, j, :], False, j == 3,
                                (("mg", sl), kog[j]), (("pa", b),))
                    self.copy("act", f[:, dt, :], pa[b][:, 0:T], (("pa", b),), (fK,))
                self.rms_stats(f, sq, stp[:, 0:T], rstd, (fK,), sqK, "st", rsK, T)
                for c in range(DC):
                    self.stt("dve", f[:, c, :], f[:, c, :], vec[:, gpo + c:gpo + c + 1], rstd[:],
                             ALU.mult, ALU.mult, (fK, rsK, "vec"), (fK,))
                self.tt("dve", hT[sl][:], hT[sl][:], f[:], ALU.add, (fK, hk), (hk,))
                self.rms_stats(hT[sl], sq, stp[:, 0:T], rstd, (hk,), sqK, "st", rsK, T)
                for c in range(DC):
                    self.stt("dve", hq[:, c, :], hT[sl][:, c, :], vec[:, gmq + c:gmq + c + 1], rstd[:],
                             ALU.mult, ALU.mult, (hk, rsK, "vec"), (hqK,))
                for dt in range(DC):
                    b = 1
                    for k in range(DC):
                        self.mm(pa[b][:, 0:T], wq[:, k, dt * 128:(dt + 1) * 128], hq[:, k, :], k == 0, k == DC - 1,
                                (hqK, kq[k]), (("pa", b),))
                    self.copy("act", qT[:, dt, :], pa[b][:, 0:T], (("pa", b),), (("qT", sl, dt),))

            def half2(i):
                sl = i % 2
                t0 = i * T
                hk = ("hT", sl)
                f, sq, rstd, hq, qT, rec, oT = f_[sl], sq_[sl], rstd_[sl], hq_[sl], qT_[sl], rec_[sl], oT_[sl]
                fK, sqK, rsK, hqK, recK = ("f", sl), ("sq", sl), ("rstd", sl), ("hq", sl), ("rec", sl)
                pcnt = pcnt_box[0]
                for hd in range(MH):
                    pk = []
                    for mt in range(2):
                        b = pcnt % 2
                        pb = pcnt % 4
                        pcnt += 1
                        for dc in range(2):
                            self.mm(scp[b][:, 0:T], kmT[:, 2 * hd + dc, mt * 128:(mt + 1) * 128], qT[:, 2 * hd + dc, :],
                                    dc == 0, dc == 1, (("qT", sl, 2 * hd + dc), "kmT"), (("sc", b),))
                        self.act(pT[pb][:], scp[b][:, 0:T], AF.Exp, (("sc", b),), (("pT", pb),), scale=1.0 / 16.0)
                        pk.append(pb)
                    for mt in range(2):
                        self.mm(ssp[:, 0:T], ones1[:], pT[pk[mt]][:], mt == 0, mt == 1, (("pT", pk[mt]), "ones1"), ("ss",))
                    self.sch.add("dve", lambda e, rec=rec: e.reciprocal(rec[:], ssp[:, 0:T]), ("ss",), (recK,), cost=400.0)
                    for dc in range(2):
                        b = 0
                        for mt in range(2):
                            self.mm(op_[b][:, 0:T], vm[:, mt, (2 * hd + dc) * 128:(2 * hd + dc + 1) * 128], pT[pk[mt]][:],
                                    mt == 0, mt == 1, (("pT", pk[mt]), "vm"), (("op", b),))
                        self.tt("dve", oT[:, 2 * hd + dc, :], op_[b][:, 0:T], rec[:], ALU.mult, (("op", b), recK),
                                (("oT", sl, 2 * hd + dc),))
                for dt in range(DC):
                    for k in range(DC):
                        self.mm(op_[1][:, 0:T], wo[:, k, dt * 128:(dt + 1) * 128], oT[:, k, :], k == 0, k == DC - 1,
                                (("oT", sl, k), ko[k]), (("op", 1),))
                    self.copy("act", f[:, dt, :], op_[1][:, 0:T], (("op", 1),), (fK,))
                self.rms_stats(f, sq, stp[:, 0:T], rstd, (fK,), sqK, "st", rsK, T)
                for c in range(DC):
                    self.stt("dve", f[:, c, :], f[:, c, :], vec[:, gmo + c:gmo + c + 1], rstd[:],
                             ALU.mult, ALU.mult, (fK, rsK, "vec"), (fK,))
                self.tt("dve", hT[sl][:], hT[sl][:], f[:], ALU.add, (fK, hk), (hk,))
                self.dma("pool", H1T[:, :, t0:t0 + T], hT[sl][:], (hk,), (("H1T", i),), ("st_h", sl))
                pcnt_box[0] = pcnt

            half1(0)
            for i in range(NT):
                if i + 1 < NT:
                    half1(i + 1)
                half2(i)
            self.sch.barrier()
            self.sch.emit()
        self.phase = None

    def phase_gdn(self):
        T = 512
        NT = S // T
        C = 64
        NCH = T // C
        with ExitStack() as ph:
            self.phase = ph
            vec = self.vec
            ident = self.ident
            gno = self.VROW["gdn_out_norm"]
            GQT, GKT, GVT, ZT, BGT, MG = (self.GQT.ap(), self.GKT.ap(), self.GVT.ap(), self.ZT.ap(),
                                          self.BGT.ap(), self.MG.ap())
            f32t = lambda shape: self.sb(shape, F32)
            ones_r = f32t([1, 128])
            ones64 = f32t([64, 512])
            mSU, mIU, mSL, Iall = f32t([64, 512]), f32t([64, 512]), f32t([64, 512]), f32t([64, 512])
            self.sch.add("pool", lambda e: e.memset(ones_r[:], 1.0), (), ("ones_r",))
            self.sch.add("pool", lambda e: e.memset(ones64[:], 1.0), (), ("ones64",))
            pat = [[0, NCH], [1, C]]
            for m_, op_, base in ((mSU, ALU.is_gt, 0), (mIU, ALU.is_ge, 0), (Iall, ALU.is_equal, 0)):
                self.sch.add("pool", lambda e, m_=m_, op_=op_, base=base: e.affine_select(
                    m_[:], ones64[:], pat, op_, 0.0, base=base, channel_multiplier=-1), ("ones64",), (("mask", id(m_)),))
            self.sch.add("pool", lambda e: e.affine_select(mSL[:], ones64[:], [[0, NCH], [-1, C]], ALU.is_gt, 0.0,
                                                           base=0, channel_multiplier=1), ("ones64",), (("mask", id(mSL)),))
            mk = tuple(("mask", id(m_)) for m_ in (mSU, mIU, mSL, Iall))
            qT = [f32t([128, T]) for _ in range(2)]
            kT = [f32t([128, T]) for _ in range(2)]
            vT = [f32t([128, T]) for _ in range(2)]
            zT = [self.sb([128, T], BF16) for _ in range(2)]
            brow = [f32t([1, T]) for _ in range(2)]
            grow = [f32t([1, T]) for _ in range(2)]
            gc, ngc, eg, et = f32t([1, T]), f32t([1, T]), f32t([1, T]), f32t([1, T])
            kbT, vbT, kbgT, ktT = f32t([128, T]), f32t([128, T]), f32t([128, T]), f32t([128, T])
            dmin, dmax = f32t([64, T]), f32t([64, T])
            GTs, GTi, Gs = f32t([64, T]), f32t([64, T]), f32t([64, T])
            P = [f32t([64, T]) for _ in range(2)]
            Q = [f32t([64, T]) for _ in range(2)]
            FQ = f32t([64, T])
            Z = [f32t([64, T]) for _ in range(2)]
            vb_tok = f32t([64, NCH, 128])
            kbg_tok = f32t([64, NCH, 128])
            qgT = [f32t([128, T]) for _ in range(2)]
            AT = [f32t([64, T]) for _ in range(2)]
            kt_tok = [f32t([64, NCH, 128]) for _ in range(2)]
            U = [f32t([64, NCH, 128]) for _ in range(2)]
            WT = [f32t([128, T]) for _ in range(2)]
            eglB = [f32t([128, NCH]) for _ in range(2)]
            St = f32t([128, 128])
            vnew = [f32t([64, 128]) for _ in range(2)]
            osb = f32t([128, T])
            sqo = self.sb([128, T], BF16)
            rso = f32t([128, T])
            zsg = f32t([128, T])
            mgst = [self.sb([128, T], BF16) for _ in range(2)]
            pp = [self.ps([128, 512], F32) for _ in range(4)]
            pd = self.ps([128, 512], F32)
            pw = self.ps([128, 512], F32)
            pds = self.ps([128, 512], F32)
            po = self.ps([128, 512], F32)
            ppc = [0]

            def nextpp():
                b = ppc[0] % 4
                ppc[0] += 1
                return pp[b], ("pp", b)

            def cs(n):
                return slice(n * C, (n + 1) * C)

            def load(hh, i, ti):
                sl = ti % 2
                t0 = i * T
                self.dma("sp", qT[sl][:], GQT[hh, :, t0:t0 + T], (), (("qT", sl),), ("gq", sl))
                self.dma("sp", kT[sl][:], GKT[hh, :, t0:t0 + T], (), (("kT", sl),), ("gk", sl))
                self.dma("sp", vT[sl][:], GVT[hh, :, t0:t0 + T], (), (("vT", sl),), ("gv", sl))
                self.dma("sp", zT[sl][:], ZT[hh, :, t0:t0 + T], (), (("zT", sl),), ("gz", sl))
                self.dma("sp", brow[sl][:], BGT[hh:hh + 1, t0:t0 + T], (), (("brow", sl),), ("gb", sl))
                self.dma("sp", grow[sl][:], BGT[4 + hh:5 + hh, t0:t0 + T], (), (("grow", sl),), ("gg", sl))

            def pre(hh, i, ti):
                sl = ti % 2
                th = []
                A = th.append
                qk, kk, vk = ("qT", sl), ("kT", sl), ("vT", sl)

                def rows():
                    for n in range(NCH):
                        self.sch.add("dve", lambda e, n=n: e.tensor_tensor_scan(
                            gc[:, cs(n)], ones_r[:, 0:C], grow[sl][:, cs(n)], 0.0, ALU.mult, ALU.add),
                            (("grow", sl), "ones_r"), ("gc",), cost=200.0)
                    self.ts("dve", ngc[:], gc[:], -1.0, None, ALU.mult, None, ("gc",), ("ngc",))
                    self.act(eg[:], gc[:], AF.Exp, ("gc",), ("eg",))
                    for n in range(NCH):
                        self.ts("dve", et[:, cs(n)], gc[:, cs(n)], gc[:, n * C + C - 1:n * C + C], None, ALU.subtract, None,
                                ("gc",), ("et",))
                    self.act(et[:], et[:], AF.Exp, ("et",), ("et",), scale=-1.0)
                A(rows)

                def bcast():
                    bb, bbk = nextpp()
                    self.mm(bb[:, 0:T], ones_r[:, :], brow[sl][:], True, True, (("brow", sl), "ones_r"), (bbk,))
                    self.tt("dve", kbT[:], kT[sl][:], bb[:, 0:T], ALU.mult, (kk, bbk), ("kbT",))
                    self.tt("dve", vbT[:], vT[sl][:], bb[:, 0:T], ALU.mult, (vk, bbk), ("vbT",))
                    eb, ebk = nextpp()
                    self.mm(eb[:, 0:T], ones_r[:, :], eg[:], True, True, ("eg", "ones_r"), (ebk,))
                    self.tt("dve", kbgT[:], kbT[:], eb[:, 0:T], ALU.mult, ("kbT", ebk), ("kbgT",))
                    self.tt("dve", qgT[sl][:], qT[sl][:], eb[:, 0:T], ALU.mult, (qk, ebk), (("qgT", sl),))
                    tb, tbk = nextpp()
                    self.mm(tb[:, 0:T], ones_r[:, :], et[:], True, True, ("et", "ones_r"), (tbk,))
                    self.tt("dve", ktT[:], kT[sl][:], tb[:, 0:T], ALU.mult, (kk, tbk), ("ktT",))
                    lb, lbk = nextpp()
                    self.mm(lb[:, 0:NCH], ones_r[:, :], eg[:].rearrange("p (n c) -> p n c", c=C)[:, :, C - 1], True, True,
                            ("eg", "ones_r"), (lbk,))
                    self.copy("act", eglB[sl][:], lb[:, 0:NCH], (lbk,), (("eglB", sl),))
                A(bcast)

                def decay():
                    for n in range(NCH):
                        self.mm(pd[0:C, cs(n)], ones_r[:, 0:C], gc[:, cs(n)], True, False, ("gc", "ones_r"), ("pd",))
                        self.mm(pd[0:C, cs(n)], ngc[:, cs(n)], ones_r[:, 0:C], False, True, ("ngc", "ones_r"), ("pd",))
                    self.ts("dve", dmin[:], pd[0:C, 0:T], 0.0, None, ALU.min, None, ("pd",), ("dmin",))
                    self.ts("dve", dmax[:], pd[0:C, 0:T], 0.0, None, ALU.max, None, ("pd",), ("dmax",))
                    self.act(dmin[:], dmin[:], AF.Exp, ("dmin",), ("dmin",))
                    self.act(dmax[:], dmax[:], AF.Exp, ("dmax",), ("dmax",), scale=-1.0)
                    self.tt("pool", GTs[:], dmin[:], mSU[:], ALU.mult, ("dmin",) + mk, ("GTs",))
                    self.tt("pool", GTi[:], dmin[:], mIU[:], ALU.mult, ("dmin",) + mk, ("GTi",))
                    self.tt("pool", Gs[:], dmax[:], mSL[:], ALU.mult, ("dmax",) + mk, ("Gs",))
                A(decay)

                def gram():
                    b1, k1 = nextpp()
                    for n in range(NCH):
                        self.mm(b1[0:C, cs(n)], kT[sl][:, cs(n)], kbT[:, cs(n)], True, True, (kk, "kbT"), (k1,))
                    self.stt("dve", P[0][:], b1[0:C, 0:T], -1.0, GTs[:], ALU.mult, ALU.mult, (k1, "GTs"), (("P", 0),))
                    b2, k2 = nextpp()
                    for n in range(NCH):
                        self.mm(b2[0:C, cs(n)], kbT[:, cs(n)], kT[sl][:, cs(n)], True, True, (kk, "kbT"), (k2,))
                    self.stt("dve", Q[0][:], b2[0:C, 0:T], -1.0, Gs[:], ALU.mult, ALU.mult, (k2, "Gs"), (("Q", 0),))
                    b3, k3 = nextpp()
                    for n in range(NCH):
                        self.mm(b3[0:C, cs(n)], kT[sl][:, cs(n)], qT[sl][:, cs(n)], True, True, (kk, qk), (k3,))
                    self.tt("dve", AT[sl][:], b3[0:C, 0:T], GTi[:], ALU.mult, (k3, "GTi"), (("AT", sl),))
                    self.tt("pool", Z[0][:], P[0][:], Iall[:], ALU.add, (("P", 0),) + mk, (("Z", 0),))
                A(gram)

                def level(lv):
                    def f():
                        a, b = lv % 2, (lv + 1) % 2
                        bq, kq_ = nextpp()
                        for n in range(NCH):
                            self.mm(bq[0:C, cs(n)], P[a][:, cs(n)], Q[a][:, cs(n)], True, True, (("P", a), ("Q", a)), (kq_,))
                        if lv < 4:
                            bp, kp_ = nextpp()
                            for n in range(NCH):
                                self.mm(bp[0:C, cs(n)], Q[a][:, cs(n)], P[a][:, cs(n)], True, True, (("P", a), ("Q", a)), (kp_,))
                            self.copy("act", P[b][:], bp[0:C, 0:T], (kp_,), (("P", b),))
                            self.copy("act", Q[b][:], bq[0:C, 0:T], (kq_,), (("Q", b),))
                        self.tt("dve", FQ[:], bq[0:C, 0:T], Iall[:], ALU.add, (kq_,) + mk, ("FQ",))
                        bz, kz_ = nextpp()
                        for n in range(NCH):
                            self.mm(bz[0:C, cs(n)], FQ[:, cs(n)], Z[a][:, cs(n)], True, True, ("FQ", ("Z", a)), (kz_,))
                        self.copy("act", Z[b][:], bz[0:C, 0:T], (kz_,), (("Z", b),))
                    return f
                for lv in range(5):
                    A(level(lv))

                def toks():
                    for src, skey, dst, dkey in ((vbT, "vbT", vb_tok, "vb_tok"), (kbgT, "kbgT", kbg_tok, "kbg_tok"),
                                                 (ktT, "ktT", kt_tok[sl], ("kt_tok", sl))):
                        for half in range(2):
                            bt, kt_ = nextpp()
                            for n4 in range(4):
                                n = 4 * half + n4
                                self.tr(bt[0:C, n4 * 128:(n4 + 1) * 128], src[:, cs(n)], ident[:], (skey, "ident"), (kt_,))
                            self.copy("act", dst[:, 4 * half:4 * half + 4, :],
                                      bt[0:C, 0:512].rearrange("p (n e) -> p n e", e=128), (kt_,), (dkey,))
                A(toks)

                def uw():
                    TT = Z[1]
                    tk = ("Z", 1)
                    for half in range(2):
                        bu, ku_ = nextpp()
                        for n4 in range(4):
                            n = 4 * half + n4
                            self.mm(bu[0:C, n4 * 128:(n4 + 1) * 128], TT[:, cs(n)], vb_tok[:, n, :], True, True,
                                    (tk, "vb_tok"), (ku_,))
                        self.copy("act", U[sl][:, 4 * half:4 * half + 4, :],
                                  bu[0:C, 0:512].rearrange("p (n e) -> p n e", e=128), (ku_,), (("U", sl),))
                    bw, kw_ = nextpp()
                    for n in range(NCH):
                        self.mm(bw[:, cs(n)], kbg_tok[:, n, :], TT[:, cs(n)], True, True, (tk, "kbg_tok"), (kw_,))
                    self.copy("act", WT[sl][:], bw[:, 0:T], (kw_,), (("WT", sl),))
                A(uw)
                return th

            def seq_chunk(hh, i, n, ti):
                sl = ti % 2
                v = vnew[n % 2]
                vk = ("vnew", n % 2)
                self.mm(pw[0:C, 0:128], WT[sl][:, cs(n)], St[:], True, True, (("WT", sl), "St"), ("pw",))
                self.tt("dve", v[:], U[sl][:, n, :], pw[0:C, 0:128], ALU.subtract, (("U", sl), "pw"), (vk,))
                self.mm(po[:, cs(n)], St[:], qgT[sl][:, cs(n)], True, False, ("St", ("qgT", sl)), ("po",))
                self.mm(po[:, cs(n)], v[:], AT[sl][:, cs(n)], False, True, (vk, ("AT", sl)), ("po",))
                self.mm(pds[:, 0:128], kt_tok[sl][:, n, :], v[:], True, True, (("kt_tok", sl), vk), ("pds",))
                self.stt("dve", St[:], St[:], eglB[sl][:, n:n + 1], pds[:, 0:128], ALU.mult, ALU.add,
                         ("St", ("eglB", sl), "pds"), ("St",))

            def finish(hh, i, ti):
                sl = ti % 2
                t0 = i * T
                ms = ti % 2
                self.copy("act", osb[:], po[:, 0:T], ("po",), ("osb",))
                self.tt("dve", sqo[:], osb[:], osb[:], ALU.mult, ("osb",), ("sqo",))
                bs, ks_ = nextpp()
                self.mm(bs[:, 0:T], self.ones_bf[:], sqo[:], True, True, ("sqo", "ones_bf"), (ks_,))
                self.act(rso[:], bs[:, 0:T], AF.Ln, (ks_,), ("rso",), bias=EPS, scale=8.0)
                self.act(rso[:], rso[:], AF.Exp, ("rso",), ("rso",), scale=-0.5)
                self.stt("dve", osb[:], osb[:], vec[:, gno:gno + 1], rso[:], ALU.mult, ALU.mult, ("osb", "rso", "vec"), ("osb",))
                self.act(zsg[:], zT[sl][:], AF.Exp, (("zT", sl),), ("zsg",), scale=-1.0)
                self.act(zsg[:], zsg[:], AF.Ln, ("zsg",), ("zsg",), bias=1.0)
                self.act(zsg[:], zsg[:], AF.Exp, ("zsg",), ("zsg",), scale=-1.0)
                self.tt("dve", zsg[:], zsg[:], zT[sl][:], ALU.mult, ("zsg", ("zT", sl)), ("zsg",))
                self.tt("dve", mgst[ms][:], osb[:], zsg[:], ALU.mult, ("osb", "zsg"), (("mgst", ms),))
                self.dma("pool", MG[hh, :, t0:t0 + T], mgst[ms][:], (("mgst", ms),), (("MG", hh, i),), ("mgst", ms))

            interleave = os.environ.get("KGDN_IL", "1") == "1"
            tiles = [(hh, i) for hh in range(4) for i in range(NT)]
            load(*tiles[0], 0)
            cur = pre(*tiles[0], 0)
            for th in cur:
                th()
            for ti, (hh, i) in enumerate(tiles):
                nxt = []
                if ti + 1 < len(tiles):
                    load(*tiles[ti + 1], ti + 1)
                    nxt = pre(*tiles[ti + 1], ti + 1)
                if i == 0:
                    self.sch.add("pool", lambda e: e.memset(St[:], 0.0), ("St",), ("St",))
                per = (len(nxt) + NCH - 1) // NCH if interleave else 0
                for n in range(NCH):
                    seq_chunk(hh, i, n, ti)
                    for th in nxt[n * per:(n + 1) * per]:
                        th()
                finish(hh, i, ti)
                for th in nxt[NCH * per:]:
                    th()
            self.sch.barrier()
            self.sch.emit(reorder="gdn" if os.environ.get("KGDN_RE", "1") == "1" else False)
        self.phase = None

    def phase_gdn_stub(self):
        with ExitStack() as ph:
            self.phase = ph
            z = self.sb([128, 4, 2048], BF16)
            self.sch.add("pool", lambda e: e.memset(z[:], 0.0), (), ("z",))
            MG = self.MG.ap()
            for t0 in range(0, S, 2048):
                n = min(2048, S - t0)
                self.dma("pool", MG[:, :, t0:t0 + n].rearrange("h p t -> p h t"), z[:, :, 0:n], ("z",), (("MGz", t0),), "mgz")
            self.sch.barrier()
            self.sch.emit()
        self.phase = None


def make_vecs(inp):
    v = np.zeros((128, 128), np.float32)
    for nm, r0 in KB.VROW.items():
        arr = np.asarray(inp[nm], np.float32).reshape(-1)
        n = arr.size // 128
        v[r0:r0 + n, :] = arr.reshape(n, 128)
    return v


WNAMES = ("ffn1_w_gate", "ffn1_w_up", "ffn1_w_down", "w_in", "w_out", "mem_w_q", "mem_w_kv", "mem_w_o",
          "ffn2_w_gate", "ffn2_w_up", "ffn2_w_down")


def make_in_maps(inp):
    vecs = make_vecs(inp)
    small = np.zeros((8, 4), np.float32)
    small[:, 0] = np.asarray(inp["fox_f_bias"], np.float32).reshape(-1)
    small[4:8, 1] = np.asarray(inp["gdn_a_log"], np.float32).reshape(-1)
    small[4:8, 2] = np.asarray(inp["gdn_dt_bias"], np.float32).reshape(-1)
    shared = {nm: np.ascontiguousarray(np.asarray(inp[nm], np.float32)[0]) for nm in WNAMES}
    maps = []
    for b in range(NCORES):
        m = dict(shared)
        m["x"] = np.ascontiguousarray(np.asarray(inp["x"], np.float32)[b][:S])
        m["mem"] = np.ascontiguousarray(np.asarray(inp["mem"], np.float32)[b])
        m["vecs"] = vecs
        m["small"] = small
        maps.append(m)
    return maps


def kernel(**inputs):
    kb = KB()
    nc = kb.build()
    res = run_bass_kernel_spmd(nc, make_in_maps(inputs), core_ids=list(range(NCORES)))
    return np.stack([np.asarray(r["out"], np.float32) for r in res.results], axis=0)
```
